# Optimizing a Trainium2 kernel written in Bass

```python
import math
import jax, jax.numpy as jnp
from jax import lax
import numpy as np

D_MODEL = 2048
BATCH = 4
SEQ = 2048
DEPTH = 4

HEAD_DIM = D_MODEL // 16
N_MEM = 256
MEM_HEADS = 4
DIL_GROUPS = ((128, 1), (512, 4), (2048, 16))
A_HEADS_PER_GROUP = D_MODEL // 256
A_BLOCK = 128
B_HEADS = 12
B_KV_GROUPS = 4
CMP_LEN = 32
CMP_STRIDE = 16
CMP_HIDDEN = 512
SLC_LEN = 64
SLC_TOPK = 16
SLC_Q_CHUNK = 16
SEL_FORCE = 1e4
WIN_LEN = 512
WIN_BLOCK = 128
D_FF = ((8 * D_MODEL // 3 + 255) // 256) * 256

kernel_name = "hybrid_dilated_nsa_yoco"


def rmsnorm(x, g, eps=1e-6):
    xf = x.astype(jnp.float32)
    y = xf * lax.rsqrt(jnp.mean(xf * xf, axis=-1, keepdims=True) + eps)
    return (y * g.astype(jnp.float32)).astype(x.dtype)


def alibi_slopes(n):
    return 2.0 ** (-8.0 * jnp.arange(1, n + 1, dtype=jnp.float32) / n)


def masked_softmax(s, valid):
    s = jnp.where(valid, s, -jnp.inf)
    m = jnp.max(s, axis=-1, keepdims=True)
    m = jnp.where(jnp.isfinite(m), m, 0.0)
    e = jnp.where(valid, jnp.exp(s - m), 0.0)
    den = jnp.maximum(jnp.sum(e, axis=-1, keepdims=True), 1e-30)
    return e / den, (m + jnp.log(den))[..., 0]


def banded_attention(q, k, v, slopes, max_dist, blk, pos_scale):
    B, L, H, D = q.shape
    G = k.shape[2]
    r = H // G
    nb = -(-L // blk)
    Lp = nb * blk
    n_prev = -(-max_dist // blk)
    qb = jnp.pad(q, ((0, 0), (0, Lp - L), (0, 0), (0, 0))).reshape(B, nb, blk, G, r, D)

    def band(t):
        t = jnp.pad(t, ((0, 0), (n_prev * blk, Lp - L), (0, 0), (0, 0)))
        t = t.reshape(B, nb + n_prev, blk, G, D)
        return jnp.concatenate([t[:, j:j + nb] for j in range(n_prev + 1)], axis=2)

    kb, vb = band(k), band(v)
    s = jnp.einsum('bnqgrd,bnkgd->bngrqk', qb, kb,
                   preferred_element_type=jnp.float32) / math.sqrt(D)
    qpos = jnp.arange(nb)[:, None] * blk + jnp.arange(blk)[None, :]
    kpos = jnp.arange(nb)[:, None] * blk + jnp.arange((n_prev + 1) * blk)[None, :] - n_prev * blk
    dist = qpos[:, :, None] - kpos[:, None, :]
    valid = (dist >= 0) & (dist <= max_dist) & (kpos[:, None, :] >= 0)
    bias = -slopes.reshape(1, G, r, 1, 1) * (dist * pos_scale).astype(jnp.float32)[:, None, None]
    p, lse = masked_softmax(s + bias, valid[:, None, None])
    o = jnp.einsum('bngrqk,bnkgd->bnqgrd', p.astype(v.dtype), vb).reshape(B, Lp, H, D)[:, :L]
    lse = lse.transpose(0, 1, 4, 2, 3).reshape(B, Lp, H)[:, :L]
    return o, lse


def dilated_attention(q, k, v):
    B, S, _, D = q.shape
    Hg = A_HEADS_PER_GROUP
    slopes = alibi_slopes(len(DIL_GROUPS) * Hg)
    outs, lses = [], []
    for gi, (w, d) in enumerate(DIL_GROUPS):
        sl = slice(gi * Hg, (gi + 1) * Hg)

        def to_cls(t):
            return t.reshape(B, S // d, d, Hg, D).transpose(0, 2, 1, 3, 4).reshape(B * d, S // d, Hg, D)

        o, lse = banded_attention(to_cls(q[:, :, sl]), to_cls(k[:, :, sl]), to_cls(v[:, :, sl]),
                                  slopes[sl], w // d, A_BLOCK, d)
        outs.append(o.reshape(B, d, S // d, Hg, D).transpose(0, 2, 1, 3, 4).reshape(B, S, Hg, D))
        lses.append(lse.reshape(B, d, S // d, Hg).transpose(0, 2, 1, 3).reshape(B, S, Hg))
    alpha = jax.nn.softmax(jnp.stack(lses, 0), axis=0)
    o = jnp.sum(alpha[..., None] * jnp.stack(outs, 0).astype(jnp.float32), axis=0)
    return o.astype(q.dtype)


def memory_attention(q, mem_n, w_kv):
    B, M, _ = mem_n.shape
    kv = (mem_n @ w_kv).reshape(B, M, 2, MEM_HEADS, HEAD_DIM)
    s = jnp.einsum('bshd,bmhd->bhsm', q, kv[:, :, 0],
                   preferred_element_type=jnp.float32) / math.sqrt(HEAD_DIM)
    p = jax.nn.softmax(s, axis=-1)
    return jnp.einsum('bhsm,bmhd->bshd', p.astype(q.dtype), kv[:, :, 1])


def compress_blocks(kr, pe, w1, w2):
    B, S, G, D = kr.shape
    n = (S - CMP_LEN) // CMP_STRIDE + 1
    idx = jnp.arange(n)[:, None] * CMP_STRIDE + jnp.arange(CMP_LEN)[None, :]
    blk = kr[:, idx] + pe[:, None, :].astype(kr.dtype)
    blk = blk.transpose(0, 1, 3, 2, 4).reshape(B, n, G, CMP_LEN * D)
    return jax.nn.gelu(blk @ w1) @ w2


def nsa_shared_kv(h, g, w, pe, wk1, wk2, wv1, wv2):
    B, S, _ = h.shape
    kv = (rmsnorm(h, g) @ w).reshape(B, S, 6, B_KV_GROUPS, HEAD_DIM)
    kc = compress_blocks(kv[:, :, 0], pe[0], wk1, wk2)
    vc = compress_blocks(kv[:, :, 1], pe[1], wv1, wv2)
    return kc, vc, kv[:, :, 2], kv[:, :, 3], kv[:, :, 4], kv[:, :, 5]


def selected_attention(qg, ks, vs, sel, slopes):
    B, S, G, r, D = qg.shape
    kk = sel.shape[-1]
    n_sel = S // SLC_LEN
    kb = ks.reshape(B, n_sel, SLC_LEN, G, D).transpose(0, 3, 1, 2, 4)
    vb = vs.reshape(B, n_sel, SLC_LEN, G, D).transpose(0, 3, 1, 2, 4)
    C = min(SLC_Q_CHUNK, S)
    nC = S // C
    q_ch = qg.reshape(B, nC, C, G, r, D).transpose(1, 0, 2, 3, 4, 5)
    s_ch = sel.reshape(B, nC, C, G, kk).transpose(1, 0, 2, 3, 4)
    t_ch = jnp.arange(S).reshape(nC, C)
    bi = jnp.arange(B)[:, None, None, None]
    gi = jnp.arange(G)[None, None, :, None]
    sl = slopes.reshape(G, r)

    def one(args):
        qc, sc, tc = args
        kg = kb[bi, gi, sc]
        vg = vb[bi, gi, sc]
        s = jnp.einsum('bcgrd,bcgkld->bcgrkl', qc, kg,
                       preferred_element_type=jnp.float32) / math.sqrt(D)
        kpos = sc[..., None] * SLC_LEN + jnp.arange(SLC_LEN)
        dist = tc[None, :, None, None, None] - kpos
        bias = -sl[None, None, :, :, None, None] * dist[:, :, :, None].astype(jnp.float32)
        p, _ = masked_softmax((s + bias).reshape(B, C, G, r, kk * SLC_LEN),
                              (dist >= 0).reshape(B, C, G, 1, kk * SLC_LEN))
        p = p.reshape(B, C, G, r, kk, SLC_LEN)
        return jnp.einsum('bcgrkl,bcgkld->bcgrd', p.astype(vg.dtype), vg)

    o = lax.map(one, (q_ch, s_ch, t_ch))
    return o.transpose(1, 0, 2, 3, 4, 5).reshape(B, S, G * r, D)


def nsa_attention(q, gates, shared):
    kc, vc, ks, vs, kw, vw = shared
    B, S, H, D = q.shape
    G = B_KV_GROUPS
    r = H // G
    slopes = alibi_slopes(H)
    sl = slopes.reshape(G, r)
    qg = q.reshape(B, S, G, r, D)
    t = jnp.arange(S)
    n_cmp = kc.shape[1]
    cend = jnp.arange(n_cmp) * CMP_STRIDE + CMP_LEN - 1
    dist_c = t[:, None] - cend[None, :]
    s = jnp.einsum('bsgrd,bngd->bsgrn', qg, kc,
                   preferred_element_type=jnp.float32) / math.sqrt(D)
    bias = -sl[None, :, :, None] * dist_c[:, None, None, :].astype(jnp.float32)
    p_c, _ = masked_softmax(s + bias, (dist_c >= 0)[:, None, None, :])
    o_cmp = jnp.einsum('bsgrn,bngd->bsgrd', p_c.astype(vc.dtype), vc).reshape(B, S, H, D)
    n_sel = S // SLC_LEN
    cs = jnp.arange(n_cmp) * CMP_STRIDE
    ss = jnp.arange(n_sel) * SLC_LEN
    ov = jnp.clip(jnp.minimum(cs[:, None] + CMP_LEN, ss[None, :] + SLC_LEN)
                  - jnp.maximum(cs[:, None], ss[None, :]), 0).astype(jnp.float32) / CMP_LEN
    imp = jnp.einsum('bsgrn,nj->bsgj', p_c, ov)
    jb = jnp.arange(n_sel)[None, :]
    cur = (t // SLC_LEN)[:, None]
    forced = (jb == 0) | (jb == cur) | (jb == cur - 1)
    imp = jnp.where(forced[:, None], SEL_FORCE, jnp.where((jb > cur)[:, None], -SEL_FORCE, imp))
    _, sel = lax.top_k(imp, min(SLC_TOPK, n_sel))
    o_slc = selected_attention(qg, ks, vs, sel, slopes)
    o_win, _ = banded_attention(q, kw, vw, slopes, WIN_LEN - 1, WIN_BLOCK, 1)
    o = (gates[..., 0:1] * o_cmp.astype(jnp.float32) + gates[..., 1:2] * o_slc.astype(jnp.float32)
         + gates[..., 2:3] * o_win.astype(jnp.float32))
    return o.astype(q.dtype)


def mixer_a(u, mem_n, w_in, w_mem_kv, w_out):
    B, S, _ = u.shape
    na = len(DIL_GROUPS) * A_HEADS_PER_GROUP
    proj = u @ w_in
    qkv = proj[..., :3 * na * HEAD_DIM].reshape(B, S, 3, na, HEAD_DIM)
    mq = proj[..., 3 * na * HEAD_DIM:].reshape(B, S, MEM_HEADS, HEAD_DIM)
    o_dil = dilated_attention(qkv[:, :, 0], qkv[:, :, 1], qkv[:, :, 2])
    o_mem = memory_attention(mq, mem_n, w_mem_kv)
    return jnp.concatenate([o_dil.reshape(B, S, -1), o_mem.reshape(B, S, -1)], axis=-1) @ w_out


def mixer_b(u, mem_n, shared, w_in, w_mem_kv, w_out):
    B, S, _ = u.shape
    qd = B_HEADS * HEAD_DIM
    proj = u @ w_in
    q = proj[..., :qd].reshape(B, S, B_HEADS, HEAD_DIM)
    gates = jax.nn.sigmoid(proj[..., qd:qd + 3 * B_HEADS].astype(jnp.float32)).reshape(B, S, B_HEADS, 3)
    mq = proj[..., qd + 3 * B_HEADS:].reshape(B, S, MEM_HEADS, HEAD_DIM)
    o_nsa = nsa_attention(q, gates, shared)
    o_mem = memory_attention(mq, mem_n, w_mem_kv)
    return jnp.concatenate([o_nsa.reshape(B, S, -1), o_mem.reshape(B, S, -1)], axis=-1) @ w_out


def swiglu(u, w_gu, w_down):
    gu = u @ w_gu
    return (jax.nn.silu(gu[..., :D_FF]) * gu[..., D_FF:]) @ w_down


def setup_inputs(seed: int = 0) -> dict:
    key = jax.random.key(seed)
    ks = jax.random.split(key, 17)
    n_a = DEPTH // 2
    n_b = DEPTH - n_a
    na = len(DIL_GROUPS) * A_HEADS_PER_GROUP
    a_cols = 3 * na * HEAD_DIM + MEM_HEADS * HEAD_DIM
    b_cols = B_HEADS * HEAD_DIM + 3 * B_HEADS + MEM_HEADS * HEAD_DIM
    a_out_in = A_HEADS_PER_GROUP * HEAD_DIM + MEM_HEADS * HEAD_DIM
    b_out_in = B_HEADS * HEAD_DIM + MEM_HEADS * HEAD_DIM
    f = jnp.float32

    def w(k, shape, fan_in):
        return jax.random.normal(k, shape, f) * fan_in ** -0.5

    return {
        "x": jax.random.normal(ks[0], (BATCH, SEQ, D_MODEL), f),
        "mem": jax.random.normal(ks[1], (BATCH, N_MEM, D_MODEL), f),
        "norm_g": 1.0 + 0.01 * jax.random.normal(ks[2], (DEPTH, 5, D_MODEL), f),
        "a_w_in": w(ks[3], (n_a, D_MODEL, a_cols), D_MODEL),
        "a_w_out": w(ks[4], (n_a, a_out_in, D_MODEL), a_out_in),
        "b_w_in": w(ks[5], (n_b, D_MODEL, b_cols), D_MODEL),
        "b_w_out": w(ks[6], (n_b, b_out_in, D_MODEL), b_out_in),
        "mem_w_kv": w(ks[7], (DEPTH, D_MODEL, 2 * MEM_HEADS * HEAD_DIM), D_MODEL),
        "ffn_w_gu": w(ks[8], (DEPTH, D_MODEL, 2 * D_FF), D_MODEL),
        "ffn_w_down": w(ks[9], (DEPTH, D_FF, D_MODEL), D_FF),
        "kv_norm_g": 1.0 + 0.01 * jax.random.normal(ks[10], (D_MODEL,), f),
        "kv_w": w(ks[11], (D_MODEL, 6 * B_KV_GROUPS * HEAD_DIM), D_MODEL),
        "cmp_pe": 0.02 * jax.random.normal(ks[12], (2, CMP_LEN, HEAD_DIM), f),
        "cmp_wk1": w(ks[13], (CMP_LEN * HEAD_DIM, CMP_HIDDEN), CMP_LEN * HEAD_DIM),
        "cmp_wk2": w(ks[14], (CMP_HIDDEN, HEAD_DIM), CMP_HIDDEN),
        "cmp_wv1": w(ks[15], (CMP_LEN * HEAD_DIM, CMP_HIDDEN), CMP_LEN * HEAD_DIM),
        "cmp_wv2": w(ks[16], (CMP_HIDDEN, HEAD_DIM), CMP_HIDDEN),
    }


def reference(x, mem, norm_g, a_w_in, a_w_out, b_w_in, b_w_out, mem_w_kv, ffn_w_gu,
              ffn_w_down, kv_norm_g, kv_w, cmp_pe, cmp_wk1, cmp_wk2, cmp_wv1, cmp_wv2):
    n_a = DEPTH // 2
    shared = None
    for l in range(DEPTH):
        g = norm_g[l]
        mem_n = rmsnorm(mem, g[4])
        u = rmsnorm(x, g[0])
        if l < n_a:
            o = mixer_a(u, mem_n, a_w_in[l], mem_w_kv[l], a_w_out[l])
        else:
            if l == n_a:
                shared = nsa_shared_kv(x, kv_norm_g, kv_w, cmp_pe, cmp_wk1, cmp_wk2, cmp_wv1, cmp_wv2)
            o = mixer_b(u, mem_n, shared, b_w_in[l - n_a], mem_w_kv[l], b_w_out[l - n_a])
        x = x + rmsnorm(o, g[1])
        x = x + rmsnorm(swiglu(rmsnorm(x, g[2]), ffn_w_gu[l], ffn_w_down[l]), g[3])
    return x
```

```python
import math
from contextlib import ExitStack

import numpy as np
import concourse.bass as bass
import concourse.mybir as mybir
from concourse.bass_utils import run_bass_kernel_spmd

F32 = mybir.dt.float32
BF16 = mybir.dt.bfloat16
AF = mybir.ActivationFunctionType
ALU = mybir.AluOpType

ENGS = ("tensor", "vector", "scalar", "gpsimd", "sync")

D = 2048
S = 2048
NT = 16
KC = 16
DFF = 5632
NFC = 44
ISQ = 1.0 / math.sqrt(128.0)
NEG = -1.0e6
TABW = 1536
ARENA_N = 102 * 1024


class Tile:
    __slots__ = ("name", "w", "r")

    def __init__(self, name=""):
        self.name = name
        self.w = None
        self.r = []


class Prog:
    def __init__(self, nc, n_dma_sems=32):
        self.nc = nc
        self.allow_silent = True
        self.pe_selfwait = False
        self.ops = {e: [] for e in ENGS}
        self.count = {e: 0 for e in ENGS}
        self.waited = {e: {} for e in ENGS}
        self.n_dma_sems = n_dma_sems
        self.dma_cnt = [0] * n_dma_sems
        self.dma_rr = 0
        self.sems = {}

    def _deps(self, eng, reads, writes):
        need = {}

        def add(tok):
            if tok is None:
                return
            k, v = tok
            if need.get(k, 0) < v:
                need[k] = v

        for t in reads:
            add(t.w)
        for t in writes:
            add(t.w)
            for tok in t.r:
                add(tok)
        waits = []
        wd = self.waited[eng]
        for k, v in need.items():
            if wd.get(k, 0) >= v:
                continue
            if k == eng and v > self.count[eng]:
                continue
            if k == eng and eng == "tensor" and not self.pe_selfwait:
                continue
            wd[k] = v
            waits.append((k, v))
        return waits

    def _mark(self, tok, reads, writes):
        for t in reads:
            t.r.append(tok)
            if len(t.r) > 48:
                m = {}
                for k, v in t.r:
                    if m.get(k, 0) < v:
                        m[k] = v
                t.r = list(m.items())
        for t in writes:
            t.w = tok
            t.r = []

    def op(self, eng, fn, reads=(), writes=(), silent=False):
        waits = self._deps(eng, reads, writes)
        if silent and self.allow_silent:
            tok = (eng, self.count[eng] + 1)
            self.ops[eng].append((waits, fn, None))
        else:
            self.count[eng] += 1
            tok = (eng, self.count[eng])
            self.ops[eng].append((waits, fn, (eng, 1)))
        self._mark(tok, reads, writes)
        return tok

    def dma(self, eng, fn, reads=(), writes=()):
        s = self.dma_rr
        self.dma_rr = (self.dma_rr + 1) % self.n_dma_sems
        key = ("dma", s)
        waits = self._deps(eng, reads, writes)
        prev = self.dma_cnt[s]
        if prev > 0 and self.waited[eng].get(key, 0) < prev:
            self.waited[eng][key] = prev
            waits.append((key, prev))
        self.dma_cnt[s] += 16
        tok = (key, self.dma_cnt[s])
        self.ops[eng].append((waits, fn, (key, 16)))
        self._mark(tok, reads, writes)
        return tok

    def barrier(self):
        toks = [(e, self.count[e]) for e in ENGS if self.count[e] > 0]
        toks += [(("dma", s), v) for s, v in enumerate(self.dma_cnt) if v > 0]
        for e in ENGS:
            waits = []
            for k, v in toks:
                if k == e:
                    continue
                if self.waited[e].get(k, 0) < v:
                    self.waited[e][k] = v
                    waits.append((k, v))
            if waits:
                self.ops[e].append((waits, None, None))

    def final_wait(self, eng="sync"):
        toks = [(e, self.count[e]) for e in ENGS if self.count[e] > 0 and e != eng]
        toks += [(("dma", s), v) for s, v in enumerate(self.dma_cnt) if v > 0]
        waits = []
        for k, v in toks:
            if self.waited[eng].get(k, 0) < v:
                self.waited[eng][k] = v
                waits.append((k, v))
        self.ops[eng].append((waits, None, None))

    def emit(self, es):
        nc = self.nc
        for e in ENGS:
            self.sems[e] = es.enter_context(nc.semaphore("s_" + e))
        for s in range(self.n_dma_sems):
            self.sems[("dma", s)] = es.enter_context(nc.semaphore("s_dma%d" % s))
        block = es.enter_context(nc.Block())
        sems = self.sems

        def run(engname):
            def body(e):
                for waits, fn, inc in self.ops[engname]:
                    for k, v in waits:
                        e.wait_ge(sems[k], v)
                    if fn is None:
                        continue
                    ins = fn(e)
                    if inc is not None:
                        ins.then_inc(sems[inc[0]], inc[1])
            return body

        block.sync(run("sync"))
        block.tensor(run("tensor"))
        block.vector(run("vector"))
        block.scalar(run("scalar"))
        block.gpsimd(run("gpsimd"))


class Arena:
    def __init__(self, base_ap_bf16, nelem_bf16):
        self.base = base_ap_bf16
        self.n = nelem_bf16
        self.top = 0
        self.stack = []
        self.peak = 0

    def push(self):
        self.stack.append(self.top)

    def pop(self):
        self.top = self.stack.pop()

    def alloc(self, nelem, dtype):
        w = 2 if dtype == F32 else 1
        n16 = (nelem * w + 31) // 32 * 32
        if self.top + n16 > self.n:
            raise MemoryError("SBUF arena overflow: need %d have %d" % (self.top + n16, self.n))
        ap = self.base[:, self.top:self.top + nelem * w]
        self.top += n16
        self.peak = max(self.peak, self.top)
        if dtype != BF16:
            ap = ap.bitcast(dtype)
        return ap


def _toeplitz(fn):
    j = np.arange(128)[:, None]
    c = np.arange(TABW)[None, :]
    dist = c - 511 - j
    valid = fn(dist)
    return np.where(valid, -dist.astype(np.float32), np.float32(NEG)).astype(np.float32)


def make_consts():
    c = {}
    c["c_ident"] = np.eye(128, dtype=np.float32)
    tabA = np.stack([
        _toeplitz(lambda d: (d >= 0) & (d <= 128)),
        _toeplitz(lambda d: (d >= 0) & (d <= 512) & (d % 4 == 0)),
        _toeplitz(lambda d: (d >= 0) & (d % 16 == 0)),
    ], 0)
    c["c_tabA"] = np.ascontiguousarray(tabA.transpose(1, 0, 2))
    tabB = np.stack([
        _toeplitz(lambda d: (d >= 0)),
        _toeplitz(lambda d: (d >= 0) & (d <= 511)),
    ], 0)
    c["c_tabB"] = np.ascontiguousarray(tabB.transpose(1, 0, 2))
    n = np.arange(128)[:, None]
    t = np.arange(S)[None, :]
    cend = 16 * n + 31
    dc = t - cend
    cm = np.where((dc >= 0) & (n < 127), -dc.astype(np.float32), np.float32(NEG)).astype(np.float32)
    c["c_cmp"] = np.ascontiguousarray(cm)
    cs = np.arange(127) * 16
    ss = np.arange(32) * 64
    ov = np.clip(np.minimum(cs[:, None] + 32, ss[None, :] + 64) - np.maximum(cs[:, None], ss[None, :]), 0, None) / 32.0
    ovp = np.zeros((128, 32), np.float32)
    ovp[:127] = ov
    c["c_ov"] = ovp
    E = np.zeros((128, 16, 128), np.float32)
    for kt in range(16):
        for s_ in range(128):
            E[2 * kt + s_ // 64, kt, s_] = 1.0
    c["c_E"] = E
    keep = np.zeros((128, 16, 32), np.float32)
    force = np.zeros((128, 16, 32), np.float32)
    for tt in range(16):
        for p in range(128):
            cur = (tt * 128 + p) // 64
            for jb in range(32):
                if jb == 0 or jb == cur or jb == cur - 1:
                    force[p, tt, jb] = 1.0e4 + (32 - jb)
                elif jb > cur:
                    force[p, tt, jb] = -1.0e4 - jb
                else:
                    keep[p, tt, jb] = 1.0
    c["c_keep"] = keep
    c["c_force"] = force
    oh = np.zeros((128, 36, 128), np.float32)
    for r in range(36):
        oh[r, r, :] = 1.0
    c["c_onehot"] = oh
    return c


class KB:
    def __init__(self, nc, es):
        self.nc = nc
        self.P = Prog(nc)
        arena_t = es.enter_context(nc.sbuf_tensor("arena", [128, ARENA_N], BF16))
        self.ar = Arena(arena_t[:, :], ARENA_N)
        self.ps = es.enter_context(nc.psum_tensor("ps", [128, 8, 512], F32))
        self.PB = [Tile("pb%d" % i) for i in range(8)]

    def V(self, fn, r=(), w=()):
        return self.P.op("vector", fn, r, w)

    def A(self, fn, r=(), w=()):
        return self.P.op("scalar", fn, r, w)

    def T(self, fn, r=(), w=(), silent=False):
        return self.P.op("tensor", fn, r, w, silent=silent)

    def G(self, fn, r=(), w=()):
        return self.P.op("gpsimd", fn, r, w)

    def DMA(self, fn, r=(), w=(), q="sync"):
        return self.P.dma(q, fn, r, w)

    def psb(self, b):
        return self.ps[:, b, :]

    def setup_consts(self, cin):
        ar = self.ar
        self.identf = ar.alloc(128, F32)
        self.Tidf = Tile("identf")
        self.identb = ar.alloc(128, BF16)
        self.Tidb = Tile("identb")
        self.onesb = ar.alloc(128, BF16)
        self.Tones = Tile("ones")
        self.DMA(lambda e: e.dma_start(out=self.identf, in_=cin["c_ident"][:, :]), w=[self.Tidf])
        self.V(lambda e: e.tensor_copy(out=self.identb, in_=self.identf), r=[self.Tidf], w=[self.Tidb])
        self.V(lambda e: e.memset(self.onesb, 1.0), w=[self.Tones])
        self.stage = [ar.alloc(2048, F32) for _ in range(3)]
        self.Tstage = [Tile("stage%d" % i) for i in range(3)]
        self.stage_i = 0
        self.cast_i = 0

    def wload(self, dram_ap, dst_ap, Tdst, k=None, n=None, cast="gpsimd"):
        s = self.stage_i % 3
        self.stage_i += 1
        st = self.stage[s]
        if k is not None:
            st = st[:, 0:k * n].rearrange("p (k n) -> p k n", k=k)
        elif n is not None:
            st = st[:, 0:n]
        Ts = self.Tstage[s]
        self.DMA(lambda e: e.dma_start(out=st, in_=dram_ap), w=[Ts])
        self.P.op(cast, lambda e: e.tensor_copy(out=dst_ap, in_=st), [Ts], [Tdst])

    def rstd_of(self, src_ap, Tsrc, junk, Tjunk, ss, Tss):
        self.A(lambda e: e.activation(out=junk, in_=src_ap, func=AF.Square, accum_out=ss), r=[Tsrc], w=[Tjunk, Tss])
        self.V(lambda e: e.tensor_scalar(out=ss, in0=ss, scalar1=1.0 / D, scalar2=1e-6, op0=ALU.mult, op1=ALU.add), r=[Tss], w=[Tss])
        self.A(lambda e: e.activation(out=ss, in_=ss, func=AF.Sqrt), r=[Tss], w=[Tss])
        self.V(lambda e: e.reciprocal(out=ss, in_=ss), r=[Tss], w=[Tss])

    def normT(self, src_fn, src_tiles, g_row_ap, hT3, ThT, ntt=NT):
        ar = self.ar
        ar.push()
        gt = ar.alloc(D, F32)
        Tg = Tile("g")
        self.DMA(lambda e: e.dma_start(out=gt, in_=g_row_ap.partition_broadcast(128)), w=[Tg])
        xt = [ar.alloc(D, F32) for _ in range(2)]
        Tx = [Tile("xt%d" % i) for i in range(2)]
        hb = [ar.alloc(D, BF16) for _ in range(2)]
        Th = [Tile("hb%d" % i) for i in range(2)]
        junk = ar.alloc(D, BF16)
        Tj = Tile("junk")
        ss = [ar.alloc(1, F32) for _ in range(2)]
        Tss = [Tile("ss%d" % i) for i in range(2)]
        for tt in range(ntt):
            b = tt % 2
            src = src_fn(tt)
            self.DMA(lambda e, b=b, src=src: e.dma_start(out=xt[b], in_=src), r=[src_tiles[tt]], w=[Tx[b]])
            self.rstd_of(xt[b], Tx[b], junk, Tj, ss[b], Tss[b])
            self.V(lambda e, b=b: e.scalar_tensor_tensor(out=hb[b], in0=xt[b], scalar=ss[b], in1=gt, op0=ALU.mult, op1=ALU.mult),
                   r=[Tx[b], Tss[b], Tg], w=[Th[b]])
            pv = self.ps[:, 2 * b:2 * b + 2, :].bitcast(BF16).rearrange("p a b -> p (a b)")
            for k in range(KC):
                self.T(lambda e, b=b, k=k, pv=pv: e.transpose(out=pv[:, k * 128:(k + 1) * 128], in_=hb[b][:, k * 128:(k + 1) * 128], identity=self.identb),
                       r=[Th[b], self.Tidb], w=[self.PB[2 * b], self.PB[2 * b + 1]], silent=(k != KC - 1))
            self.A(lambda e, tt=tt, pv=pv: e.copy(out=hT3[:, :, tt * 128:(tt + 1) * 128], in_=pv.rearrange("p (k t) -> p k t", k=KC)),
                   r=[self.PB[2 * b], self.PB[2 * b + 1]], w=[ThT[tt]])
        ar.pop()
        self.P.barrier()

    def proj_fm(self, w3, Tw, hT3, ThT, out_ap, Tout, scale, ntok=S, banks=(4, 5)):
        nblk = (ntok + 511) // 512
        for tb in range(nblk):
            n = min(512, ntok - tb * 512)
            bk = banks[tb % 2]
            tts = list(range(tb * 4, tb * 4 + (n + 127) // 128))
            for k in range(KC):
                self.T(lambda e, k=k, tb=tb, n=n, bk=bk: e.matmul(self.ps[:, bk, 0:n], lhsT=w3[:, k, :], rhs=hT3[:, k, tb * 512:tb * 512 + n], start=(k == 0), stop=(k == KC - 1)),
                       r=[Tw] + [ThT[t] for t in tts], w=[self.PB[bk]], silent=(k != KC - 1))
            self.A(lambda e, tb=tb, n=n, bk=bk: e.mul(out_ap[:, tb * 512:tb * 512 + n], self.ps[:, bk, 0:n], scale), r=[self.PB[bk]], w=[Tout])

    def proj_tm(self, w3, Tw, hT3, ThT, out3, Tout, ntt=NT, banks=(6, 7), ncol=128):
        ngrp = (ntt + 3) // 4
        for g4 in range(ngrp):
            bk = banks[g4 % 2]
            cnt = min(4, ntt - g4 * 4)
            for i in range(cnt):
                tt = g4 * 4 + i
                for k in range(KC):
                    self.T(lambda e, k=k, tt=tt, i=i, bk=bk: e.matmul(self.ps[:, bk, i * ncol:(i + 1) * ncol], lhsT=hT3[:, k, tt * 128:(tt + 1) * 128], rhs=w3[:, k, 0:ncol], start=(k == 0), stop=(k == KC - 1)),
                           r=[Tw, ThT[tt]], w=[self.PB[bk]], silent=(k != KC - 1))
            self.V(lambda e, g4=g4, cnt=cnt, bk=bk: e.tensor_copy(out=out3[:, g4 * 4:g4 * 4 + cnt, :], in_=self.ps[:, bk, 0:cnt * ncol].rearrange("p (a d) -> p a d", a=cnt)),
                   r=[self.PB[bk]], w=[Tout])

    def attn_bufs(self):
        ar = self.ar
        self.sb = [ar.alloc(512, F32) for _ in range(4)]
        self.Tsb = [Tile("sb%d" % i) for i in range(4)]
        self.pt = [ar.alloc(512, BF16) for _ in range(4)]
        self.Tpt = [Tile("pt%d" % i) for i in range(4)]
        self.rec = ar.alloc(512, F32)
        self.Trec = Tile("rec")
        self.osb = [ar.alloc(512, BF16) for _ in range(2)]
        self.Tosb = [Tile("osb%d" % i) for i in range(2)]
        self.osb_i = 0

    def attn_units(self, units, q_ap, Tq, imp=None):
        nu = len(units)
        OB, DB = 2, 3

        def s_stage(i):
            u = units[i]
            b = i % 2
            pen = u.get("pen")
            self.T(lambda e, u=u, b=b: e.matmul(self.ps[:, b, :], lhsT=u["kT"], rhs=q_ap, start=True, stop=(u.get("pen") is None)),
                   r=list(u["rd"]) + [Tq], w=[self.PB[b]])
            if pen is not None:
                self.T(lambda e, pen=pen, b=b: e.matmul(self.ps[:, b, :], lhsT=pen[0], rhs=pen[1], start=False, stop=True),
                       r=list(pen[2]), w=[self.PB[b]])

        def mid(i):
            u = units[i]
            b = i % 2
            bias = float(u.get("bias", 0.0))
            if u.get("tab") is not None:
                self.V(lambda e, u=u, b=b: e.scalar_tensor_tensor(out=self.sb[b], in0=u["tab"], scalar=float(u["slope"]), in1=self.ps[:, b, :], op0=ALU.mult, op1=ALU.add),
                       r=[self.PB[b]] + list(u.get("tabrd", [])), w=[self.Tsb[b]])
                src, rd = self.sb[b], [self.Tsb[b]]
            else:
                src, rd = self.ps[:, b, :], [self.PB[b]]
            if bias != 0.0:
                self.A(lambda e, b=b, src=src, bias=bias: e.activation(out=self.pt[b], in_=src, func=AF.Exp, bias=bias), r=rd, w=[self.Tpt[b]])
            else:
                self.A(lambda e, b=b, src=src: e.activation(out=self.pt[b], in_=src, func=AF.Exp), r=rd, w=[self.Tpt[b]])

        def pv(i):
            u = units[i]
            b = i % 2
            first = (i == 0)
            last = (i == nu - 1)
            self.T(lambda e, u=u, b=b, first=first, last=last: e.matmul(self.ps[:, OB, :], lhsT=u["V"], rhs=self.pt[b], start=first, stop=last),
                   r=list(u["rd"]) + [self.Tpt[b]], w=[self.PB[OB]])
            self.T(lambda e, b=b, first=first, last=last: e.matmul(self.ps[:, DB, :], lhsT=self.onesb, rhs=self.pt[b], start=first, stop=last),
                   r=[self.Tones, self.Tpt[b]], w=[self.PB[DB]])
            if imp is not None:
                self.T(lambda e, b=b, first=first, last=last: e.matmul(self.ps[0:32, 7, :], lhsT=imp[0], rhs=self.pt[b], start=first, stop=last),
                       r=[imp[1], self.Tpt[b]], w=[self.PB[7]])

        s_stage(0)
        for i in range(nu):
            if i + 1 < nu:
                s_stage(i + 1)
            mid(i)
            pv(i)

    def recip_den(self):
        rec = self.rec
        self.V(lambda e: e.tensor_scalar(out=rec, in0=self.ps[:, 3, :], scalar1=1e-30, scalar2=None, op0=ALU.max), r=[self.PB[3]], w=[self.Trec])
        self.V(lambda e: e.reciprocal(out=rec, in_=rec), r=[self.Trec], w=[self.Trec])

    def finalize_plain(self, dst_dram, Tdst):
        self.recip_den()
        ob = self.osb_i % 2
        self.osb_i += 1
        rec = self.rec
        osb = self.osb[ob]
        self.V(lambda e: e.tensor_tensor(out=osb, in0=self.ps[:, 2, :], in1=rec, op=ALU.mult), r=[self.PB[2], self.Trec], w=[self.Tosb[ob]])
        self.DMA(lambda e: e.dma_start(out=dst_dram, in_=osb.rearrange("p (t c) -> p t c", t=4)), r=[self.Tosb[ob]], w=[Tdst])

    def mem_kv(self, mem_ap, Tmem, g_row, wkv, hT3mem, ThTm, kTm, TkTm, Vm, TVm, wbuf, Twbuf):
        self.normT(lambda tt: mem_ap[tt * 128:(tt + 1) * 128, :], [Tmem, Tmem], g_row, hT3mem, ThTm, ntt=2)
        wcols = wkv.rearrange("(k p) n -> p k n", p=128)
        for h in range(4):
            b = h % 2
            self.wload(wcols[:, :, h * 128:(h + 1) * 128], wbuf[b], Twbuf[b], k=KC, n=128)
            self.proj_fm(wbuf[b], Twbuf[b], hT3mem, ThTm, kTm[:, h, :], TkTm, 1.0, ntok=256)
        for h in range(4):
            b = h % 2
            self.wload(wcols[:, :, 512 + h * 128:512 + (h + 1) * 128], wbuf[b], Twbuf[b], k=KC, n=128)
            self.proj_tm(wbuf[b], Twbuf[b], hT3mem, ThTm, Vm[:, h, :, :], TVm, ntt=2)

    def resid_update(self, y_ap, Ty, x_src_ap, Tsrc, x_dst_ap, Tdst, gt, Tg, bufs):
        xt, Tx, tmp, Ttmp, junk, Tj, ss, Tss = bufs
        self.DMA(lambda e: e.dma_start(out=xt, in_=x_src_ap), r=[Tsrc], w=[Tx])
        self.rstd_of(y_ap, Ty, junk, Tj, ss, Tss)
        self.V(lambda e: e.scalar_tensor_tensor(out=tmp, in0=y_ap, scalar=ss, in1=gt, op0=ALU.mult, op1=ALU.mult), r=[Ty, Tss, Tg], w=[Ttmp])
        self.V(lambda e: e.tensor_tensor(out=xt, in0=xt, in1=tmp, op=ALU.add), r=[Ttmp, Tx], w=[Tx])
        self.DMA(lambda e: e.dma_start(out=x_dst_ap, in_=xt), r=[Tx], w=[Tdst])


def alibi(n, i):
    return float(2.0 ** (-8.0 * (i + 1) / n))


def build(n_layers=4, stop_after_mixer=False, l_start=0, dbg=False, decl=None, allow_silent=True, pe_selfwait=False, rep=1, dummy=(), seq=None):
    nc = bass.Bass("TRN2", target_bir_lowering=False)

    def din(name, shape):
        if decl is not None and name not in decl:
            return nc.dram_tensor(name, [1] * len(shape), F32).ap()
        return nc.dram_tensor(name, list(shape), F32, kind="ExternalInput").ap()

    x_in = din("x", [S, D])
    mem = din("mem", [256, D])
    norm_g = din("norm_g", [4, 5, D])
    a_w_in = din("a_w_in", [2, D, 9728])
    a_w_out = din("a_w_out", [2, 1536, D])
    b_w_in = din("b_w_in", [2, D, 2084])
    b_w_out = din("b_w_out", [2, 2048, D])
    mem_w_kv = din("mem_w_kv", [4, D, 1024])
    ffn_w_gu = din("ffn_w_gu", [4, D, 2 * DFF])
    ffn_w_down = din("ffn_w_down", [4, DFF, D])
    kv_norm_g = din("kv_norm_g", [1, D])
    kv_w = din("kv_w", [D, 3072])
    cmp_pe = din("cmp_pe", [2, 32, 128])
    cmp_wk1 = din("cmp_wk1", [4096, 512])
    cmp_wk2 = din("cmp_wk2", [512, 128])
    cmp_wv1 = din("cmp_wv1", [4096, 512])
    cmp_wv2 = din("cmp_wv2", [512, 128])
    cshapes = {"c_ident": [128, 128], "c_tabA": [128, 3, TABW], "c_tabB": [128, 2, TABW], "c_cmp": [128, S],
               "c_ov": [128, 32], "c_E": [128, 16, 128], "c_keep": [128, 16, 32], "c_force": [128, 16, 32],
               "c_onehot": [128, 36, 128]}
    cin = {k: din(k, v) for k, v in cshapes.items()}
    out = nc.dram_tensor("out", [S, D], F32, kind="ExternalOutput").ap()
    oT_d = nc.dram_tensor("oT_d", [NT, 128, 16, 128], BF16).ap()
    qT_d = nc.dram_tensor("qT_d", [16, 128, S], BF16).ap()
    actT_d = nc.dram_tensor("actT_d", [NT, 128, NFC, 128], BF16).ap()
    y_d = nc.dram_tensor("y_d", [S, D], F32).ap()
    sh_d = nc.dram_tensor("sh_d", [4, 4, 128, S], BF16).ap()
    kc_d = nc.dram_tensor("kc_d", [2, 128, 4, 128], BF16).ap()

    dbg_t = {}
    if dbg:
        for l_ in range(4):
            for nm in ("xm", "x"):
                dbg_t[nm + str(l_)] = nc.dram_tensor("dbg_" + nm + str(l_), [S, D], F32, kind="ExternalOutput").ap()

    with ExitStack() as es:
        kb = KB(nc, es)
        P = kb.P
        P.allow_silent = allow_silent
        P.pe_selfwait = pe_selfwait
        ar = kb.ar
        ps = kb.ps
        PB = kb.PB
        kb.setup_consts(cin)

        Tmem = Tile("mem")
        Tin = Tile("x_in")
        Tout = [Tile("out%d" % i) for i in range(NT)]
        ToT = [Tile("oT%d" % i) for i in range(16)]
        TqT = [Tile("qT%d" % i) for i in range(16)]
        Tact = [Tile("act%d" % i) for i in range(NFC)]
        Ty = [Tile("y%d" % i) for i in range(NT)]
        Tsh = [Tile("sh%d" % i) for i in range(4)]
        Tkc = Tile("kc")

        state = {"first": True}

        def xsrc(tt):
            if state["first"]:
                return x_in[tt * 128:(tt + 1) * 128, :], Tin
            return out[tt * 128:(tt + 1) * 128, :], Tout[tt]

        def w_out_phase(w_out_l, nheads, g_row):
            P.barrier()
            ar.push()
            wo = ar.alloc(nheads * D, BF16).rearrange("p (h n) -> p h n", h=nheads)
            Two = [Tile("wo%d" % i) for i in range(nheads)]
            for h in range(nheads):
                kb.wload(w_out_l[h * 128:(h + 1) * 128, :], wo[:, h, :], Two[h], n=D)
            gt = ar.alloc(D, F32)
            Tg = Tile("g")
            kb.DMA(lambda e: e.dma_start(out=gt, in_=g_row.partition_broadcast(128)), w=[Tg])
            ot = [ar.alloc(nheads * 128, BF16).rearrange("p (h t) -> p h t", h=nheads) for _ in range(2)]
            Tot = [Tile("ot%d" % i) for i in range(2)]
            xts = [ar.alloc(D, F32) for _ in range(2)]
            Txs = [Tile("xt%d" % i) for i in range(2)]
            sss = [ar.alloc(1, F32) for _ in range(2)]
            Tsss = [Tile("ss%d" % i) for i in range(2)]
            tmp = ar.alloc(D, F32)
            Ttmp = Tile("tmp")
            junk = ar.alloc(D, BF16)
            Tj = Tile("junk")
            g4 = gt.rearrange("p (a b) -> p a b", a=4)
            t4 = tmp.rearrange("p (a b) -> p a b", a=4)
            j4 = junk.rearrange("p (a b) -> p a b", a=4)
            for tt in range(NT):
                b = tt % 2
                pb0 = 4 * b
                kb.DMA(lambda e, b=b, tt=tt: e.dma_start(out=ot[b], in_=oT_d[tt, :, 0:nheads, :]), r=ToT[:nheads], w=[Tot[b]])
                for c4 in range(4):
                    for h in range(nheads):
                        kb.T(lambda e, b=b, c4=c4, h=h, pb0=pb0: e.matmul(ps[:, pb0 + c4, :], lhsT=ot[b][:, h, :], rhs=wo[:, h, c4 * 512:(c4 + 1) * 512], start=(h == 0), stop=(h == nheads - 1)),
                             r=[Tot[b], Two[h]], w=[PB[pb0 + c4]], silent=(h != nheads - 1))
                src, Tsrc = xsrc(tt)
                xt, Tx, ss, Tss = xts[b], Txs[b], sss[b], Tsss[b]
                y_ap = ps[:, pb0:pb0 + 4, :]
                pbs = PB[pb0:pb0 + 4]
                kb.DMA(lambda e, xt=xt, src=src: e.dma_start(out=xt, in_=src), r=[Tsrc], w=[Tx])
                kb.A(lambda e, ss=ss, y_ap=y_ap: e.activation(out=j4, in_=y_ap, func=AF.Square, accum_out=ss), r=pbs, w=[Tj, Tss])
                kb.V(lambda e, ss=ss: e.tensor_scalar(out=ss, in0=ss, scalar1=1.0 / D, scalar2=1e-6, op0=ALU.mult, op1=ALU.add), r=[Tss], w=[Tss])
                kb.A(lambda e, ss=ss: e.activation(out=ss, in_=ss, func=AF.Sqrt), r=[Tss], w=[Tss])
                kb.V(lambda e, ss=ss: e.reciprocal(out=ss, in_=ss), r=[Tss], w=[Tss])
                kb.V(lambda e, ss=ss, y_ap=y_ap: e.scalar_tensor_tensor(out=t4, in0=y_ap, scalar=ss, in1=g4, op0=ALU.mult, op1=ALU.mult),
                     r=pbs + [Tss, Tg], w=[Ttmp])
                kb.V(lambda e, xt=xt: e.tensor_tensor(out=xt, in0=xt, in1=tmp, op=ALU.add), r=[Ttmp, Tx], w=[Tx])
                kb.DMA(lambda e, xt=xt, tt=tt: e.dma_start(out=out[tt * 128:(tt + 1) * 128, :], in_=xt), r=[Tx], w=[Tout[tt]])
            ar.pop()
            P.barrier()
            state["first"] = False

        def ffn_phase(l):
            P.barrier()
            ar.push()
            wd = [ar.alloc(NFC * 512, BF16).rearrange("p (f n) -> p f n", f=NFC), None]
            Twd = [[Tile("wd%d_%d" % (i, q)) for q in range(11)] for i in range(2)]
            wdv = ffn_w_down[l].rearrange("(f p) n -> p f n", p=128)

            def load_wd_unit(c4, q):
                b = c4 % 2
                kb.wload(wdv[:, q * 4:(q + 1) * 4, c4 * 512:(c4 + 1) * 512], wd[b][:, q * 4:(q + 1) * 4, :], Twd[b][q], k=4, n=512)

            ar.push()
            hT = ar.alloc(KC * S, BF16)
            hT3 = hT.rearrange("p (k t) -> p k t", k=KC)
            ThT = [Tile("hT%d" % i) for i in range(NT)]
            kb.normT(lambda tt: out[tt * 128:(tt + 1) * 128, :], Tout, norm_g[l, 2:3, :], hT3, ThT)
            wg = [ar.alloc(KC * 128, BF16).rearrange("p (k n) -> p k n", k=KC) for _ in range(2)]
            wu = [ar.alloc(KC * 128, BF16).rearrange("p (k n) -> p k n", k=KC) for _ in range(2)]
            Twg = [Tile("wg%d" % i) for i in range(2)]
            Twu = [Tile("wu%d" % i) for i in range(2)]
            sg = [ar.alloc(512, F32) for _ in range(2)]
            Tsg = [Tile("sg%d" % i) for i in range(2)]
            ao = [ar.alloc(512, BF16) for _ in range(2)]
            Tao = [Tile("ao%d" % i) for i in range(2)]
            wgu = ffn_w_gu[l].rearrange("(k p) n -> p k n", p=128)

            def load_fc(fc):
                b = fc % 2
                kb.wload(wgu[:, :, fc * 128:(fc + 1) * 128], wg[b], Twg[b], k=KC, n=128)
                kb.wload(wgu[:, :, DFF + fc * 128:DFF + (fc + 1) * 128], wu[b], Twu[b], k=KC, n=128)

            load_fc(0)
            it = 0
            for fc in range(NFC):
                if fc + 1 < NFC:
                    load_fc(fc + 1)
                if 30 <= fc < 41:
                    load_wd_unit(0, fc - 30)
                b = fc % 2
                for tb in range(4):
                    gb, ub = 4 + 2 * (it % 2), 5 + 2 * (it % 2)
                    i2 = it % 2
                    it += 1
                    tts = [ThT[t] for t in range(tb * 4, tb * 4 + 4)]
                    for k in range(KC):
                        kb.T(lambda e, k=k, b=b, tb=tb, gb=gb: e.matmul(ps[:, gb, :], lhsT=wg[b][:, k, :], rhs=hT3[:, k, tb * 512:(tb + 1) * 512], start=(k == 0), stop=(k == KC - 1)),
                             r=[Twg[b]] + tts, w=[PB[gb]], silent=(k != KC - 1))
                    for k in range(KC):
                        kb.T(lambda e, k=k, b=b, tb=tb, ub=ub: e.matmul(ps[:, ub, :], lhsT=wu[b][:, k, :], rhs=hT3[:, k, tb * 512:(tb + 1) * 512], start=(k == 0), stop=(k == KC - 1)),
                             r=[Twu[b]] + tts, w=[PB[ub]], silent=(k != KC - 1))
                    kb.A(lambda e, i2=i2, gb=gb: e.activation(out=sg[i2], in_=ps[:, gb, :], func=AF.Silu), r=[PB[gb]], w=[Tsg[i2]])
                    kb.V(lambda e, i2=i2, ub=ub: e.tensor_tensor(out=ao[i2], in0=ps[:, ub, :], in1=sg[i2], op=ALU.mult), r=[PB[ub], Tsg[i2]], w=[Tao[i2]])
                    kb.DMA(lambda e, i2=i2, fc=fc, tb=tb: e.dma_start(out=actT_d[tb * 4:(tb + 1) * 4, :, fc, :].rearrange("t p c -> p t c"), in_=ao[i2].rearrange("p (t c) -> p t c", t=4)),
                           r=[Tao[i2]], w=[Tact[fc]])
            ar.pop()
            P.barrier()
            wd[1] = ar.alloc(NFC * 512, BF16).rearrange("p (f n) -> p f n", f=NFC)
            at = [ar.alloc(NFC * 128, BF16).rearrange("p (f t) -> p f t", f=NFC) for _ in range(2)]
            Tat = [Tile("at%d" % i) for i in range(2)]
            yo = [ar.alloc(512, F32) for _ in range(2)]
            Tyo = [Tile("yo%d" % i) for i in range(2)]
            it = 0
            for c4 in range(4):
                b = c4 % 2
                for tt in range(NT):
                    if c4 + 1 < 4 and 1 <= tt < 12:
                        load_wd_unit(c4 + 1, tt - 1)
                    i2 = it % 2
                    bk = 4 + (it % 2)
                    it += 1
                    kb.DMA(lambda e, i2=i2, tt=tt: e.dma_start(out=at[i2], in_=actT_d[tt]), r=Tact, w=[Tat[i2]])
                    for f in range(NFC):
                        kb.T(lambda e, f=f, i2=i2, b=b, bk=bk: e.matmul(ps[:, bk, :], lhsT=at[i2][:, f, :], rhs=wd[b][:, f, :], start=(f == 0), stop=(f == NFC - 1)),
                             r=[Tat[i2], Twd[b][f // 4]], w=[PB[bk]], silent=(f != NFC - 1))
                    kb.A(lambda e, i2=i2, bk=bk: e.copy(out=yo[i2], in_=ps[:, bk, :]), r=[PB[bk]], w=[Tyo[i2]])
                    kb.DMA(lambda e, i2=i2, tt=tt, c4=c4: e.dma_start(out=y_d[tt * 128:(tt + 1) * 128, c4 * 512:(c4 + 1) * 512], in_=yo[i2]), r=[Tyo[i2]], w=[Ty[tt]])
            ar.pop()
            P.barrier()
            ar.push()
            gt = ar.alloc(D, F32)
            Tg = Tile("g")
            kb.DMA(lambda e: e.dma_start(out=gt, in_=norm_g[l, 3:4, :].partition_broadcast(128)), w=[Tg])
            yt = [ar.alloc(D, F32) for _ in range(2)]
            Tyt = [Tile("yt%d" % i) for i in range(2)]
            xt = [ar.alloc(D, F32) for _ in range(2)]
            Tx = [Tile("xt%d" % i) for i in range(2)]
            ss = [ar.alloc(1, F32) for _ in range(2)]
            Tss = [Tile("ss%d" % i) for i in range(2)]
            tmp = ar.alloc(D, F32)
            Ttmp = Tile("tmp")
            junk = ar.alloc(D, BF16)
            Tj = Tile("junk")
            for tt in range(NT):
                b = tt % 2
                kb.DMA(lambda e, b=b, tt=tt: e.dma_start(out=yt[b], in_=y_d[tt * 128:(tt + 1) * 128, :]), r=[Ty[tt]], w=[Tyt[b]])
                kb.resid_update(yt[b], Tyt[b], out[tt * 128:(tt + 1) * 128, :], Tout[tt], out[tt * 128:(tt + 1) * 128, :], Tout[tt], gt, Tg,
                                (xt[b], Tx[b], tmp, Ttmp, junk, Tj, ss[b], Tss[b]))
            ar.pop()
            P.barrier()

        def layer_a(l):
            P.barrier()
            ar.push()
            hT = ar.alloc(KC * S, BF16)
            hT3 = hT.rearrange("p (k t) -> p k t", k=KC)
            ThT = [Tile("hT%d" % i) for i in range(NT)]
            wb = [ar.alloc(KC * 128, BF16).rearrange("p (k n) -> p k n", k=KC) for _ in range(3)]
            Twb = [Tile("wb%d" % i) for i in range(3)]
            hTm = ar.alloc(KC * 256, BF16).rearrange("p (k t) -> p k t", k=KC)
            ThTm = [Tile("hTm0"), Tile("hTm1")]
            kTm = ar.alloc(4 * 256, BF16).rearrange("p (h t) -> p h t", h=4)
            TkTm = Tile("kTm")
            Vm = ar.alloc(4 * 2 * 128, BF16).rearrange("p (h a d) -> p h a d", h=4, a=2)
            TVm = Tile("Vm")
            kb.mem_kv(mem, Tmem, norm_g[l, 4:5, :], mem_w_kv[l], hTm, ThTm, kTm, TkTm, Vm, TVm, wb, Twb)
            if state["first"]:
                kb.normT(lambda tt: x_in[tt * 128:(tt + 1) * 128, :], [Tin] * NT, norm_g[l, 0:1, :], hT3, ThT)
            else:
                kb.normT(lambda tt: out[tt * 128:(tt + 1) * 128, :], Tout, norm_g[l, 0:1, :], hT3, ThT)
            tab = ar.alloc(3 * TABW, F32).rearrange("p (g c) -> p g c", g=3)
            Ttab = Tile("tabA")
            kb.DMA(lambda e: e.dma_start(out=tab, in_=cin["c_tabA"]), w=[Ttab])
            kb.attn_bufs()
            qT = [ar.alloc(S, BF16) for _ in range(3)]
            kT = [ar.alloc(S, BF16) for _ in range(3)]
            Vt = [ar.alloc(NT * 128, BF16).rearrange("p (t d) -> p t d", t=NT) for _ in range(3)]
            Tq = [Tile("q%d" % i) for i in range(3)]
            Tk = [Tile("k%d" % i) for i in range(3)]
            Tv = [Tile("v%d" % i) for i in range(3)]
            win = a_w_in[l].rearrange("(k p) n -> p k n", p=128)
            for j in range(8):
                for g in range(3):
                    hq = g * 8 + j
                    kb.wload(win[:, :, hq * 128:(hq + 1) * 128], wb[0], Twb[0], k=KC, n=128)
                    kb.proj_fm(wb[0], Twb[0], hT3, ThT, qT[g], Tq[g], ISQ)
                    kb.wload(win[:, :, (24 + hq) * 128:(24 + hq + 1) * 128], wb[1], Twb[1], k=KC, n=128)
                    kb.proj_fm(wb[1], Twb[1], hT3, ThT, kT[g], Tk[g], 1.0)
                    kb.wload(win[:, :, (48 + hq) * 128:(48 + hq + 1) * 128], wb[2], Twb[2], k=KC, n=128)
                    kb.proj_tm(wb[2], Twb[2], hT3, ThT, Vt[g], Tv[g])
                for tb in range(4):
                    first = True
                    for g in range(3):
                        slope = alibi(24, g * 8 + j)
                        lo = {0: 4 * tb - 1, 1: 4 * tb - 4, 2: 0}[g]
                        units = []
                        for kt in range(max(0, lo), 4 * tb + 4):
                            delta = 512 * tb - 128 * kt
                            de = min(delta, 128) if g == 2 else delta
                            bias = -slope * (delta - de)
                            units.append(dict(kT=kT[g][:, kt * 128:(kt + 1) * 128], V=Vt[g][:, kt, :], rd=[Tk[g], Tv[g]],
                                              tab=tab[:, g, de + 511:de + 511 + 512], tabrd=[Ttab], slope=slope, bias=bias))
                        kb._chain_first = first
                        run_units_chain(units, qT[g][:, tb * 512:(tb + 1) * 512], Tq[g], first, g == 2)
                        first = False
                    kb.finalize_plain(oT_d[tb * 4:(tb + 1) * 4, :, j, :].rearrange("t p c -> p t c"), ToT[j])
            for h in range(4):
                kb.wload(win[:, :, 9216 + h * 128:9216 + (h + 1) * 128], wb[0], Twb[0], k=KC, n=128)
                kb.proj_fm(wb[0], Twb[0], hT3, ThT, qT[0], Tq[0], ISQ)
                for tb in range(4):
                    units = [dict(kT=kTm[:, h, kt * 128:(kt + 1) * 128], V=Vm[:, h, kt, :], rd=[TkTm, TVm], tab=None) for kt in range(2)]
                    run_units_chain(units, qT[0][:, tb * 512:(tb + 1) * 512], Tq[0], True, True)
                    kb.finalize_plain(oT_d[tb * 4:(tb + 1) * 4, :, 8 + h, :].rearrange("t p c -> p t c"), ToT[8 + h])
            ar.pop()
            w_out_phase(a_w_out[l], 12, norm_g[l, 1:2, :])

        def run_units_chain(units, q_ap, Tq, first, last):
            nu = len(units)
            OB, DB = 2, 3
            kbx = kb
            sbL, ptL, onesL = kb.sb, kb.pt, kb.onesb
            SBK = (0, 1, 4, 5)
            LA = 3

            def s_stage(i):
                u = units[i]
                b = i % 4
                pen = u.get("pen")
                kbx.T(lambda e, u=u, b=b: e.matmul(ps[:, SBK[b], :], lhsT=u["kT"], rhs=q_ap, start=True, stop=(u.get("pen") is None)),
                      r=list(u["rd"]) + [Tq], w=[PB[SBK[b]]], silent=(pen is not None))
                if pen is not None:
                    kbx.T(lambda e, pen=pen, b=b: e.matmul(ps[:, SBK[b], :], lhsT=pen[0], rhs=pen[1], start=False, stop=True),
                          r=list(pen[2]), w=[PB[SBK[b]]])

            def mid(i):
                u = units[i]
                b = i % 4
                bias = float(u.get("bias", 0.0))
                if u.get("tab") is not None:
                    kbx.V(lambda e, u=u, b=b: e.scalar_tensor_tensor(out=sbL[b], in0=u["tab"], scalar=float(u["slope"]), in1=ps[:, SBK[b], :], op0=ALU.mult, op1=ALU.add),
                          r=[PB[SBK[b]]] + list(u.get("tabrd", [])), w=[kbx.Tsb[b]])
                    src, rd = sbL[b], [kbx.Tsb[b]]
                else:
                    src, rd = ps[:, SBK[b], :], [PB[SBK[b]]]
                if bias != 0.0:
                    kbx.A(lambda e, b=b, src=src, bias=bias: e.activation(out=ptL[b], in_=src, func=AF.Exp, bias=bias), r=rd, w=[kbx.Tpt[b]])
                else:
                    kbx.A(lambda e, b=b, src=src: e.activation(out=ptL[b], in_=src, func=AF.Exp), r=rd, w=[kbx.Tpt[b]])

            def pv(i):
                u = units[i]
                b = i % 4
                st = first and (i == 0)
                sp = last and (i == nu - 1)
                imp = u.get("imp")
                kbx.T(lambda e, u=u, b=b: e.matmul(ps[:, OB, :], lhsT=u["V"], rhs=ptL[b], start=st, stop=sp),
                      r=list(u["rd"]) + [kbx.Tpt[b]], w=[PB[OB]], silent=True)
                kbx.T(lambda e, b=b: e.matmul(ps[:, DB, :], lhsT=onesL, rhs=ptL[b], start=st, stop=sp),
                      r=[kbx.Tones, kbx.Tpt[b]], w=[PB[DB]], silent=(imp is not None))
                if imp is not None:
                    kbx.T(lambda e, b=b, imp=imp: e.matmul(ps[0:32, 7, :], lhsT=imp[0], rhs=ptL[b], start=st, stop=sp),
                          r=[imp[1], kbx.Tpt[b]], w=[PB[7]])

            for i in range(min(LA, nu)):
                s_stage(i)
            for i in range(nu):
                if i + LA < nu:
                    s_stage(i + LA)
                mid(i)
                pv(i)

        def shared_kv_phase():
            P.barrier()
            ar.push()
            hT = ar.alloc(KC * S, BF16)
            hT3 = hT.rearrange("p (k t) -> p k t", k=KC)
            ThT = [Tile("hT%d" % i) for i in range(NT)]
            if state["first"]:
                kb.normT(lambda tt: x_in[tt * 128:(tt + 1) * 128, :], [Tin] * NT, kv_norm_g[0:1, :], hT3, ThT)
            else:
                kb.normT(lambda tt: out[tt * 128:(tt + 1) * 128, :], Tout, kv_norm_g[0:1, :], hT3, ThT)
            wb = [ar.alloc(KC * 128, BF16).rearrange("p (k n) -> p k n", k=KC) for _ in range(2)]
            Twb = [Tile("wb%d" % i) for i in range(2)]
            tmpT = [ar.alloc(S, BF16) for _ in range(2)]
            Ttmp = [Tile("tmpT%d" % i) for i in range(2)]
            kvw = kv_w.rearrange("(k p) n -> p k n", p=128)
            it = 0
            for g in range(4):
                for which, slot, fm in ((2, 0, True), (3, 1, False), (4, 2, True), (5, 3, False)):
                    b = it % 2
                    it += 1
                    col = which * 512 + g * 128
                    kb.wload(kvw[:, :, col:col + 128], wb[b], Twb[b], k=KC, n=128)
                    if fm:
                        kb.proj_fm(wb[b], Twb[b], hT3, ThT, tmpT[b], Ttmp[b], 1.0)
                    else:
                        kb.proj_tm(wb[b], Twb[b], hT3, ThT, tmpT[b].rearrange("p (t d) -> p t d", t=NT), Ttmp[b])
                    kb.DMA(lambda e, b=b, g=g, slot=slot: e.dma_start(out=sh_d[g, slot], in_=tmpT[b]), r=[Ttmp[b]], w=[Tsh[g]])
            w1sb = ar.alloc(32 * 512, BF16).rearrange("p (l n) -> p l n", l=32)
            Tw1 = [Tile("w1_%d" % q) for q in range(8)]
            w2sb = ar.alloc(4 * 128, BF16).rearrange("p (c n) -> p c n", c=4)
            Tw2 = Tile("w2")
            pef = ar.alloc(128, F32)
            Tpef = Tile("pef")
            peT = ar.alloc(32, BF16)
            TpeT = Tile("peT")
            b1 = ar.alloc(4, F32)
            Tb1 = Tile("b1")
            hx = ar.alloc(128, F32)
            Thx = Tile("hx")
            x2 = ar.alloc(128, F32)
            Tx2 = Tile("x2")
            sgm = ar.alloc(128, F32)
            Tsgm = Tile("sgm")
            gel = ar.alloc(4 * 128, BF16).rearrange("p (c n) -> p c n", c=4)
            Tgel = Tile("gel")
            csb = ar.alloc(4 * 128, BF16).rearrange("p (g n) -> p g n", g=4)
            Tcsb = Tile("csb")
            for which, w1, w2 in ((0, cmp_wk1, cmp_wk2), (1, cmp_wv1, cmp_wv2)):
                w1v = w1.rearrange("(l p) n -> p l n", p=128)
                for q in range(8):
                    kb.wload(w1v[:, q * 4:(q + 1) * 4, :], w1sb[:, q * 4:(q + 1) * 4, :], Tw1[q], k=4, n=512)
                kb.wload(w2.rearrange("(c p) n -> p c n", p=128), w2sb, Tw2, k=4, n=128)
                kb.DMA(lambda e, which=which: e.dma_start(out=pef[0:32, :], in_=cmp_pe[which]), w=[Tpef])
                kb.T(lambda e: e.transpose(out=ps[:, 6, 0:32], in_=pef[0:32, :], identity=kb.identf[0:32, 0:32]), r=[Tpef, kb.Tidf], w=[PB[6]])
                kb.V(lambda e: e.tensor_copy(out=peT, in_=ps[:, 6, 0:32]), r=[PB[6]], w=[TpeT])
                for hc in range(4):
                    for l_ in range(32):
                        kb.T(lambda e, hc=hc, l_=l_: e.matmul(ps[:, 7, hc:hc + 1], lhsT=w1sb[:, l_, hc * 128:(hc + 1) * 128], rhs=peT[:, l_:l_ + 1], start=(l_ == 0), stop=(l_ == 31)),
                             r=[Tw1[l_ // 4], TpeT], w=[PB[7]], silent=(l_ != 31))
                kb.V(lambda e: e.tensor_copy(out=b1, in_=ps[:, 7, 0:4]), r=[PB[7]], w=[Tb1])
                kb.V(lambda e: e.memset(csb, 0.0), w=[Tcsb])
                for g in range(4):
                    b = it % 2
                    it += 1
                    col = which * 512 + g * 128
                    kb.wload(kvw[:, :, col:col + 128], wb[b], Twb[b], k=KC, n=128)
                    kb.proj_fm(wb[b], Twb[b], hT3, ThT, tmpT[b], Ttmp[b], 1.0)
                    kr3 = tmpT[b].rearrange("p (n s) -> p n s", s=16)
                    for hc in range(4):
                        bk = 4 + hc % 2
                        for l_ in range(32):
                            rhs = kr3[:, 0:127, l_] if l_ < 16 else kr3[:, 1:128, l_ - 16]
                            kb.T(lambda e, hc=hc, l_=l_, rhs=rhs, bk=bk: e.matmul(ps[:, bk, 0:127], lhsT=w1sb[:, l_, hc * 128:(hc + 1) * 128], rhs=rhs, start=(l_ == 0), stop=(l_ == 31)),
                                 r=[Tw1[l_ // 4], Ttmp[b]], w=[PB[bk]], silent=(l_ != 31))
                        kb.V(lambda e, hc=hc, bk=bk: e.tensor_scalar(out=hx[:, 0:127], in0=ps[:, bk, 0:127], scalar1=b1[:, hc:hc + 1], scalar2=None, op0=ALU.add), r=[PB[bk], Tb1], w=[Thx])
                        kb.V(lambda e: e.tensor_tensor(out=x2[:, 0:127], in0=hx[:, 0:127], in1=hx[:, 0:127], op=ALU.mult), r=[Thx], w=[Tx2])
                        kb.V(lambda e: e.tensor_scalar(out=x2[:, 0:127], in0=x2[:, 0:127], scalar1=0.044715, scalar2=1.0, op0=ALU.mult, op1=ALU.add), r=[Tx2], w=[Tx2])
                        kb.V(lambda e: e.tensor_tensor(out=x2[:, 0:127], in0=x2[:, 0:127], in1=hx[:, 0:127], op=ALU.mult), r=[Tx2, Thx], w=[Tx2])
                        kb.A(lambda e: e.activation(out=sgm[:, 0:127], in_=x2[:, 0:127], func=AF.Sigmoid, scale=1.5957691216057308), r=[Tx2], w=[Tsgm])
                        kb.V(lambda e, hc=hc: e.tensor_tensor(out=gel[:, hc, 0:127], in0=hx[:, 0:127], in1=sgm[:, 0:127], op=ALU.mult), r=[Thx, Tsgm], w=[Tgel])
                    if which == 0:
                        for hc in range(4):
                            kb.T(lambda e, hc=hc: e.matmul(ps[:, 6, 0:127], lhsT=w2sb[:, hc, :], rhs=gel[:, hc, 0:127], start=(hc == 0), stop=(hc == 3)), r=[Tw2, Tgel], w=[PB[6]], silent=(hc != 3))
                        kb.V(lambda e, g=g: e.tensor_copy(out=csb[:, g, 0:127], in_=ps[:, 6, 0:127]), r=[PB[6]], w=[Tcsb])
                    else:
                        for hc in range(4):
                            kb.T(lambda e, hc=hc: e.matmul(ps[0:127, 6, 0:128], lhsT=gel[:, hc, 0:127], rhs=w2sb[:, hc, :], start=(hc == 0), stop=(hc == 3)), r=[Tw2, Tgel], w=[PB[6]], silent=(hc != 3))
                        kb.V(lambda e, g=g: e.tensor_copy(out=csb[0:127, g, :], in_=ps[0:127, 6, 0:128]), r=[PB[6]], w=[Tcsb])
                kb.DMA(lambda e, which=which: e.dma_start(out=kc_d[which], in_=csb), r=[Tcsb], w=[Tkc])
            ar.pop()
            P.barrier()

        def layer_b(l):
            lb = l - 2
            P.barrier()
            ar.push()
            kTm = ar.alloc(4 * 256, BF16).rearrange("p (h t) -> p h t", h=4)
            TkTm = Tile("kTm")
            Vm = ar.alloc(4 * 2 * 128, BF16).rearrange("p (h a d) -> p h a d", h=4, a=2)
            TVm = Tile("Vm")
            ghi = ar.alloc(S, BF16)
            glo = ar.alloc(S, BF16)
            Tgh = Tile("ghi")
            Tgl = Tile("glo")
            win = b_w_in[lb].rearrange("(k p) n -> p k n", p=128)
            ar.push()
            hT = ar.alloc(KC * S, BF16)
            hT3 = hT.rearrange("p (k t) -> p k t", k=KC)
            ThT = [Tile("hT%d" % i) for i in range(NT)]
            wb = [ar.alloc(KC * 128, BF16).rearrange("p (k n) -> p k n", k=KC) for _ in range(2)]
            Twb = [Tile("wb%d" % i) for i in range(2)]
            hTm = ar.alloc(KC * 256, BF16).rearrange("p (k t) -> p k t", k=KC)
            ThTm = [Tile("hTm0"), Tile("hTm1")]
            kb.mem_kv(mem, Tmem, norm_g[l, 4:5, :], mem_w_kv[l], hTm, ThTm, kTm, TkTm, Vm, TVm, wb, Twb)
            if state["first"]:
                kb.normT(lambda tt: x_in[tt * 128:(tt + 1) * 128, :], [Tin] * NT, norm_g[l, 0:1, :], hT3, ThT)
            else:
                kb.normT(lambda tt: out[tt * 128:(tt + 1) * 128, :], Tout, norm_g[l, 0:1, :], hT3, ThT)
            qtmp = [ar.alloc(S, BF16) for _ in range(2)]
            Tqtmp = [Tile("qtmp%d" % i) for i in range(2)]
            for h in range(16):
                b = h % 2
                col = h * 128 if h < 12 else 1572 + (h - 12) * 128
                kb.wload(win[:, :, col:col + 128], wb[b], Twb[b], k=KC, n=128)
                kb.proj_fm(wb[b], Twb[b], hT3, ThT, qtmp[b], Tqtmp[b], ISQ)
                kb.DMA(lambda e, b=b, h=h: e.dma_start(out=qT_d[h], in_=qtmp[b]), r=[Tqtmp[b]], w=[TqT[h]])
            wgt = ar.alloc(KC * 36, BF16).rearrange("p (k n) -> p k n", k=KC)
            Twgt = Tile("wgt")
            kb.wload(win[:, :, 1536:1572], wgt, Twgt, k=KC, n=36)
            gtok = [ar.alloc(36, F32) for _ in range(2)]
            Tgtok = [Tile("gtok%d" % i) for i in range(2)]
            gTf = ar.alloc(S, F32)
            TgTf = Tile("gTf")
            kb.V(lambda e: e.memset(ghi, 0.0), w=[Tgh])
            kb.V(lambda e: e.memset(glo, 0.0), w=[Tgl])
            for tt in range(NT):
                b = tt % 2
                bk = 6 + b
                bk2 = 4 + b
                for k in range(KC):
                    kb.T(lambda e, k=k, tt=tt, bk=bk: e.matmul(ps[:, bk, 0:36], lhsT=hT3[:, k, tt * 128:(tt + 1) * 128], rhs=wgt[:, k, :], start=(k == 0), stop=(k == KC - 1)),
                         r=[Twgt, ThT[tt]], w=[PB[bk]], silent=(k != KC - 1))
                kb.A(lambda e, b=b, bk=bk: e.activation(out=gtok[b], in_=ps[:, bk, 0:36], func=AF.Sigmoid), r=[PB[bk]], w=[Tgtok[b]])
                kb.T(lambda e, b=b, bk2=bk2: e.transpose(out=ps[0:36, bk2, 0:128], in_=gtok[b], identity=kb.identf), r=[Tgtok[b], kb.Tidf], w=[PB[bk2]])
                kb.V(lambda e, tt=tt, bk2=bk2: e.tensor_copy(out=gTf[0:36, tt * 128:(tt + 1) * 128], in_=ps[0:36, bk2, 0:128]), r=[PB[bk2]], w=[TgTf])
            kb.V(lambda e: e.tensor_copy(out=ghi[0:36, :], in_=gTf[0:36, :]), r=[TgTf], w=[Tgh])
            kb.V(lambda e: e.tensor_tensor(out=gTf[0:36, :], in0=gTf[0:36, :], in1=ghi[0:36, :], op=ALU.subtract), r=[TgTf, Tgh], w=[TgTf])
            kb.V(lambda e: e.tensor_copy(out=glo[0:36, :], in_=gTf[0:36, :]), r=[TgTf], w=[Tgl])
            ar.pop()
            P.barrier()
            ar.push()
            tabB = ar.alloc(2 * TABW, F32).rearrange("p (g c) -> p g c", g=2)
            TtabB = Tile("tabB")
            kb.DMA(lambda e: e.dma_start(out=tabB, in_=cin["c_tabB"]), w=[TtabB])
            cmpT = ar.alloc(S, F32)
            Tcmp = Tile("cmpT")
            kb.DMA(lambda e: e.dma_start(out=cmpT, in_=cin["c_cmp"]), w=[Tcmp])
            keep = ar.alloc(16 * 32, F32).rearrange("p (t j) -> p t j", t=16)
            force = ar.alloc(16 * 32, F32).rearrange("p (t j) -> p t j", t=16)
            Tkeep = Tile("keep")
            Tforce = Tile("force")
            kb.DMA(lambda e: e.dma_start(out=keep, in_=cin["c_keep"]), w=[Tkeep])
            kb.DMA(lambda e: e.dma_start(out=force, in_=cin["c_force"]), w=[Tforce])
            ovf = ar.alloc(32, F32)
            Tovf = Tile("ovf")
            ov_b = ar.alloc(32, BF16)
            Tov = Tile("ov")
            kb.DMA(lambda e: e.dma_start(out=ovf, in_=cin["c_ov"]), w=[Tovf])
            kb.V(lambda e: e.tensor_copy(out=ov_b, in_=ovf), r=[Tovf], w=[Tov])
            E_b = ar.alloc(16 * 128, BF16).rearrange("p (k s) -> p k s", k=16)
            TE = Tile("E")
            kb.wload(cin["c_E"], E_b, TE, k=16, n=128)
            oh_b = ar.alloc(36 * 128, BF16).rearrange("p (r m) -> p r m", r=36)
            Toh = Tile("oh")
            for q in range(3):
                kb.wload(cin["c_onehot"][:, q * 12:(q + 1) * 12, :], oh_b[:, q * 12:(q + 1) * 12, :], Toh, k=12, n=128)
            kcs = ar.alloc(4 * 128, BF16).rearrange("p (g n) -> p g n", g=4)
            vcs = ar.alloc(4 * 128, BF16).rearrange("p (g n) -> p g n", g=4)
            Tkcs = Tile("kcs")
            kb.DMA(lambda e: e.dma_start(out=kcs, in_=kc_d[0]), r=[Tkc], w=[Tkcs])
            kb.DMA(lambda e: e.dma_start(out=vcs, in_=kc_d[1]), r=[Tkc], w=[Tkcs])
            kb.attn_bufs()
            shg = [ar.alloc(4 * S, BF16).rearrange("p (s t) -> p s t", s=4) for _ in range(2)]
            Tshg = [Tile("shg%d" % i) for i in range(2)]
            qb = [ar.alloc(S, BF16) for _ in range(3)]
            Tqb = [Tile("qb%d" % i) for i in range(3)]
            ocmp = [ar.alloc(S, F32) for _ in range(3)]
            Toc = [Tile("ocmp%d" % i) for i in range(3)]
            impT = ar.alloc(S, F32)
            Timp = Tile("impT")
            penT = ar.alloc(S, BF16)
            Tpen = Tile("penT")
            kb.V(lambda e: e.memset(penT, 0.0), w=[Tpen])
            rg = ar.alloc(512, F32)
            Trg = Tile("rg")
            tmpo = ar.alloc(512, F32)
            Ttmpo = Tile("tmpo")
            v1 = ar.alloc(32, F32)
            v2 = ar.alloc(32, F32)
            mxa = ar.alloc(8, F32)
            mxb = ar.alloc(8, F32)
            selp = ar.alloc(32, F32)
            Tv1, Tv2, Tmxa, Tmxb, Tselp = Tile("v1"), Tile("v2"), Tile("mxa"), Tile("mxb"), Tile("selp")

            def finalize_gated(h, branch, tb, dst, Tdst, mode, dram=None, Tdram=None):
                kb.recip_den()
                recL = kb.rec
                r_ = h * 3 + branch
                kb.T(lambda e: e.matmul(ps[:, 6, :], lhsT=oh_b[:, r_, :], rhs=ghi[:, tb * 512:(tb + 1) * 512], start=True, stop=False), r=[Toh, Tgh], w=[PB[6]], silent=True)
                kb.T(lambda e: e.matmul(ps[:, 6, :], lhsT=oh_b[:, r_, :], rhs=glo[:, tb * 512:(tb + 1) * 512], start=False, stop=True), r=[Toh, Tgl], w=[PB[6]])
                kb.V(lambda e: e.tensor_tensor(out=rg, in0=ps[:, 6, :], in1=recL, op=ALU.mult), r=[PB[6], kb.Trec], w=[Trg])
                if mode == "set":
                    kb.V(lambda e: e.tensor_tensor(out=dst, in0=ps[:, 2, :], in1=rg, op=ALU.mult), r=[PB[2], Trg], w=[Tdst])
                elif mode == "add":
                    kb.V(lambda e: e.tensor_tensor(out=tmpo, in0=ps[:, 2, :], in1=rg, op=ALU.mult), r=[PB[2], Trg], w=[Ttmpo])
                    kb.G(lambda e: e.tensor_tensor(out=dst, in0=dst, in1=tmpo, op=ALU.add), r=[Ttmpo, Tdst], w=[Tdst])
                else:
                    kb.V(lambda e: e.tensor_tensor(out=tmpo, in0=ps[:, 2, :], in1=rg, op=ALU.mult), r=[PB[2], Trg], w=[Ttmpo])
                    ob = kb.osb_i % 2
                    kb.osb_i += 1
                    osbL = kb.osb[ob]
                    kb.G(lambda e: e.tensor_tensor(out=osbL, in0=dst, in1=tmpo, op=ALU.add), r=[Ttmpo, Tdst], w=[kb.Tosb[ob]])
                    kb.DMA(lambda e: e.dma_start(out=dram, in_=osbL.rearrange("p (t c) -> p t c", t=4)), r=[kb.Tosb[ob]], w=[Tdram])

            for g in range(4):
                sb_ = g % 2
                kb.DMA(lambda e, g=g, sb_=sb_: e.dma_start(out=shg[sb_], in_=sh_d[g].rearrange("s p t -> p s t")), r=[Tsh[g]], w=[Tshg[sb_]])
                ksT = shg[sb_][:, 0, :]
                vs = shg[sb_][:, 1, :].rearrange("p (t d) -> p t d", t=NT)
                kwT = shg[sb_][:, 2, :]
                vw = shg[sb_][:, 3, :].rearrange("p (t d) -> p t d", t=NT)
                for hh in range(3):
                    h = 3 * g + hh
                    slope = alibi(12, h)
                    kb.DMA(lambda e, hh=hh, h=h: e.dma_start(out=qb[hh], in_=qT_d[h]), r=[TqT[h]], w=[Tqb[hh]])
                    for tb in range(4):
                        units = [dict(kT=kcs[:, g, :], V=vcs[:, g, :], rd=[Tkcs], tab=cmpT[:, tb * 512:(tb + 1) * 512], tabrd=[Tcmp], slope=slope, bias=0.0, imp=(ov_b, Tov))]
                        run_units_chain(units, qb[hh][:, tb * 512:(tb + 1) * 512], Tqb[hh], True, True)
                        finalize_gated(h, 0, tb, ocmp[hh][:, tb * 512:(tb + 1) * 512], Toc[hh], "set")
                        recI = kb.rec
                        if hh == 0:
                            kb.V(lambda e, tb=tb: e.tensor_tensor(out=impT[0:32, tb * 512:(tb + 1) * 512], in0=ps[0:32, 7, :], in1=recI[0:32, :], op=ALU.mult), r=[PB[7], kb.Trec], w=[Timp])
                        else:
                            kb.V(lambda e: e.tensor_tensor(out=tmpo[0:32, :], in0=ps[0:32, 7, :], in1=recI[0:32, :], op=ALU.mult), r=[PB[7], kb.Trec], w=[Ttmpo])
                            kb.V(lambda e, tb=tb: e.tensor_tensor(out=impT[0:32, tb * 512:(tb + 1) * 512], in0=impT[0:32, tb * 512:(tb + 1) * 512], in1=tmpo[0:32, :], op=ALU.add), r=[Ttmpo, Timp], w=[Timp])
                for tt in range(NT):
                    kb.T(lambda e, tt=tt: e.transpose(out=ps[:, 6, 0:32], in_=impT[0:32, tt * 128:(tt + 1) * 128], identity=kb.identf[0:32, 0:32]), r=[Timp, kb.Tidf], w=[PB[6]])
                    kb.V(lambda e, tt=tt: e.tensor_tensor(out=v1, in0=ps[:, 6, 0:32], in1=keep[:, tt, :], op=ALU.mult), r=[PB[6], Tkeep], w=[Tv1])
                    kb.V(lambda e, tt=tt: e.tensor_tensor(out=v1, in0=v1, in1=force[:, tt, :], op=ALU.add), r=[Tv1, Tforce], w=[Tv1])
                    kb.V(lambda e: e.max(out=mxa, in_=v1), r=[Tv1], w=[Tmxa])
                    kb.V(lambda e: e.match_replace(out=v2, in_to_replace=mxa, in_values=v1, imm_value=-1.0e9), r=[Tv1, Tmxa], w=[Tv2])
                    kb.V(lambda e: e.max(out=mxb, in_=v2), r=[Tv2], w=[Tmxb])
                    kb.V(lambda e: e.tensor_scalar(out=selp, in0=v1, scalar1=mxb[:, 7:8], scalar2=None, op0=ALU.is_ge), r=[Tv1, Tmxb], w=[Tselp])
                    kb.V(lambda e: e.tensor_scalar(out=selp, in0=selp, scalar1=1.0, scalar2=30000.0, op0=ALU.subtract, op1=ALU.mult), r=[Tselp], w=[Tselp])
                    kb.T(lambda e: e.transpose(out=ps[0:32, 7, 0:128], in_=selp, identity=kb.identf), r=[Tselp, kb.Tidf], w=[PB[7]])
                    kb.V(lambda e, tt=tt: e.tensor_copy(out=penT[0:32, tt * 128:(tt + 1) * 128], in_=ps[0:32, 7, 0:128]), r=[PB[7]], w=[Tpen])
                for hh in range(3):
                    h = 3 * g + hh
                    slope = alibi(12, h)
                    for tb in range(4):
                        q_ap = qb[hh][:, tb * 512:(tb + 1) * 512]
                        units = []
                        for kt in range(0, 4 * tb + 4):
                            delta = 512 * tb - 128 * kt
                            de = min(delta, 128)
                            units.append(dict(kT=ksT[:, kt * 128:(kt + 1) * 128], V=vs[:, kt, :], rd=[Tshg[sb_]], tab=tabB[:, 0, de + 511:de + 511 + 512], tabrd=[TtabB],
                                              slope=slope, bias=-slope * (delta - de), pen=(E_b[:, kt, :], penT[:, tb * 512:(tb + 1) * 512], [TE, Tpen])))
                        run_units_chain(units, q_ap, Tqb[hh], True, True)
                        finalize_gated(h, 1, tb, ocmp[hh][:, tb * 512:(tb + 1) * 512], Toc[hh], "add")
                        units = []
                        for kt in range(max(0, 4 * tb - 4), 4 * tb + 4):
                            delta = 512 * tb - 128 * kt
                            units.append(dict(kT=kwT[:, kt * 128:(kt + 1) * 128], V=vw[:, kt, :], rd=[Tshg[sb_]], tab=tabB[:, 1, delta + 511:delta + 511 + 512], tabrd=[TtabB],
                                              slope=slope, bias=0.0))
                        run_units_chain(units, q_ap, Tqb[hh], True, True)
                        finalize_gated(h, 2, tb, ocmp[hh][:, tb * 512:(tb + 1) * 512], Toc[hh], "final", dram=oT_d[tb * 4:(tb + 1) * 4, :, h, :].rearrange("t p c -> p t c"), Tdram=ToT[h])
            for h in range(4):
                kb.DMA(lambda e, h=h: e.dma_start(out=qb[0], in_=qT_d[12 + h]), r=[TqT[12 + h]], w=[Tqb[0]])
                for tb in range(4):
                    units = [dict(kT=kTm[:, h, kt * 128:(kt + 1) * 128], V=Vm[:, h, kt, :], rd=[TkTm, TVm], tab=None) for kt in range(2)]
                    run_units_chain(units, qb[0][:, tb * 512:(tb + 1) * 512], Tqb[0], True, True)
                    kb.finalize_plain(oT_d[tb * 4:(tb + 1) * 4, :, 12 + h, :].rearrange("t p c -> p t c"), ToT[12 + h])
            ar.pop()
            ar.pop()
            w_out_phase(b_w_out[lb], 16, norm_g[l, 1:2, :])

        if seq is not None:
            for ph in seq:
                if ph == "a0":
                    layer_a(0)
                elif ph == "skv":
                    shared_kv_phase()
                elif ph == "b2":
                    layer_b(2)
                elif ph == "f0":
                    ffn_phase(0)
                if dbg and ph == "a0":
                    P.barrier()
                    Td = Tile("dbg")
                    for q in range(4):
                        kb.DMA(lambda e, q=q: e.dma_start(out=dbg_t["xm0"][q * 512:(q + 1) * 512, :], in_=out[q * 512:(q + 1) * 512, :]), r=Tout, w=[Td])
                    P.barrier()
        for l in range(l_start, n_layers if seq is None else 0):
            if l < 2:
                for _r in range(rep):
                    layer_a(l)
            else:
                if l == 2 or l == l_start:
                    shared_kv_phase()
                layer_b(l)
            def snap(name):
                if not dbg:
                    return
                P.barrier()
                Td = Tile("dbg")
                for q in range(4):
                    kb.DMA(lambda e, q=q: e.dma_start(out=dbg_t[name][q * 512:(q + 1) * 512, :], in_=out[q * 512:(q + 1) * 512, :]), r=Tout, w=[Td])
                P.barrier()
            snap("xm%d" % l)
            if stop_after_mixer and l == n_layers - 1:
                break
            ffn_phase(l)
            snap("x%d" % l)

        if dummy:
            P.barrier()
            dz = ar.alloc(64, F32)
            Tdz = Tile("dz")
            dz8 = ar.alloc(8, F32)
            kb.V(lambda e: e.memset(dz, 0.5), w=[Tdz])
            if "sigmoid" in dummy:
                kb.A(lambda e: e.activation(out=dz, in_=dz, func=AF.Sigmoid, scale=1.5), r=[Tdz], w=[Tdz])
            if "max" in dummy:
                kb.V(lambda e: e.max(out=dz8, in_=dz[:, 0:32]), r=[Tdz], w=[Tdz])
                kb.V(lambda e: e.match_replace(out=dz[:, 32:64], in_to_replace=dz8, in_values=dz[:, 0:32], imm_value=-1.0e9), r=[Tdz], w=[Tdz])
            if "isge" in dummy:
                kb.V(lambda e: e.tensor_scalar(out=dz[:, 0:32], in0=dz[:, 0:32], scalar1=dz8[:, 7:8], scalar2=None, op0=ALU.is_ge), r=[Tdz], w=[Tdz])
        P.final_wait("sync")
        P.emit(es)
        print("arena peak bytes/partition:", ar.peak * 2, "ops:", {e: len(P.ops[e]) for e in ENGS}, "semcounts:", P.count, max(P.dma_cnt))
    return nc


CONSTS = None


def kernel(**inputs):
    global CONSTS
    if CONSTS is None:
        CONSTS = make_consts()
    nc = build()
    x = np.ascontiguousarray(inputs["x"], dtype=np.float32)
    shared = {k: np.ascontiguousarray(v, dtype=np.float32) for k, v in inputs.items() if k not in ("x", "mem")}
    shared["kv_norm_g"] = shared["kv_norm_g"].reshape(1, D)
    in_maps = []
    for c in range(8):
        b = c % 4
        m = dict(shared)
        m.update(CONSTS)
        m["x"] = x[b]
        m["mem"] = np.ascontiguousarray(inputs["mem"][b], dtype=np.float32)
        in_maps.append(m)
    res = run_bass_kernel_spmd(nc, in_maps, core_ids=list(range(8)))
    return np.stack([res.results[b]["out"] for b in range(4)], 0).astype(np.float32)
```

```python
import math
from contextlib import ExitStack

import numpy as np
import concourse.bass as bass
import concourse.mybir as mybir
from concourse.bass_utils import run_bass_kernel_spmd

F32 = mybir.dt.float32
BF16 = mybir.dt.bfloat16
AF = mybir.ActivationFunctionType
ALU = mybir.AluOpType

ENGS = ("tensor", "vector", "scalar", "gpsimd", "sync")

D = 2048
S = 2048
NT = 16
KC = 16
DFF = 5632
NFC = 44
ISQ = 1.0 / math.sqrt(128.0)
NEG = -1.0e6
TABW = 1536
ARENA_N = 102 * 1024


class Tile:
    __slots__ = ("name", "w", "r")

    def __init__(self, name=""):
        self.name = name
        self.w = None
        self.r = []


class Prog:
    def __init__(self, nc, n_dma_sems=32):
        self.nc = nc
        self.allow_silent = True
        self.pe_selfwait = False
        self.ops = {e: [] for e in ENGS}
        self.count = {e: 0 for e in ENGS}
        self.waited = {e: {} for e in ENGS}
        self.n_dma_sems = n_dma_sems
        self.dma_cnt = [0] * n_dma_sems
        self.dma_rr = 0
        self.sems = {}

    def _deps(self, eng, reads, writes):
        need = {}

        def add(tok):
            if tok is None:
                return
            k, v = tok
            if need.get(k, 0) < v:
                need[k] = v

        for t in reads:
            add(t.w)
        for t in writes:
            add(t.w)
            for tok in t.r:
                add(tok)
        waits = []
        wd = self.waited[eng]
        for k, v in need.items():
            if wd.get(k, 0) >= v:
                continue
            if k == eng and v > self.count[eng]:
                continue
            if k == eng and eng == "tensor" and not self.pe_selfwait:
                continue
            wd[k] = v
            waits.append((k, v))
        return waits

    def _mark(self, tok, reads, writes):
        for t in reads:
            t.r.append(tok)
            if len(t.r) > 48:
                m = {}
                for k, v in t.r:
                    if m.get(k, 0) < v:
                        m[k] = v
                t.r = list(m.items())
        for t in writes:
            t.w = tok
            t.r = []

    def op(self, eng, fn, reads=(), writes=(), silent=False):
        waits = self._deps(eng, reads, writes)
        if silent and self.allow_silent:
            tok = (eng, self.count[eng] + 1)
            self.ops[eng].append((waits, fn, None))
        else:
            self.count[eng] += 1
            tok = (eng, self.count[eng])
            self.ops[eng].append((waits, fn, (eng, 1)))
        self._mark(tok, reads, writes)
        return tok

    def dma(self, eng, fn, reads=(), writes=()):
        s = self.dma_rr
        self.dma_rr = (self.dma_rr + 1) % self.n_dma_sems
        key = ("dma", s)
        waits = self._deps(eng, reads, writes)
        prev = self.dma_cnt[s]
        if prev > 0 and self.waited[eng].get(key, 0) < prev:
            self.waited[eng][key] = prev
            waits.append((key, prev))
        self.dma_cnt[s] += 16
        tok = (key, self.dma_cnt[s])
        self.ops[eng].append((waits, fn, (key, 16)))
        self._mark(tok, reads, writes)
        return tok

    def barrier(self):
        toks = [(e, self.count[e]) for e in ENGS if self.count[e] > 0]
        toks += [(("dma", s), v) for s, v in enumerate(self.dma_cnt) if v > 0]
        for e in ENGS:
            waits = []
            for k, v in toks:
                if k == e:
                    continue
                if self.waited[e].get(k, 0) < v:
                    self.waited[e][k] = v
                    waits.append((k, v))
            if waits:
                self.ops[e].append((waits, None, None))

    def final_wait(self, eng="sync"):
        toks = [(e, self.count[e]) for e in ENGS if self.count[e] > 0 and e != eng]
        toks += [(("dma", s), v) for s, v in enumerate(self.dma_cnt) if v > 0]
        waits = []
        for k, v in toks:
            if self.waited[eng].get(k, 0) < v:
                self.waited[eng][k] = v
                waits.append((k, v))
        self.ops[eng].append((waits, None, None))

    def emit(self, es):
        nc = self.nc
        for e in ENGS:
            self.sems[e] = es.enter_context(nc.semaphore("s_" + e))
        for s in range(self.n_dma_sems):
            self.sems[("dma", s)] = es.enter_context(nc.semaphore("s_dma%d" % s))
        block = es.enter_context(nc.Block())
        sems = self.sems

        def run(engname):
            def body(e):
                for waits, fn, inc in self.ops[engname]:
                    for k, v in waits:
                        e.wait_ge(sems[k], v)
                    if fn is None:
                        continue
                    ins = fn(e)
                    if inc is not None:
                        ins.then_inc(sems[inc[0]], inc[1])
            return body

        block.sync(run("sync"))
        block.tensor(run("tensor"))
        block.vector(run("vector"))
        block.scalar(run("scalar"))
        block.gpsimd(run("gpsimd"))


class Arena:
    def __init__(self, base_ap_bf16, nelem_bf16):
        self.base = base_ap_bf16
        self.n = nelem_bf16
        self.top = 0
        self.stack = []
        self.peak = 0

    def push(self):
        self.stack.append(self.top)

    def pop(self):
        self.top = self.stack.pop()

    def alloc(self, nelem, dtype):
        w = 2 if dtype == F32 else 1
        n16 = (nelem * w + 31) // 32 * 32
        if self.top + n16 > self.n:
            raise MemoryError("SBUF arena overflow: need %d have %d" % (self.top + n16, self.n))
        ap = self.base[:, self.top:self.top + nelem * w]
        self.top += n16
        self.peak = max(self.peak, self.top)
        if dtype != BF16:
            ap = ap.bitcast(dtype)
        return ap


def _toeplitz(fn):
    j = np.arange(128)[:, None]
    c = np.arange(TABW)[None, :]
    dist = c - 511 - j
    valid = fn(dist)
    return np.where(valid, -dist.astype(np.float32), np.float32(NEG)).astype(np.float32)


def make_consts():
    c = {}
    c["c_ident"] = np.eye(128, dtype=np.float32)
    tabA = np.stack([
        _toeplitz(lambda d: (d >= 0) & (d <= 128)),
        _toeplitz(lambda d: (d >= 0) & (d <= 512) & (d % 4 == 0)),
        _toeplitz(lambda d: (d >= 0) & (d % 16 == 0)),
    ], 0)
    c["c_tabA"] = np.ascontiguousarray(tabA.transpose(1, 0, 2))
    tabB = np.stack([
        _toeplitz(lambda d: (d >= 0)),
        _toeplitz(lambda d: (d >= 0) & (d <= 511)),
    ], 0)
    c["c_tabB"] = np.ascontiguousarray(tabB.transpose(1, 0, 2))
    n = np.arange(128)[:, None]
    t = np.arange(S)[None, :]
    cend = 16 * n + 31
    dc = t - cend
    cm = np.where((dc >= 0) & (n < 127), -dc.astype(np.float32), np.float32(NEG)).astype(np.float32)
    c["c_cmp"] = np.ascontiguousarray(cm)
    cs = np.arange(127) * 16
    ss = np.arange(32) * 64
    ov = np.clip(np.minimum(cs[:, None] + 32, ss[None, :] + 64) - np.maximum(cs[:, None], ss[None, :]), 0, None) / 32.0
    ovp = np.zeros((128, 32), np.float32)
    ovp[:127] = ov
    c["c_ov"] = ovp
    E = np.zeros((128, 16, 128), np.float32)
    for kt in range(16):
        for s_ in range(128):
            E[2 * kt + s_ // 64, kt, s_] = 1.0
    c["c_E"] = E
    keep = np.zeros((128, 16, 32), np.float32)
    force = np.zeros((128, 16, 32), np.float32)
    for tt in range(16):
        for p in range(128):
            cur = (tt * 128 + p) // 64
            for jb in range(32):
                if jb == 0 or jb == cur or jb == cur - 1:
                    force[p, tt, jb] = 1.0e4 + (32 - jb)
                elif jb > cur:
                    force[p, tt, jb] = -1.0e4 - jb
                else:
                    keep[p, tt, jb] = 1.0
    c["c_keep"] = keep
    c["c_force"] = force
    oh = np.zeros((128, 36, 128), np.float32)
    for r in range(36):
        oh[r, r, :] = 1.0
    c["c_onehot"] = oh
    return c


class KB:
    def __init__(self, nc, es):
        self.nc = nc
        self.P = Prog(nc)
        arena_t = es.enter_context(nc.sbuf_tensor("arena", [128, ARENA_N], BF16))
        self.ar = Arena(arena_t[:, :], ARENA_N)
        self.ps = es.enter_context(nc.psum_tensor("ps", [128, 8, 512], F32))
        self.PB = [Tile("pb%d" % i) for i in range(8)]

    def V(self, fn, r=(), w=()):
        return self.P.op("vector", fn, r, w)

    def A(self, fn, r=(), w=()):
        return self.P.op("scalar", fn, r, w)

    def T(self, fn, r=(), w=(), silent=False):
        return self.P.op("tensor", fn, r, w, silent=silent)

    def G(self, fn, r=(), w=()):
        return self.P.op("gpsimd", fn, r, w)

    def DMA(self, fn, r=(), w=(), q="sync"):
        return self.P.dma(q, fn, r, w)

    def psb(self, b):
        return self.ps[:, b, :]

    def setup_consts(self, cin):
        ar = self.ar
        self.identf = ar.alloc(128, F32)
        self.Tidf = Tile("identf")
        self.identb = ar.alloc(128, BF16)
        self.Tidb = Tile("identb")
        self.onesb = ar.alloc(128, BF16)
        self.Tones = Tile("ones")
        self.DMA(lambda e: e.dma_start(out=self.identf, in_=cin["c_ident"][:, :]), w=[self.Tidf])
        self.V(lambda e: e.tensor_copy(out=self.identb, in_=self.identf), r=[self.Tidf], w=[self.Tidb])
        self.V(lambda e: e.memset(self.onesb, 1.0), w=[self.Tones])
        self.stage = [ar.alloc(2048, F32) for _ in range(3)]
        self.Tstage = [Tile("stage%d" % i) for i in range(3)]
        self.stage_i = 0
        self.cast_i = 0

    def wload(self, dram_ap, dst_ap, Tdst, k=None, n=None, cast="alt"):
        s = self.stage_i % 3
        self.stage_i += 1
        st = self.stage[s]
        if k is not None:
            st = st[:, 0:k * n].rearrange("p (k n) -> p k n", k=k)
        elif n is not None:
            st = st[:, 0:n]
        Ts = self.Tstage[s]
        self.DMA(lambda e: e.dma_start(out=st, in_=dram_ap), w=[Ts])
        if cast == "alt":
            cast = "gpsimd" if (self.cast_i % 2 == 0) else "vector"
            self.cast_i += 1
        self.P.op(cast, lambda e: e.tensor_copy(out=dst_ap, in_=st), [Ts], [Tdst])

    def rstd_of(self, src_ap, Tsrc, junk, Tjunk, ss, Tss):
        self.A(lambda e: e.activation(out=junk, in_=src_ap, func=AF.Square, accum_out=ss), r=[Tsrc], w=[Tjunk, Tss])
        self.A(lambda e: e.activation(out=ss, in_=ss, func=AF.Sqrt, scale=1.0 / D, bias=1e-6), r=[Tss], w=[Tss])
        self.V(lambda e: e.reciprocal(out=ss, in_=ss), r=[Tss], w=[Tss])

    def normT(self, src_fn, src_tiles, g_row_ap, hT3, ThT, ntt=NT):
        ar = self.ar
        ar.push()
        gt = ar.alloc(D, F32)
        Tg = Tile("g")
        self.DMA(lambda e: e.dma_start(out=gt, in_=g_row_ap.partition_broadcast(128)), w=[Tg])
        NB = 4 if ntt > 2 else 2
        xt = [ar.alloc(D, F32) for _ in range(NB)]
        Tx = [Tile("xt%d" % i) for i in range(NB)]
        hb = [ar.alloc(D, BF16) for _ in range(NB)]
        Th = [Tile("hb%d" % i) for i in range(NB)]
        junk = ar.alloc(D, BF16)
        Tj = Tile("junk")
        ss = [ar.alloc(1, F32) for _ in range(NB)]
        Tss = [Tile("ss%d" % i) for i in range(NB)]
        for tt in range(ntt):
            b = tt % NB
            src = src_fn(tt)
            self.DMA(lambda e, b=b, src=src: e.dma_start(out=xt[b], in_=src), r=[src_tiles[tt]], w=[Tx[b]])
            self.rstd_of(xt[b], Tx[b], junk, Tj, ss[b], Tss[b])
            self.V(lambda e, b=b: e.scalar_tensor_tensor(out=hb[b], in0=xt[b], scalar=ss[b], in1=gt, op0=ALU.mult, op1=ALU.mult),
                   r=[Tx[b], Tss[b], Tg], w=[Th[b]])
            pv = self.ps[:, 2 * b:2 * b + 2, :].bitcast(BF16).rearrange("p a b -> p (a b)")
            for k in range(KC):
                self.T(lambda e, b=b, k=k, pv=pv: e.transpose(out=pv[:, k * 128:(k + 1) * 128], in_=hb[b][:, k * 128:(k + 1) * 128], identity=self.identb),
                       r=[Th[b], self.Tidb], w=[self.PB[2 * b], self.PB[2 * b + 1]], silent=(k != KC - 1))
            self.A(lambda e, tt=tt, pv=pv: e.copy(out=hT3[:, :, tt * 128:(tt + 1) * 128], in_=pv.rearrange("p (k t) -> p k t", k=KC)),
                   r=[self.PB[2 * b], self.PB[2 * b + 1]], w=[ThT[tt]])
        ar.pop()
        self.P.barrier()

    def proj_fm(self, w3, Tw, hT3, ThT, out_ap, Tout, scale, ntok=S, banks=(4, 5)):
        nblk = (ntok + 511) // 512
        for tb in range(nblk):
            n = min(512, ntok - tb * 512)
            bk = banks[tb % 2]
            tts = list(range(tb * 4, tb * 4 + (n + 127) // 128))
            for k in range(KC):
                self.T(lambda e, k=k, tb=tb, n=n, bk=bk: e.matmul(self.ps[:, bk, 0:n], lhsT=w3[:, k, :], rhs=hT3[:, k, tb * 512:tb * 512 + n], start=(k == 0), stop=(k == KC - 1)),
                       r=[Tw] + [ThT[t] for t in tts], w=[self.PB[bk]], silent=(k != KC - 1))
            self.A(lambda e, tb=tb, n=n, bk=bk: e.mul(out_ap[:, tb * 512:tb * 512 + n], self.ps[:, bk, 0:n], scale), r=[self.PB[bk]], w=[Tout])

    def proj_tm(self, w3, Tw, hT3, ThT, out3, Tout, ntt=NT, banks=(6, 7), ncol=128):
        ngrp = (ntt + 3) // 4
        for g4 in range(ngrp):
            bk = banks[g4 % 2]
            cnt = min(4, ntt - g4 * 4)
            for i in range(cnt):
                tt = g4 * 4 + i
                for k in range(KC):
                    self.T(lambda e, k=k, tt=tt, i=i, bk=bk: e.matmul(self.ps[:, bk, i * ncol:(i + 1) * ncol], lhsT=hT3[:, k, tt * 128:(tt + 1) * 128], rhs=w3[:, k, 0:ncol], start=(k == 0), stop=(k == KC - 1)),
                           r=[Tw, ThT[tt]], w=[self.PB[bk]], silent=(k != KC - 1))
            self.V(lambda e, g4=g4, cnt=cnt, bk=bk: e.tensor_copy(out=out3[:, g4 * 4:g4 * 4 + cnt, :], in_=self.ps[:, bk, 0:cnt * ncol].rearrange("p (a d) -> p a d", a=cnt)),
                   r=[self.PB[bk]], w=[Tout])

    def attn_bufs(self):
        ar = self.ar
        self.sb = [ar.alloc(512, F32) for _ in range(4)]
        self.Tsb = [Tile("sb%d" % i) for i in range(4)]
        self.pt = [ar.alloc(512, BF16) for _ in range(4)]
        self.Tpt = [Tile("pt%d" % i) for i in range(4)]
        self.rec = ar.alloc(512, F32)
        self.Trec = Tile("rec")
        self.osb = [ar.alloc(512, BF16) for _ in range(2)]
        self.Tosb = [Tile("osb%d" % i) for i in range(2)]
        self.osb_i = 0

    def attn_units(self, units, q_ap, Tq, imp=None):
        nu = len(units)
        OB, DB = 2, 3

        def s_stage(i):
            u = units[i]
            b = i % 2
            pen = u.get("pen")
            self.T(lambda e, u=u, b=b: e.matmul(self.ps[:, b, :], lhsT=u["kT"], rhs=q_ap, start=True, stop=(u.get("pen") is None)),
                   r=list(u["rd"]) + [Tq], w=[self.PB[b]])
            if pen is not None:
                self.T(lambda e, pen=pen, b=b: e.matmul(self.ps[:, b, :], lhsT=pen[0], rhs=pen[1], start=False, stop=True),
                       r=list(pen[2]), w=[self.PB[b]])

        def mid(i):
            u = units[i]
            b = i % 2
            bias = float(u.get("bias", 0.0))
            if u.get("tab") is not None:
                self.V(lambda e, u=u, b=b: e.scalar_tensor_tensor(out=self.sb[b], in0=u["tab"], scalar=float(u["slope"]), in1=self.ps[:, b, :], op0=ALU.mult, op1=ALU.add),
                       r=[self.PB[b]] + list(u.get("tabrd", [])), w=[self.Tsb[b]])
                src, rd = self.sb[b], [self.Tsb[b]]
            else:
                src, rd = self.ps[:, b, :], [self.PB[b]]
            if bias != 0.0:
                self.A(lambda e, b=b, src=src, bias=bias: e.activation(out=self.pt[b], in_=src, func=AF.Exp, bias=bias), r=rd, w=[self.Tpt[b]])
            else:
                self.A(lambda e, b=b, src=src: e.activation(out=self.pt[b], in_=src, func=AF.Exp), r=rd, w=[self.Tpt[b]])

        def pv(i):
            u = units[i]
            b = i % 2
            first = (i == 0)
            last = (i == nu - 1)
            self.T(lambda e, u=u, b=b, first=first, last=last: e.matmul(self.ps[:, OB, :], lhsT=u["V"], rhs=self.pt[b], start=first, stop=last),
                   r=list(u["rd"]) + [self.Tpt[b]], w=[self.PB[OB]])
            self.T(lambda e, b=b, first=first, last=last: e.matmul(self.ps[:, DB, :], lhsT=self.onesb, rhs=self.pt[b], start=first, stop=last),
                   r=[self.Tones, self.Tpt[b]], w=[self.PB[DB]])
            if imp is not None:
                self.T(lambda e, b=b, first=first, last=last: e.matmul(self.ps[0:32, 7, :], lhsT=imp[0], rhs=self.pt[b], start=first, stop=last),
                       r=[imp[1], self.Tpt[b]], w=[self.PB[7]])

        s_stage(0)
        for i in range(nu):
            if i + 1 < nu:
                s_stage(i + 1)
            mid(i)
            pv(i)

    def recip_den(self):
        rec = self.rec
        self.V(lambda e: e.tensor_scalar(out=rec, in0=self.ps[:, 3, :], scalar1=1e-30, scalar2=None, op0=ALU.max), r=[self.PB[3]], w=[self.Trec])
        self.V(lambda e: e.reciprocal(out=rec, in_=rec), r=[self.Trec], w=[self.Trec])

    def finalize_plain(self, dst_dram, Tdst):
        self.recip_den()
        ob = self.osb_i % 2
        self.osb_i += 1
        rec = self.rec
        osb = self.osb[ob]
        self.V(lambda e: e.tensor_tensor(out=osb, in0=self.ps[:, 2, :], in1=rec, op=ALU.mult), r=[self.PB[2], self.Trec], w=[self.Tosb[ob]])
        self.DMA(lambda e: e.dma_start(out=dst_dram, in_=osb.rearrange("p (t c) -> p t c", t=4)), r=[self.Tosb[ob]], w=[Tdst])

    def mem_kv(self, mem_ap, Tmem, g_row, wkv, hT3mem, ThTm, kTm, TkTm, Vm, TVm, wbuf, Twbuf):
        self.normT(lambda tt: mem_ap[tt * 128:(tt + 1) * 128, :], [Tmem, Tmem], g_row, hT3mem, ThTm, ntt=2)
        wcols = wkv.rearrange("(k p) n -> p k n", p=128)
        for h in range(4):
            b = h % 2
            self.wload(wcols[:, :, h * 128:(h + 1) * 128], wbuf[b], Twbuf[b], k=KC, n=128)
            self.proj_fm(wbuf[b], Twbuf[b], hT3mem, ThTm, kTm[:, h, :], TkTm, 1.0, ntok=256)
        for h in range(4):
            b = h % 2
            self.wload(wcols[:, :, 512 + h * 128:512 + (h + 1) * 128], wbuf[b], Twbuf[b], k=KC, n=128)
            self.proj_tm(wbuf[b], Twbuf[b], hT3mem, ThTm, Vm[:, h, :, :], TVm, ntt=2)

    def resid_update(self, y_ap, Ty, x_src_ap, Tsrc, x_dst_ap, Tdst, gt, Tg, bufs):
        xt, Tx, tmp, Ttmp, junk, Tj, ss, Tss = bufs
        self.DMA(lambda e: e.dma_start(out=xt, in_=x_src_ap), r=[Tsrc], w=[Tx])
        self.rstd_of(y_ap, Ty, junk, Tj, ss, Tss)
        self.V(lambda e: e.scalar_tensor_tensor(out=tmp, in0=y_ap, scalar=ss, in1=gt, op0=ALU.mult, op1=ALU.mult), r=[Ty, Tss, Tg], w=[Ttmp])
        self.V(lambda e: e.tensor_tensor(out=xt, in0=xt, in1=tmp, op=ALU.add), r=[Ttmp, Tx], w=[Tx])
        self.DMA(lambda e: e.dma_start(out=x_dst_ap, in_=xt), r=[Tx], w=[Tdst])


def alibi(n, i):
    return float(2.0 ** (-8.0 * (i + 1) / n))


def build(n_layers=4, stop_after_mixer=False, l_start=0, dbg=False, decl=None, allow_silent=True, pe_selfwait=False, rep=1, dummy=(), seq=None):
    nc = bass.Bass("TRN2", target_bir_lowering=False)

    def din(name, shape):
        if decl is not None and name not in decl:
            return nc.dram_tensor(name, [1] * len(shape), F32).ap()
        return nc.dram_tensor(name, list(shape), F32, kind="ExternalInput").ap()

    x_in = din("x", [S, D])
    mem = din("mem", [256, D])
    norm_g = din("norm_g", [4, 5, D])
    a_w_in = din("a_w_in", [2, D, 9728])
    a_w_out = din("a_w_out", [2, 1536, D])
    b_w_in = din("b_w_in", [2, D, 2084])
    b_w_out = din("b_w_out", [2, 2048, D])
    mem_w_kv = din("mem_w_kv", [4, D, 1024])
    ffn_w_gu = din("ffn_w_gu", [4, D, 2 * DFF])
    ffn_w_down = din("ffn_w_down", [4, DFF, D])
    kv_norm_g = din("kv_norm_g", [1, D])
    kv_w = din("kv_w", [D, 3072])
    cmp_pe = din("cmp_pe", [2, 32, 128])
    cmp_wk1 = din("cmp_wk1", [4096, 512])
    cmp_wk2 = din("cmp_wk2", [512, 128])
    cmp_wv1 = din("cmp_wv1", [4096, 512])
    cmp_wv2 = din("cmp_wv2", [512, 128])
    cshapes = {"c_ident": [128, 128], "c_tabA": [128, 3, TABW], "c_tabB": [128, 2, TABW], "c_cmp": [128, S],
               "c_ov": [128, 32], "c_E": [128, 16, 128], "c_keep": [128, 16, 32], "c_force": [128, 16, 32],
               "c_onehot": [128, 36, 128]}
    cin = {k: din(k, v) for k, v in cshapes.items()}
    out = nc.dram_tensor("out", [S, D], F32, kind="ExternalOutput").ap()
    oT_d = nc.dram_tensor("oT_d", [NT, 128, 16, 128], BF16).ap()
    qT_d = nc.dram_tensor("qT_d", [16, 128, S], BF16).ap()
    actT_d = nc.dram_tensor("actT_d", [NT, 128, NFC, 128], BF16).ap()
    y_d = nc.dram_tensor("y_d", [S, D], F32).ap()
    sh_d = nc.dram_tensor("sh_d", [4, 4, 128, S], BF16).ap()
    kc_d = nc.dram_tensor("kc_d", [2, 128, 4, 128], BF16).ap()

    dbg_t = {}
    if dbg:
        for l_ in range(4):
            for nm in ("xm", "x"):
                dbg_t[nm + str(l_)] = nc.dram_tensor("dbg_" + nm + str(l_), [S, D], F32, kind="ExternalOutput").ap()

    with ExitStack() as es:
        kb = KB(nc, es)
        P = kb.P
        P.allow_silent = allow_silent
        P.pe_selfwait = pe_selfwait
        ar = kb.ar
        ps = kb.ps
        PB = kb.PB
        kb.setup_consts(cin)

        Tmem = Tile("mem")
        Tin = Tile("x_in")
        Tout = [Tile("out%d" % i) for i in range(NT)]
        ToT = [Tile("oT%d" % i) for i in range(16)]
        TqT = [Tile("qT%d" % i) for i in range(16)]
        Tact = [Tile("act%d" % i) for i in range(NFC)]
        Ty = [Tile("y%d" % i) for i in range(NT)]
        Tsh = [Tile("sh%d" % i) for i in range(4)]
        Tkc = Tile("kc")

        state = {"first": True}

        def xsrc(tt):
            if state["first"]:
                return x_in[tt * 128:(tt + 1) * 128, :], Tin
            return out[tt * 128:(tt + 1) * 128, :], Tout[tt]

        def w_out_phase(w_out_l, nheads, g_row):
            P.barrier()
            ar.push()
            wo = ar.alloc(nheads * D, BF16).rearrange("p (h n) -> p h n", h=nheads)
            Two = [Tile("wo%d" % i) for i in range(nheads)]
            for h in range(nheads):
                kb.wload(w_out_l[h * 128:(h + 1) * 128, :], wo[:, h, :], Two[h], n=D)
            gt = ar.alloc(D, F32)
            Tg = Tile("g")
            kb.DMA(lambda e: e.dma_start(out=gt, in_=g_row.partition_broadcast(128)), w=[Tg])
            ot = [ar.alloc(nheads * 128, BF16).rearrange("p (h t) -> p h t", h=nheads) for _ in range(2)]
            Tot = [Tile("ot%d" % i) for i in range(2)]
            xts = [ar.alloc(D, F32) for _ in range(2)]
            Txs = [Tile("xt%d" % i) for i in range(2)]
            sss = [ar.alloc(1, F32) for _ in range(2)]
            Tsss = [Tile("ss%d" % i) for i in range(2)]
            tmp = ar.alloc(D, F32)
            Ttmp = Tile("tmp")
            junk = ar.alloc(D, BF16)
            Tj = Tile("junk")
            g4 = gt.rearrange("p (a b) -> p a b", a=4)
            t4 = tmp.rearrange("p (a b) -> p a b", a=4)
            j4 = junk.rearrange("p (a b) -> p a b", a=4)
            for tt in range(NT):
                b = tt % 2
                pb0 = 4 * b
                kb.DMA(lambda e, b=b, tt=tt: e.dma_start(out=ot[b], in_=oT_d[tt, :, 0:nheads, :]), r=ToT[:nheads], w=[Tot[b]])
                for c4 in range(4):
                    for h in range(nheads):
                        kb.T(lambda e, b=b, c4=c4, h=h, pb0=pb0: e.matmul(ps[:, pb0 + c4, :], lhsT=ot[b][:, h, :], rhs=wo[:, h, c4 * 512:(c4 + 1) * 512], start=(h == 0), stop=(h == nheads - 1)),
                             r=[Tot[b], Two[h]], w=[PB[pb0 + c4]], silent=(h != nheads - 1))
                src, Tsrc = xsrc(tt)
                xt, Tx, ss, Tss = xts[b], Txs[b], sss[b], Tsss[b]
                y_ap = ps[:, pb0:pb0 + 4, :]
                pbs = PB[pb0:pb0 + 4]
                kb.DMA(lambda e, xt=xt, src=src: e.dma_start(out=xt, in_=src), r=[Tsrc], w=[Tx])
                kb.A(lambda e, ss=ss, y_ap=y_ap: e.activation(out=j4, in_=y_ap, func=AF.Square, accum_out=ss), r=pbs, w=[Tj, Tss])
                kb.A(lambda e, ss=ss: e.activation(out=ss, in_=ss, func=AF.Sqrt, scale=1.0 / D, bias=1e-6), r=[Tss], w=[Tss])
                kb.V(lambda e, ss=ss: e.reciprocal(out=ss, in_=ss), r=[Tss], w=[Tss])
                kb.V(lambda e, ss=ss, y_ap=y_ap: e.scalar_tensor_tensor(out=t4, in0=y_ap, scalar=ss, in1=g4, op0=ALU.mult, op1=ALU.mult),
                     r=pbs + [Tss, Tg], w=[Ttmp])
                kb.V(lambda e, xt=xt: e.tensor_tensor(out=xt, in0=xt, in1=tmp, op=ALU.add), r=[Ttmp, Tx], w=[Tx])
                kb.DMA(lambda e, xt=xt, tt=tt: e.dma_start(out=out[tt * 128:(tt + 1) * 128, :], in_=xt), r=[Tx], w=[Tout[tt]])
            ar.pop()
            P.barrier()
            state["first"] = False

        def ffn_phase(l):
            P.barrier()
            ar.push()
            wd = [ar.alloc(NFC * 512, BF16).rearrange("p (f n) -> p f n", f=NFC), None]
            Twd = [[Tile("wd%d_%d" % (i, q)) for q in range(11)] for i in range(2)]
            wdv = ffn_w_down[l].rearrange("(f p) n -> p f n", p=128)

            def load_wd_unit(c4, q):
                b = c4 % 2
                kb.wload(wdv[:, q * 4:(q + 1) * 4, c4 * 512:(c4 + 1) * 512], wd[b][:, q * 4:(q + 1) * 4, :], Twd[b][q], k=4, n=512)

            ar.push()
            hT = ar.alloc(KC * S, BF16)
            hT3 = hT.rearrange("p (k t) -> p k t", k=KC)
            ThT = [Tile("hT%d" % i) for i in range(NT)]
            kb.normT(lambda tt: out[tt * 128:(tt + 1) * 128, :], Tout, norm_g[l, 2:3, :], hT3, ThT)
            wg = [ar.alloc(KC * 128, BF16).rearrange("p (k n) -> p k n", k=KC) for _ in range(2)]
            wu = [ar.alloc(KC * 128, BF16).rearrange("p (k n) -> p k n", k=KC) for _ in range(2)]
            Twg = [Tile("wg%d" % i) for i in range(2)]
            Twu = [Tile("wu%d" % i) for i in range(2)]
            sg = [ar.alloc(512, F32) for _ in range(2)]
            Tsg = [Tile("sg%d" % i) for i in range(2)]
            ao = [ar.alloc(512, BF16) for _ in range(2)]
            Tao = [Tile("ao%d" % i) for i in range(2)]
            wgu = ffn_w_gu[l].rearrange("(k p) n -> p k n", p=128)

            def load_fc(fc):
                b = fc % 2
                kb.wload(wgu[:, :, fc * 128:(fc + 1) * 128], wg[b], Twg[b], k=KC, n=128)
                kb.wload(wgu[:, :, DFF + fc * 128:DFF + (fc + 1) * 128], wu[b], Twu[b], k=KC, n=128)

            load_fc(0)
            it = 0
            for fc in range(NFC):
                if fc + 1 < NFC:
                    load_fc(fc + 1)
                if 30 <= fc < 41:
                    load_wd_unit(0, fc - 30)
                b = fc % 2
                for tb in range(4):
                    gb, ub = 4 + 2 * (it % 2), 5 + 2 * (it % 2)
                    i2 = it % 2
                    it += 1
                    tts = [ThT[t] for t in range(tb * 4, tb * 4 + 4)]
                    for k in range(KC):
                        kb.T(lambda e, k=k, b=b, tb=tb, gb=gb: e.matmul(ps[:, gb, :], lhsT=wg[b][:, k, :], rhs=hT3[:, k, tb * 512:(tb + 1) * 512], start=(k == 0), stop=(k == KC - 1)),
                             r=[Twg[b]] + tts, w=[PB[gb]], silent=(k != KC - 1))
                    for k in range(KC):
                        kb.T(lambda e, k=k, b=b, tb=tb, ub=ub: e.matmul(ps[:, ub, :], lhsT=wu[b][:, k, :], rhs=hT3[:, k, tb * 512:(tb + 1) * 512], start=(k == 0), stop=(k == KC - 1)),
                             r=[Twu[b]] + tts, w=[PB[ub]], silent=(k != KC - 1))
                    kb.A(lambda e, i2=i2, gb=gb: e.activation(out=sg[i2], in_=ps[:, gb, :], func=AF.Silu), r=[PB[gb]], w=[Tsg[i2]])
                    kb.V(lambda e, i2=i2, ub=ub: e.tensor_tensor(out=ao[i2], in0=ps[:, ub, :], in1=sg[i2], op=ALU.mult), r=[PB[ub], Tsg[i2]], w=[Tao[i2]])
                    kb.DMA(lambda e, i2=i2, fc=fc, tb=tb: e.dma_start(out=actT_d[tb * 4:(tb + 1) * 4, :, fc, :].rearrange("t p c -> p t c"), in_=ao[i2].rearrange("p (t c) -> p t c", t=4)),
                           r=[Tao[i2]], w=[Tact[fc]])
            ar.pop()
            P.barrier()
            wd[1] = ar.alloc(NFC * 512, BF16).rearrange("p (f n) -> p f n", f=NFC)
            at = [ar.alloc(NFC * 128, BF16).rearrange("p (f t) -> p f t", f=NFC) for _ in range(2)]
            Tat = [Tile("at%d" % i) for i in range(2)]
            yo = [ar.alloc(512, F32) for _ in range(2)]
            Tyo = [Tile("yo%d" % i) for i in range(2)]
            it = 0
            for c4 in range(4):
                b = c4 % 2
                for tt in range(NT):
                    if c4 + 1 < 4 and 1 <= tt < 12:
                        load_wd_unit(c4 + 1, tt - 1)
                    i2 = it % 2
                    bk = 4 + (it % 2)
                    it += 1
                    kb.DMA(lambda e, i2=i2, tt=tt: e.dma_start(out=at[i2], in_=actT_d[tt]), r=Tact, w=[Tat[i2]])
                    for f in range(NFC):
                        kb.T(lambda e, f=f, i2=i2, b=b, bk=bk: e.matmul(ps[:, bk, :], lhsT=at[i2][:, f, :], rhs=wd[b][:, f, :], start=(f == 0), stop=(f == NFC - 1)),
                             r=[Tat[i2], Twd[b][f // 4]], w=[PB[bk]], silent=(f != NFC - 1))
                    kb.A(lambda e, i2=i2, bk=bk: e.copy(out=yo[i2], in_=ps[:, bk, :]), r=[PB[bk]], w=[Tyo[i2]])
                    kb.DMA(lambda e, i2=i2, tt=tt, c4=c4: e.dma_start(out=y_d[tt * 128:(tt + 1) * 128, c4 * 512:(c4 + 1) * 512], in_=yo[i2]), r=[Tyo[i2]], w=[Ty[tt]])
            ar.pop()
            P.barrier()
            ar.push()
            gt = ar.alloc(D, F32)
            Tg = Tile("g")
            kb.DMA(lambda e: e.dma_start(out=gt, in_=norm_g[l, 3:4, :].partition_broadcast(128)), w=[Tg])
            yt = [ar.alloc(D, F32) for _ in range(4)]
            Tyt = [Tile("yt%d" % i) for i in range(4)]
            xt = [ar.alloc(D, F32) for _ in range(4)]
            Tx = [Tile("xt%d" % i) for i in range(4)]
            ss = [ar.alloc(1, F32) for _ in range(4)]
            Tss = [Tile("ss%d" % i) for i in range(4)]
            tmp = ar.alloc(D, F32)
            Ttmp = Tile("tmp")
            junk = ar.alloc(D, BF16)
            Tj = Tile("junk")
            for tt in range(NT):
                b = tt % 4
                kb.DMA(lambda e, b=b, tt=tt: e.dma_start(out=yt[b], in_=y_d[tt * 128:(tt + 1) * 128, :]), r=[Ty[tt]], w=[Tyt[b]])
                kb.resid_update(yt[b], Tyt[b], out[tt * 128:(tt + 1) * 128, :], Tout[tt], out[tt * 128:(tt + 1) * 128, :], Tout[tt], gt, Tg,
                                (xt[b], Tx[b], tmp, Ttmp, junk, Tj, ss[b], Tss[b]))
            ar.pop()
            P.barrier()

        def layer_a(l):
            P.barrier()
            ar.push()
            hT = ar.alloc(KC * S, BF16)
            hT3 = hT.rearrange("p (k t) -> p k t", k=KC)
            ThT = [Tile("hT%d" % i) for i in range(NT)]
            wb = [ar.alloc(KC * 128, BF16).rearrange("p (k n) -> p k n", k=KC) for _ in range(3)]
            Twb = [Tile("wb%d" % i) for i in range(3)]
            hTm = ar.alloc(KC * 256, BF16).rearrange("p (k t) -> p k t", k=KC)
            ThTm = [Tile("hTm0"), Tile("hTm1")]
            kTm = ar.alloc(4 * 256, BF16).rearrange("p (h t) -> p h t", h=4)
            TkTm = Tile("kTm")
            Vm = ar.alloc(4 * 2 * 128, BF16).rearrange("p (h a d) -> p h a d", h=4, a=2)
            TVm = Tile("Vm")
            kb.mem_kv(mem, Tmem, norm_g[l, 4:5, :], mem_w_kv[l], hTm, ThTm, kTm, TkTm, Vm, TVm, wb, Twb)
            if state["first"]:
                kb.normT(lambda tt: x_in[tt * 128:(tt + 1) * 128, :], [Tin] * NT, norm_g[l, 0:1, :], hT3, ThT)
            else:
                kb.normT(lambda tt: out[tt * 128:(tt + 1) * 128, :], Tout, norm_g[l, 0:1, :], hT3, ThT)
            tab = ar.alloc(3 * TABW, F32).rearrange("p (g c) -> p g c", g=3)
            Ttab = Tile("tabA")
            kb.DMA(lambda e: e.dma_start(out=tab, in_=cin["c_tabA"]), w=[Ttab])
            kb.attn_bufs()
            qT = [ar.alloc(S, BF16) for _ in range(3)]
            kT = [ar.alloc(S, BF16) for _ in range(3)]
            Vt = [ar.alloc(NT * 128, BF16).rearrange("p (t d) -> p t d", t=NT) for _ in range(3)]
            Tq = [Tile("q%d" % i) for i in range(3)]
            Tk = [Tile("k%d" % i) for i in range(3)]
            Tv = [Tile("v%d" % i) for i in range(3)]
            win = a_w_in[l].rearrange("(k p) n -> p k n", p=128)
            for j in range(8):
                for g in range(3):
                    hq = g * 8 + j
                    kb.wload(win[:, :, hq * 128:(hq + 1) * 128], wb[0], Twb[0], k=KC, n=128)
                    kb.proj_fm(wb[0], Twb[0], hT3, ThT, qT[g], Tq[g], ISQ)
                    kb.wload(win[:, :, (24 + hq) * 128:(24 + hq + 1) * 128], wb[1], Twb[1], k=KC, n=128)
                    kb.proj_fm(wb[1], Twb[1], hT3, ThT, kT[g], Tk[g], 1.0)
                    kb.wload(win[:, :, (48 + hq) * 128:(48 + hq + 1) * 128], wb[2], Twb[2], k=KC, n=128)
                    kb.proj_tm(wb[2], Twb[2], hT3, ThT, Vt[g], Tv[g])
                for tb in range(4):
                    first = True
                    for g in range(3):
                        slope = alibi(24, g * 8 + j)
                        lo = {0: 4 * tb - 1, 1: 4 * tb - 4, 2: 0}[g]
                        units = []
                        for kt in range(max(0, lo), 4 * tb + 4):
                            delta = 512 * tb - 128 * kt
                            de = min(delta, 128) if g == 2 else delta
                            bias = -slope * (delta - de)
                            units.append(dict(kT=kT[g][:, kt * 128:(kt + 1) * 128], V=Vt[g][:, kt, :], rd=[Tk[g], Tv[g]],
                                              tab=tab[:, g, de + 511:de + 511 + 512], tabrd=[Ttab], slope=slope, bias=bias))
                        kb._chain_first = first
                        run_units_chain(units, qT[g][:, tb * 512:(tb + 1) * 512], Tq[g], first, g == 2)
                        first = False
                    kb.finalize_plain(oT_d[tb * 4:(tb + 1) * 4, :, j, :].rearrange("t p c -> p t c"), ToT[j])
            for h in range(4):
                kb.wload(win[:, :, 9216 + h * 128:9216 + (h + 1) * 128], wb[0], Twb[0], k=KC, n=128)
                kb.proj_fm(wb[0], Twb[0], hT3, ThT, qT[0], Tq[0], ISQ)
                for tb in range(4):
                    units = [dict(kT=kTm[:, h, kt * 128:(kt + 1) * 128], V=Vm[:, h, kt, :], rd=[TkTm, TVm], tab=None) for kt in range(2)]
                    run_units_chain(units, qT[0][:, tb * 512:(tb + 1) * 512], Tq[0], True, True)
                    kb.finalize_plain(oT_d[tb * 4:(tb + 1) * 4, :, 8 + h, :].rearrange("t p c -> p t c"), ToT[8 + h])
            ar.pop()
            w_out_phase(a_w_out[l], 12, norm_g[l, 1:2, :])

        def run_units_chain(units, q_ap, Tq, first, last):
            nu = len(units)
            OB, DB = 2, 3
            kbx = kb
            sbL, ptL, onesL = kb.sb, kb.pt, kb.onesb
            SBK = (0, 1, 4, 5)
            LA = 3

            def s_stage(i):
                u = units[i]
                b = i % 4
                pen = u.get("pen")
                kbx.T(lambda e, u=u, b=b: e.matmul(ps[:, SBK[b], :], lhsT=u["kT"], rhs=q_ap, start=True, stop=(u.get("pen") is None)),
                      r=list(u["rd"]) + [Tq], w=[PB[SBK[b]]], silent=(pen is not None))
                if pen is not None:
                    kbx.T(lambda e, pen=pen, b=b: e.matmul(ps[:, SBK[b], :], lhsT=pen[0], rhs=pen[1], start=False, stop=True),
                          r=list(pen[2]), w=[PB[SBK[b]]])

            def mid(i):
                u = units[i]
                b = i % 4
                bias = float(u.get("bias", 0.0))
                if u.get("tab") is not None:
                    kbx.V(lambda e, u=u, b=b: e.scalar_tensor_tensor(out=sbL[b], in0=u["tab"], scalar=float(u["slope"]), in1=ps[:, SBK[b], :], op0=ALU.mult, op1=ALU.add),
                          r=[PB[SBK[b]]] + list(u.get("tabrd", [])), w=[kbx.Tsb[b]])
                    src, rd = sbL[b], [kbx.Tsb[b]]
                else:
                    src, rd = ps[:, SBK[b], :], [PB[SBK[b]]]
                if bias != 0.0:
                    kbx.A(lambda e, b=b, src=src, bias=bias: e.activation(out=ptL[b], in_=src, func=AF.Exp, bias=bias), r=rd, w=[kbx.Tpt[b]])
                else:
                    kbx.A(lambda e, b=b, src=src: e.activation(out=ptL[b], in_=src, func=AF.Exp), r=rd, w=[kbx.Tpt[b]])

            def pv(i):
                u = units[i]
                b = i % 4
                st = first and (i == 0)
                sp = last and (i == nu - 1)
                imp = u.get("imp")
                kbx.T(lambda e, u=u, b=b: e.matmul(ps[:, OB, :], lhsT=u["V"], rhs=ptL[b], start=st, stop=sp),
                      r=list(u["rd"]) + [kbx.Tpt[b]], w=[PB[OB]], silent=True)
                kbx.T(lambda e, b=b: e.matmul(ps[:, DB, :], lhsT=onesL, rhs=ptL[b], start=st, stop=sp),
                      r=[kbx.Tones, kbx.Tpt[b]], w=[PB[DB]], silent=(imp is not None))
                if imp is not None:
                    kbx.T(lambda e, b=b, imp=imp: e.matmul(ps[0:32, 7, :], lhsT=imp[0], rhs=ptL[b], start=st, stop=sp),
                          r=[imp[1], kbx.Tpt[b]], w=[PB[7]])

            for i in range(min(LA, nu)):
                s_stage(i)
            for i in range(nu):
                if i + LA < nu:
                    s_stage(i + LA)
                mid(i)
                pv(i)

        def shared_kv_phase():
            P.barrier()
            ar.push()
            hT = ar.alloc(KC * S, BF16)
            hT3 = hT.rearrange("p (k t) -> p k t", k=KC)
            ThT = [Tile("hT%d" % i) for i in range(NT)]
            if state["first"]:
                kb.normT(lambda tt: x_in[tt * 128:(tt + 1) * 128, :], [Tin] * NT, kv_norm_g[0:1, :], hT3, ThT)
            else:
                kb.normT(lambda tt: out[tt * 128:(tt + 1) * 128, :], Tout, kv_norm_g[0:1, :], hT3, ThT)
            wb = [ar.alloc(KC * 128, BF16).rearrange("p (k n) -> p k n", k=KC) for _ in range(2)]
            Twb = [Tile("wb%d" % i) for i in range(2)]
            tmpT = [ar.alloc(S, BF16) for _ in range(2)]
            Ttmp = [Tile("tmpT%d" % i) for i in range(2)]
            kvw = kv_w.rearrange("(k p) n -> p k n", p=128)
            it = 0
            for g in range(4):
                for which, slot, fm in ((2, 0, True), (3, 1, False), (4, 2, True), (5, 3, False)):
                    b = it % 2
                    it += 1
                    col = which * 512 + g * 128
                    kb.wload(kvw[:, :, col:col + 128], wb[b], Twb[b], k=KC, n=128)
                    if fm:
                        kb.proj_fm(wb[b], Twb[b], hT3, ThT, tmpT[b], Ttmp[b], 1.0)
                    else:
                        kb.proj_tm(wb[b], Twb[b], hT3, ThT, tmpT[b].rearrange("p (t d) -> p t d", t=NT), Ttmp[b])
                    kb.DMA(lambda e, b=b, g=g, slot=slot: e.dma_start(out=sh_d[g, slot], in_=tmpT[b]), r=[Ttmp[b]], w=[Tsh[g]])
            w1sb = ar.alloc(32 * 512, BF16).rearrange("p (l n) -> p l n", l=32)
            Tw1 = [Tile("w1_%d" % q) for q in range(8)]
            w2sb = ar.alloc(4 * 128, BF16).rearrange("p (c n) -> p c n", c=4)
            Tw2 = Tile("w2")
            pef = ar.alloc(128, F32)
            Tpef = Tile("pef")
            peT = ar.alloc(32, BF16)
            TpeT = Tile("peT")
            b1 = ar.alloc(4, F32)
            Tb1 = Tile("b1")
            hx = ar.alloc(128, F32)
            Thx = Tile("hx")
            x2 = ar.alloc(128, F32)
            Tx2 = Tile("x2")
            sgm = ar.alloc(128, F32)
            Tsgm = Tile("sgm")
            gel = ar.alloc(4 * 128, BF16).rearrange("p (c n) -> p c n", c=4)
            Tgel = Tile("gel")
            csb = ar.alloc(4 * 128, BF16).rearrange("p (g n) -> p g n", g=4)
            Tcsb = Tile("csb")
            for which, w1, w2 in ((0, cmp_wk1, cmp_wk2), (1, cmp_wv1, cmp_wv2)):
                w1v = w1.rearrange("(l p) n -> p l n", p=128)
                for q in range(8):
                    kb.wload(w1v[:, q * 4:(q + 1) * 4, :], w1sb[:, q * 4:(q + 1) * 4, :], Tw1[q], k=4, n=512)
                kb.wload(w2.rearrange("(c p) n -> p c n", p=128), w2sb, Tw2, k=4, n=128)
                kb.DMA(lambda e, which=which: e.dma_start(out=pef[0:32, :], in_=cmp_pe[which]), w=[Tpef])
                kb.T(lambda e: e.transpose(out=ps[:, 6, 0:32], in_=pef[0:32, :], identity=kb.identf[0:32, 0:32]), r=[Tpef, kb.Tidf], w=[PB[6]])
                kb.V(lambda e: e.tensor_copy(out=peT, in_=ps[:, 6, 0:32]), r=[PB[6]], w=[TpeT])
                for hc in range(4):
                    for l_ in range(32):
                        kb.T(lambda e, hc=hc, l_=l_: e.matmul(ps[:, 7, hc:hc + 1], lhsT=w1sb[:, l_, hc * 128:(hc + 1) * 128], rhs=peT[:, l_:l_ + 1], start=(l_ == 0), stop=(l_ == 31)),
                             r=[Tw1[l_ // 4], TpeT], w=[PB[7]], silent=(l_ != 31))
                kb.V(lambda e: e.tensor_copy(out=b1, in_=ps[:, 7, 0:4]), r=[PB[7]], w=[Tb1])
                kb.V(lambda e: e.memset(csb, 0.0), w=[Tcsb])
                for g in range(4):
                    b = it % 2
                    it += 1
                    col = which * 512 + g * 128
                    kb.wload(kvw[:, :, col:col + 128], wb[b], Twb[b], k=KC, n=128)
                    kb.proj_fm(wb[b], Twb[b], hT3, ThT, tmpT[b], Ttmp[b], 1.0)
                    kr3 = tmpT[b].rearrange("p (n s) -> p n s", s=16)
                    for hc in range(4):
                        bk = 4 + hc % 2
                        for l_ in range(32):
                            rhs = kr3[:, 0:127, l_] if l_ < 16 else kr3[:, 1:128, l_ - 16]
                            kb.T(lambda e, hc=hc, l_=l_, rhs=rhs, bk=bk: e.matmul(ps[:, bk, 0:127], lhsT=w1sb[:, l_, hc * 128:(hc + 1) * 128], rhs=rhs, start=(l_ == 0), stop=(l_ == 31)),
                                 r=[Tw1[l_ // 4], Ttmp[b]], w=[PB[bk]], silent=(l_ != 31))
                        kb.V(lambda e, hc=hc, bk=bk: e.tensor_scalar(out=hx[:, 0:127], in0=ps[:, bk, 0:127], scalar1=b1[:, hc:hc + 1], scalar2=None, op0=ALU.add), r=[PB[bk], Tb1], w=[Thx])
                        kb.V(lambda e: e.tensor_tensor(out=x2[:, 0:127], in0=hx[:, 0:127], in1=hx[:, 0:127], op=ALU.mult), r=[Thx], w=[Tx2])
                        kb.V(lambda e: e.tensor_scalar(out=x2[:, 0:127], in0=x2[:, 0:127], scalar1=0.044715, scalar2=1.0, op0=ALU.mult, op1=ALU.add), r=[Tx2], w=[Tx2])
                        kb.V(lambda e: e.tensor_tensor(out=x2[:, 0:127], in0=x2[:, 0:127], in1=hx[:, 0:127], op=ALU.mult), r=[Tx2, Thx], w=[Tx2])
                        kb.A(lambda e: e.activation(out=sgm[:, 0:127], in_=x2[:, 0:127], func=AF.Sigmoid, scale=1.5957691216057308), r=[Tx2], w=[Tsgm])
                        kb.V(lambda e, hc=hc: e.tensor_tensor(out=gel[:, hc, 0:127], in0=hx[:, 0:127], in1=sgm[:, 0:127], op=ALU.mult), r=[Thx, Tsgm], w=[Tgel])
                    if which == 0:
                        for hc in range(4):
                            kb.T(lambda e, hc=hc: e.matmul(ps[:, 6, 0:127], lhsT=w2sb[:, hc, :], rhs=gel[:, hc, 0:127], start=(hc == 0), stop=(hc == 3)), r=[Tw2, Tgel], w=[PB[6]], silent=(hc != 3))
                        kb.V(lambda e, g=g: e.tensor_copy(out=csb[:, g, 0:127], in_=ps[:, 6, 0:127]), r=[PB[6]], w=[Tcsb])
                    else:
                        for hc in range(4):
                            kb.T(lambda e, hc=hc: e.matmul(ps[0:127, 6, 0:128], lhsT=gel[:, hc, 0:127], rhs=w2sb[:, hc, :], start=(hc == 0), stop=(hc == 3)), r=[Tw2, Tgel], w=[PB[6]], silent=(hc != 3))
                        kb.V(lambda e, g=g: e.tensor_copy(out=csb[0:127, g, :], in_=ps[0:127, 6, 0:128]), r=[PB[6]], w=[Tcsb])
                kb.DMA(lambda e, which=which: e.dma_start(out=kc_d[which], in_=csb), r=[Tcsb], w=[Tkc])
            ar.pop()
            P.barrier()

        def layer_b(l):
            lb = l - 2
            P.barrier()
            ar.push()
            kTm = ar.alloc(4 * 256, BF16).rearrange("p (h t) -> p h t", h=4)
            TkTm = Tile("kTm")
            Vm = ar.alloc(4 * 2 * 128, BF16).rearrange("p (h a d) -> p h a d", h=4, a=2)
            TVm = Tile("Vm")
            ghi = ar.alloc(S, BF16)
            glo = ar.alloc(S, BF16)
            Tgh = Tile("ghi")
            Tgl = Tile("glo")
            win = b_w_in[lb].rearrange("(k p) n -> p k n", p=128)
            ar.push()
            hT = ar.alloc(KC * S, BF16)
            hT3 = hT.rearrange("p (k t) -> p k t", k=KC)
            ThT = [Tile("hT%d" % i) for i in range(NT)]
            wb = [ar.alloc(KC * 128, BF16).rearrange("p (k n) -> p k n", k=KC) for _ in range(2)]
            Twb = [Tile("wb%d" % i) for i in range(2)]
            hTm = ar.alloc(KC * 256, BF16).rearrange("p (k t) -> p k t", k=KC)
            ThTm = [Tile("hTm0"), Tile("hTm1")]
            kb.mem_kv(mem, Tmem, norm_g[l, 4:5, :], mem_w_kv[l], hTm, ThTm, kTm, TkTm, Vm, TVm, wb, Twb)
            if state["first"]:
                kb.normT(lambda tt: x_in[tt * 128:(tt + 1) * 128, :], [Tin] * NT, norm_g[l, 0:1, :], hT3, ThT)
            else:
                kb.normT(lambda tt: out[tt * 128:(tt + 1) * 128, :], Tout, norm_g[l, 0:1, :], hT3, ThT)
            qtmp = [ar.alloc(S, BF16) for _ in range(2)]
            Tqtmp = [Tile("qtmp%d" % i) for i in range(2)]
            for h in range(16):
                b = h % 2
                col = h * 128 if h < 12 else 1572 + (h - 12) * 128
                kb.wload(win[:, :, col:col + 128], wb[b], Twb[b], k=KC, n=128)
                kb.proj_fm(wb[b], Twb[b], hT3, ThT, qtmp[b], Tqtmp[b], ISQ)
                kb.DMA(lambda e, b=b, h=h: e.dma_start(out=qT_d[h], in_=qtmp[b]), r=[Tqtmp[b]], w=[TqT[h]])
            wgt = ar.alloc(KC * 36, BF16).rearrange("p (k n) -> p k n", k=KC)
            Twgt = Tile("wgt")
            kb.wload(win[:, :, 1536:1572], wgt, Twgt, k=KC, n=36)
            gtok = [ar.alloc(36, F32) for _ in range(2)]
            Tgtok = [Tile("gtok%d" % i) for i in range(2)]
            gTf = ar.alloc(S, F32)
            TgTf = Tile("gTf")
            kb.V(lambda e: e.memset(ghi, 0.0), w=[Tgh])
            kb.V(lambda e: e.memset(glo, 0.0), w=[Tgl])
            for tt in range(NT):
                b = tt % 2
                bk = 6 + b
                bk2 = 4 + b
                for k in range(KC):
                    kb.T(lambda e, k=k, tt=tt, bk=bk: e.matmul(ps[:, bk, 0:36], lhsT=hT3[:, k, tt * 128:(tt + 1) * 128], rhs=wgt[:, k, :], start=(k == 0), stop=(k == KC - 1)),
                         r=[Twgt, ThT[tt]], w=[PB[bk]], silent=(k != KC - 1))
                kb.A(lambda e, b=b, bk=bk: e.activation(out=gtok[b], in_=ps[:, bk, 0:36], func=AF.Sigmoid), r=[PB[bk]], w=[Tgtok[b]])
                kb.T(lambda e, b=b, bk2=bk2: e.transpose(out=ps[0:36, bk2, 0:128], in_=gtok[b], identity=kb.identf), r=[Tgtok[b], kb.Tidf], w=[PB[bk2]])
                kb.V(lambda e, tt=tt, bk2=bk2: e.tensor_copy(out=gTf[0:36, tt * 128:(tt + 1) * 128], in_=ps[0:36, bk2, 0:128]), r=[PB[bk2]], w=[TgTf])
            kb.V(lambda e: e.tensor_copy(out=ghi[0:36, :], in_=gTf[0:36, :]), r=[TgTf], w=[Tgh])
            kb.V(lambda e: e.tensor_tensor(out=gTf[0:36, :], in0=gTf[0:36, :], in1=ghi[0:36, :], op=ALU.subtract), r=[TgTf, Tgh], w=[TgTf])
            kb.V(lambda e: e.tensor_copy(out=glo[0:36, :], in_=gTf[0:36, :]), r=[TgTf], w=[Tgl])
            ar.pop()
            P.barrier()
            ar.push()
            tabB = ar.alloc(2 * TABW, F32).rearrange("p (g c) -> p g c", g=2)
            TtabB = Tile("tabB")
            kb.DMA(lambda e: e.dma_start(out=tabB, in_=cin["c_tabB"]), w=[TtabB])
            cmpT = ar.alloc(S, F32)
            Tcmp = Tile("cmpT")
            kb.DMA(lambda e: e.dma_start(out=cmpT, in_=cin["c_cmp"]), w=[Tcmp])
            keep = ar.alloc(16 * 32, F32).rearrange("p (t j) -> p t j", t=16)
            force = ar.alloc(16 * 32, F32).rearrange("p (t j) -> p t j", t=16)
            Tkeep = Tile("keep")
            Tforce = Tile("force")
            kb.DMA(lambda e: e.dma_start(out=keep, in_=cin["c_keep"]), w=[Tkeep])
            kb.DMA(lambda e: e.dma_start(out=force, in_=cin["c_force"]), w=[Tforce])
            ovf = ar.alloc(32, F32)
            Tovf = Tile("ovf")
            ov_b = ar.alloc(32, BF16)
            Tov = Tile("ov")
            kb.DMA(lambda e: e.dma_start(out=ovf, in_=cin["c_ov"]), w=[Tovf])
            kb.V(lambda e: e.tensor_copy(out=ov_b, in_=ovf), r=[Tovf], w=[Tov])
            E_b = ar.alloc(16 * 128, BF16).rearrange("p (k s) -> p k s", k=16)
            TE = Tile("E")
            kb.wload(cin["c_E"], E_b, TE, k=16, n=128)
            oh_b = ar.alloc(36 * 128, BF16).rearrange("p (r m) -> p r m", r=36)
            Toh = Tile("oh")
            for q in range(3):
                kb.wload(cin["c_onehot"][:, q * 12:(q + 1) * 12, :], oh_b[:, q * 12:(q + 1) * 12, :], Toh, k=12, n=128)
            kcs = ar.alloc(4 * 128, BF16).rearrange("p (g n) -> p g n", g=4)
            vcs = ar.alloc(4 * 128, BF16).rearrange("p (g n) -> p g n", g=4)
            Tkcs = Tile("kcs")
            kb.DMA(lambda e: e.dma_start(out=kcs, in_=kc_d[0]), r=[Tkc], w=[Tkcs])
            kb.DMA(lambda e: e.dma_start(out=vcs, in_=kc_d[1]), r=[Tkc], w=[Tkcs])
            kb.attn_bufs()
            shg = [ar.alloc(4 * S, BF16).rearrange("p (s t) -> p s t", s=4) for _ in range(2)]
            Tshg = [Tile("shg%d" % i) for i in range(2)]
            qb = [ar.alloc(S, BF16) for _ in range(3)]
            Tqb = [Tile("qb%d" % i) for i in range(3)]
            ocmp = [ar.alloc(S, F32) for _ in range(3)]
            Toc = [Tile("ocmp%d" % i) for i in range(3)]
            impT = ar.alloc(S, F32)
            Timp = Tile("impT")
            penT = ar.alloc(S, BF16)
            Tpen = Tile("penT")
            kb.V(lambda e: e.memset(penT, 0.0), w=[Tpen])
            rg = ar.alloc(512, F32)
            Trg = Tile("rg")
            tmpo = ar.alloc(512, F32)
            Ttmpo = Tile("tmpo")
            v1 = ar.alloc(32, F32)
            v2 = ar.alloc(32, F32)
            mxa = ar.alloc(8, F32)
            mxb = ar.alloc(8, F32)
            selp = ar.alloc(32, F32)
            Tv1, Tv2, Tmxa, Tmxb, Tselp = Tile("v1"), Tile("v2"), Tile("mxa"), Tile("mxb"), Tile("selp")

            def finalize_gated(h, branch, tb, dst, Tdst, mode, dram=None, Tdram=None):
                kb.recip_den()
                recL = kb.rec
                r_ = h * 3 + branch
                kb.T(lambda e: e.matmul(ps[:, 6, :], lhsT=oh_b[:, r_, :], rhs=ghi[:, tb * 512:(tb + 1) * 512], start=True, stop=False), r=[Toh, Tgh], w=[PB[6]], silent=True)
                kb.T(lambda e: e.matmul(ps[:, 6, :], lhsT=oh_b[:, r_, :], rhs=glo[:, tb * 512:(tb + 1) * 512], start=False, stop=True), r=[Toh, Tgl], w=[PB[6]])
                kb.V(lambda e: e.tensor_tensor(out=rg, in0=ps[:, 6, :], in1=recL, op=ALU.mult), r=[PB[6], kb.Trec], w=[Trg])
                if mode == "set":
                    kb.V(lambda e: e.tensor_tensor(out=dst, in0=ps[:, 2, :], in1=rg, op=ALU.mult), r=[PB[2], Trg], w=[Tdst])
                elif mode == "add":
                    kb.V(lambda e: e.tensor_tensor(out=tmpo, in0=ps[:, 2, :], in1=rg, op=ALU.mult), r=[PB[2], Trg], w=[Ttmpo])
                    kb.G(lambda e: e.tensor_tensor(out=dst, in0=dst, in1=tmpo, op=ALU.add), r=[Ttmpo, Tdst], w=[Tdst])
                else:
                    kb.V(lambda e: e.tensor_tensor(out=tmpo, in0=ps[:, 2, :], in1=rg, op=ALU.mult), r=[PB[2], Trg], w=[Ttmpo])
                    ob = kb.osb_i % 2
                    kb.osb_i += 1
                    osbL = kb.osb[ob]
                    kb.G(lambda e: e.tensor_tensor(out=osbL, in0=dst, in1=tmpo, op=ALU.add), r=[Ttmpo, Tdst], w=[kb.Tosb[ob]])
                    kb.DMA(lambda e: e.dma_start(out=dram, in_=osbL.rearrange("p (t c) -> p t c", t=4)), r=[kb.Tosb[ob]], w=[Tdram])

            for g in range(4):
                sb_ = g % 2
                kb.DMA(lambda e, g=g, sb_=sb_: e.dma_start(out=shg[sb_], in_=sh_d[g].rearrange("s p t -> p s t")), r=[Tsh[g]], w=[Tshg[sb_]])
                ksT = shg[sb_][:, 0, :]
                vs = shg[sb_][:, 1, :].rearrange("p (t d) -> p t d", t=NT)
                kwT = shg[sb_][:, 2, :]
                vw = shg[sb_][:, 3, :].rearrange("p (t d) -> p t d", t=NT)
                for hh in range(3):
                    h = 3 * g + hh
                    slope = alibi(12, h)
                    kb.DMA(lambda e, hh=hh, h=h: e.dma_start(out=qb[hh], in_=qT_d[h]), r=[TqT[h]], w=[Tqb[hh]])
                    for tb in range(4):
                        units = [dict(kT=kcs[:, g, :], V=vcs[:, g, :], rd=[Tkcs], tab=cmpT[:, tb * 512:(tb + 1) * 512], tabrd=[Tcmp], slope=slope, bias=0.0, imp=(ov_b, Tov))]
                        run_units_chain(units, qb[hh][:, tb * 512:(tb + 1) * 512], Tqb[hh], True, True)
                        finalize_gated(h, 0, tb, ocmp[hh][:, tb * 512:(tb + 1) * 512], Toc[hh], "set")
                        recI = kb.rec
                        if hh == 0:
                            kb.V(lambda e, tb=tb: e.tensor_tensor(out=impT[0:32, tb * 512:(tb + 1) * 512], in0=ps[0:32, 7, :], in1=recI[0:32, :], op=ALU.mult), r=[PB[7], kb.Trec], w=[Timp])
                        else:
                            kb.V(lambda e: e.tensor_tensor(out=tmpo[0:32, :], in0=ps[0:32, 7, :], in1=recI[0:32, :], op=ALU.mult), r=[PB[7], kb.Trec], w=[Ttmpo])
                            kb.V(lambda e, tb=tb: e.tensor_tensor(out=impT[0:32, tb * 512:(tb + 1) * 512], in0=impT[0:32, tb * 512:(tb + 1) * 512], in1=tmpo[0:32, :], op=ALU.add), r=[Ttmpo, Timp], w=[Timp])
                for tt in range(NT):
                    kb.T(lambda e, tt=tt: e.transpose(out=ps[:, 6, 0:32], in_=impT[0:32, tt * 128:(tt + 1) * 128], identity=kb.identf[0:32, 0:32]), r=[Timp, kb.Tidf], w=[PB[6]])
                    kb.V(lambda e, tt=tt: e.tensor_tensor(out=v1, in0=ps[:, 6, 0:32], in1=keep[:, tt, :], op=ALU.mult), r=[PB[6], Tkeep], w=[Tv1])
                    kb.V(lambda e, tt=tt: e.tensor_tensor(out=v1, in0=v1, in1=force[:, tt, :], op=ALU.add), r=[Tv1, Tforce], w=[Tv1])
                    kb.V(lambda e: e.max(out=mxa, in_=v1), r=[Tv1], w=[Tmxa])
                    kb.V(lambda e: e.match_replace(out=v2, in_to_replace=mxa, in_values=v1, imm_value=-1.0e9), r=[Tv1, Tmxa], w=[Tv2])
                    kb.V(lambda e: e.max(out=mxb, in_=v2), r=[Tv2], w=[Tmxb])
                    kb.V(lambda e: e.tensor_scalar(out=selp, in0=v1, scalar1=mxb[:, 7:8], scalar2=None, op0=ALU.is_ge), r=[Tv1, Tmxb], w=[Tselp])
                    kb.V(lambda e: e.tensor_scalar(out=selp, in0=selp, scalar1=1.0, scalar2=30000.0, op0=ALU.subtract, op1=ALU.mult), r=[Tselp], w=[Tselp])
                    kb.T(lambda e: e.transpose(out=ps[0:32, 7, 0:128], in_=selp, identity=kb.identf), r=[Tselp, kb.Tidf], w=[PB[7]])
                    kb.V(lambda e, tt=tt: e.tensor_copy(out=penT[0:32, tt * 128:(tt + 1) * 128], in_=ps[0:32, 7, 0:128]), r=[PB[7]], w=[Tpen])
                for hh in range(3):
                    h = 3 * g + hh
                    slope = alibi(12, h)
                    for tb in range(4):
                        q_ap = qb[hh][:, tb * 512:(tb + 1) * 512]
                        units = []
                        for kt in range(0, 4 * tb + 4):
                            delta = 512 * tb - 128 * kt
                            de = min(delta, 128)
                            units.append(dict(kT=ksT[:, kt * 128:(kt + 1) * 128], V=vs[:, kt, :], rd=[Tshg[sb_]], tab=tabB[:, 0, de + 511:de + 511 + 512], tabrd=[TtabB],
                                              slope=slope, bias=-slope * (delta - de), pen=(E_b[:, kt, :], penT[:, tb * 512:(tb + 1) * 512], [TE, Tpen])))
                        run_units_chain(units, q_ap, Tqb[hh], True, True)
                        finalize_gated(h, 1, tb, ocmp[hh][:, tb * 512:(tb + 1) * 512], Toc[hh], "add")
                        units = []
                        for kt in range(max(0, 4 * tb - 4), 4 * tb + 4):
                            delta = 512 * tb - 128 * kt
                            units.append(dict(kT=kwT[:, kt * 128:(kt + 1) * 128], V=vw[:, kt, :], rd=[Tshg[sb_]], tab=tabB[:, 1, delta + 511:delta + 511 + 512], tabrd=[TtabB],
                                              slope=slope, bias=0.0))
                        run_units_chain(units, q_ap, Tqb[hh], True, True)
                        finalize_gated(h, 2, tb, ocmp[hh][:, tb * 512:(tb + 1) * 512], Toc[hh], "final", dram=oT_d[tb * 4:(tb + 1) * 4, :, h, :].rearrange("t p c -> p t c"), Tdram=ToT[h])
            for h in range(4):
                kb.DMA(lambda e, h=h: e.dma_start(out=qb[0], in_=qT_d[12 + h]), r=[TqT[12 + h]], w=[Tqb[0]])
                for tb in range(4):
                    units = [dict(kT=kTm[:, h, kt * 128:(kt + 1) * 128], V=Vm[:, h, kt, :], rd=[TkTm, TVm], tab=None) for kt in range(2)]
                    run_units_chain(units, qb[0][:, tb * 512:(tb + 1) * 512], Tqb[0], True, True)
                    kb.finalize_plain(oT_d[tb * 4:(tb + 1) * 4, :, 12 + h, :].rearrange("t p c -> p t c"), ToT[12 + h])
            ar.pop()
            ar.pop()
            w_out_phase(b_w_out[lb], 16, norm_g[l, 1:2, :])

        if seq is not None:
            for ph in seq:
                if ph == "a0":
                    layer_a(0)
                elif ph == "skv":
                    shared_kv_phase()
                elif ph == "b2":
                    layer_b(2)
                elif ph == "f0":
                    ffn_phase(0)
                if dbg and ph == "a0":
                    P.barrier()
                    Td = Tile("dbg")
                    for q in range(4):
                        kb.DMA(lambda e, q=q: e.dma_start(out=dbg_t["xm0"][q * 512:(q + 1) * 512, :], in_=out[q * 512:(q + 1) * 512, :]), r=Tout, w=[Td])
                    P.barrier()
        for l in range(l_start, n_layers if seq is None else 0):
            if l < 2:
                for _r in range(rep):
                    layer_a(l)
            else:
                if l == 2 or l == l_start:
                    shared_kv_phase()
                layer_b(l)
            def snap(name):
                if not dbg:
                    return
                P.barrier()
                Td = Tile("dbg")
                for q in range(4):
                    kb.DMA(lambda e, q=q: e.dma_start(out=dbg_t[name][q * 512:(q + 1) * 512, :], in_=out[q * 512:(q + 1) * 512, :]), r=Tout, w=[Td])
                P.barrier()
            snap("xm%d" % l)
            if stop_after_mixer and l == n_layers - 1:
                break
            ffn_phase(l)
            snap("x%d" % l)

        if dummy:
            P.barrier()
            dz = ar.alloc(64, F32)
            Tdz = Tile("dz")
            dz8 = ar.alloc(8, F32)
            kb.V(lambda e: e.memset(dz, 0.5), w=[Tdz])
            if "sigmoid" in dummy:
                kb.A(lambda e: e.activation(out=dz, in_=dz, func=AF.Sigmoid, scale=1.5), r=[Tdz], w=[Tdz])
            if "max" in dummy:
                kb.V(lambda e: e.max(out=dz8, in_=dz[:, 0:32]), r=[Tdz], w=[Tdz])
                kb.V(lambda e: e.match_replace(out=dz[:, 32:64], in_to_replace=dz8, in_values=dz[:, 0:32], imm_value=-1.0e9), r=[Tdz], w=[Tdz])
            if "isge" in dummy:
                kb.V(lambda e: e.tensor_scalar(out=dz[:, 0:32], in0=dz[:, 0:32], scalar1=dz8[:, 7:8], scalar2=None, op0=ALU.is_ge), r=[Tdz], w=[Tdz])
        P.final_wait("sync")
        P.emit(es)
        print("arena peak bytes/partition:", ar.peak * 2, "ops:", {e: len(P.ops[e]) for e in ENGS}, "semcounts:", P.count, max(P.dma_cnt))
    return nc


CONSTS = None


def kernel(**inputs):
    global CONSTS
    if CONSTS is None:
        CONSTS = make_consts()
    nc = build()
    x = np.ascontiguousarray(inputs["x"], dtype=np.float32)
    shared = {k: np.ascontiguousarray(v, dtype=np.float32) for k, v in inputs.items() if k not in ("x", "mem")}
    shared["kv_norm_g"] = shared["kv_norm_g"].reshape(1, D)
    in_maps = []
    for c in range(8):
        b = c % 4
        m = dict(shared)
        m.update(CONSTS)
        m["x"] = x[b]
        m["mem"] = np.ascontiguousarray(inputs["mem"][b], dtype=np.float32)
        in_maps.append(m)
    res = run_bass_kernel_spmd(nc, in_maps, core_ids=list(range(8)))
    return np.stack([res.results[b]["out"] for b in range(4)], 0).astype(np.float32)
```

```python
import math
from contextlib import ExitStack

import numpy as np
import concourse.bass as bass
import concourse.mybir as mybir
from concourse.bass_utils import run_bass_kernel_spmd

F32 = mybir.dt.float32
BF16 = mybir.dt.bfloat16
AF = mybir.ActivationFunctionType
ALU = mybir.AluOpType

ENGS = ("tensor", "vector", "scalar", "gpsimd", "sync")
STQ = "gpsimd"

D = 2048
S = 2048
NT = 16
KC = 16
DFF = 5632
NFC = 44
ISQ = 1.0 / math.sqrt(128.0)
NEG = -1.0e6
TABW = 1536
ARENA_N = 102 * 1024


class Tile:
    __slots__ = ("name", "w", "r")

    def __init__(self, name=""):
        self.name = name
        self.w = None
        self.r = []


class Prog:
    def __init__(self, nc, n_dma_sems=32):
        self.nc = nc
        self.allow_silent = True
        self.pe_selfwait = False
        self.ops = {e: [] for e in ENGS}
        self.count = {e: 0 for e in ENGS}
        self.waited = {e: {} for e in ENGS}
        self.n_dma_sems = n_dma_sems
        self.dma_cnt = [0] * n_dma_sems
        self.dma_rr = 0
        self.sems = {}

    def _deps(self, eng, reads, writes):
        need = {}

        def add(tok):
            if tok is None:
                return
            k, v = tok
            if need.get(k, 0) < v:
                need[k] = v

        for t in reads:
            add(t.w)
        for t in writes:
            add(t.w)
            for tok in t.r:
                add(tok)
        waits = []
        wd = self.waited[eng]
        for k, v in need.items():
            if wd.get(k, 0) >= v:
                continue
            if k == eng and v > self.count[eng]:
                continue
            if k == eng and eng == "tensor" and not self.pe_selfwait:
                continue
            wd[k] = v
            waits.append((k, v))
        return waits

    def _mark(self, tok, reads, writes):
        for t in reads:
            t.r.append(tok)
            if len(t.r) > 48:
                m = {}
                for k, v in t.r:
                    if m.get(k, 0) < v:
                        m[k] = v
                t.r = list(m.items())
        for t in writes:
            t.w = tok
            t.r = []

    def op(self, eng, fn, reads=(), writes=(), silent=False):
        waits = self._deps(eng, reads, writes)
        if silent and self.allow_silent:
            tok = (eng, self.count[eng] + 1)
            self.ops[eng].append((waits, fn, None))
        else:
            self.count[eng] += 1
            tok = (eng, self.count[eng])
            self.ops[eng].append((waits, fn, (eng, 1)))
        self._mark(tok, reads, writes)
        return tok

    def dma(self, eng, fn, reads=(), writes=()):
        s = self.dma_rr
        self.dma_rr = (self.dma_rr + 1) % self.n_dma_sems
        key = ("dma", s)
        waits = self._deps(eng, reads, writes)
        prev = self.dma_cnt[s]
        if prev > 0 and self.waited[eng].get(key, 0) < prev:
            self.waited[eng][key] = prev
            waits.append((key, prev))
        self.dma_cnt[s] += 16
        tok = (key, self.dma_cnt[s])
        self.ops[eng].append((waits, fn, (key, 16)))
        self._mark(tok, reads, writes)
        return tok

    def barrier(self):
        toks = [(e, self.count[e]) for e in ENGS if self.count[e] > 0]
        toks += [(("dma", s), v) for s, v in enumerate(self.dma_cnt) if v > 0]
        for e in ENGS:
            waits = []
            for k, v in toks:
                if k == e:
                    continue
                if self.waited[e].get(k, 0) < v:
                    self.waited[e][k] = v
                    waits.append((k, v))
            if waits:
                self.ops[e].append((waits, None, None))

    def final_wait(self, eng="sync"):
        toks = [(e, self.count[e]) for e in ENGS if self.count[e] > 0 and e != eng]
        toks += [(("dma", s), v) for s, v in enumerate(self.dma_cnt) if v > 0]
        waits = []
        for k, v in toks:
            if self.waited[eng].get(k, 0) < v:
                self.waited[eng][k] = v
                waits.append((k, v))
        self.ops[eng].append((waits, None, None))

    def emit(self, es):
        nc = self.nc
        for e in ENGS:
            self.sems[e] = es.enter_context(nc.semaphore("s_" + e))
        for s in range(self.n_dma_sems):
            self.sems[("dma", s)] = es.enter_context(nc.semaphore("s_dma%d" % s))
        block = es.enter_context(nc.Block())
        sems = self.sems

        def run(engname):
            def body(e):
                for waits, fn, inc in self.ops[engname]:
                    for k, v in waits:
                        e.wait_ge(sems[k], v)
                    if fn is None:
                        continue
                    ins = fn(e)
                    if inc is not None:
                        ins.then_inc(sems[inc[0]], inc[1])
            return body

        block.sync(run("sync"))
        block.tensor(run("tensor"))
        block.vector(run("vector"))
        block.scalar(run("scalar"))
        block.gpsimd(run("gpsimd"))


class Arena:
    def __init__(self, base_ap_bf16, nelem_bf16):
        self.base = base_ap_bf16
        self.n = nelem_bf16
        self.top = 0
        self.stack = []
        self.peak = 0

    def push(self):
        self.stack.append(self.top)

    def pop(self):
        self.top = self.stack.pop()

    def alloc(self, nelem, dtype):
        w = 2 if dtype == F32 else 1
        n16 = (nelem * w + 31) // 32 * 32
        if self.top + n16 > self.n:
            raise MemoryError("SBUF arena overflow: need %d have %d" % (self.top + n16, self.n))
        ap = self.base[:, self.top:self.top + nelem * w]
        self.top += n16
        self.peak = max(self.peak, self.top)
        if dtype != BF16:
            ap = ap.bitcast(dtype)
        return ap


def _toeplitz(fn):
    j = np.arange(128)[:, None]
    c = np.arange(TABW)[None, :]
    dist = c - 511 - j
    valid = fn(dist)
    return np.where(valid, -dist.astype(np.float32), np.float32(NEG)).astype(np.float32)


def make_consts():
    c = {}
    c["c_ident"] = np.eye(128, dtype=np.float32)
    tabA = np.stack([
        _toeplitz(lambda d: (d >= 0) & (d <= 128)),
        _toeplitz(lambda d: (d >= 0) & (d <= 512) & (d % 4 == 0)),
        _toeplitz(lambda d: (d >= 0) & (d % 16 == 0)),
    ], 0)
    c["c_tabA"] = np.ascontiguousarray(tabA.transpose(1, 0, 2))
    tabB = np.stack([
        _toeplitz(lambda d: (d >= 0)),
        _toeplitz(lambda d: (d >= 0) & (d <= 511)),
    ], 0)
    c["c_tabB"] = np.ascontiguousarray(tabB.transpose(1, 0, 2))
    n = np.arange(128)[:, None]
    t = np.arange(S)[None, :]
    cend = 16 * n + 31
    dc = t - cend
    cm = np.where((dc >= 0) & (n < 127), -dc.astype(np.float32), np.float32(NEG)).astype(np.float32)
    c["c_cmp"] = np.ascontiguousarray(cm)
    cs = np.arange(127) * 16
    ss = np.arange(32) * 64
    ov = np.clip(np.minimum(cs[:, None] + 32, ss[None, :] + 64) - np.maximum(cs[:, None], ss[None, :]), 0, None) / 32.0
    ovp = np.zeros((128, 32), np.float32)
    ovp[:127] = ov
    c["c_ov"] = ovp
    E = np.zeros((128, 16, 128), np.float32)
    for kt in range(16):
        for s_ in range(128):
            E[2 * kt + s_ // 64, kt, s_] = 1.0
    c["c_E"] = E
    keep = np.zeros((128, 16, 32), np.float32)
    force = np.zeros((128, 16, 32), np.float32)
    for tt in range(16):
        for p in range(128):
            cur = (tt * 128 + p) // 64
            for jb in range(32):
                if jb == 0 or jb == cur or jb == cur - 1:
                    force[p, tt, jb] = 1.0e4 + (32 - jb)
                elif jb > cur:
                    force[p, tt, jb] = -1.0e4 - jb
                else:
                    keep[p, tt, jb] = 1.0
    c["c_keep"] = keep
    c["c_force"] = force
    oh = np.zeros((128, 36, 128), np.float32)
    for r in range(36):
        oh[r, r, :] = 1.0
    c["c_onehot"] = oh
    return c


class KB:
    def __init__(self, nc, es):
        self.nc = nc
        self.P = Prog(nc)
        arena_t = es.enter_context(nc.sbuf_tensor("arena", [128, ARENA_N], BF16))
        self.ar = Arena(arena_t[:, :], ARENA_N)
        self.ps = es.enter_context(nc.psum_tensor("ps", [128, 8, 512], F32))
        self.PB = [Tile("pb%d" % i) for i in range(8)]

    def V(self, fn, r=(), w=()):
        return self.P.op("vector", fn, r, w)

    def A(self, fn, r=(), w=()):
        return self.P.op("scalar", fn, r, w)

    def T(self, fn, r=(), w=(), silent=False):
        return self.P.op("tensor", fn, r, w, silent=silent)

    def G(self, fn, r=(), w=()):
        return self.P.op("gpsimd", fn, r, w)

    def DMA(self, fn, r=(), w=(), q="sync"):
        return self.P.dma(q, fn, r, w)

    def psb(self, b):
        return self.ps[:, b, :]

    def setup_consts(self, cin):
        ar = self.ar
        self.identf = ar.alloc(128, F32)
        self.Tidf = Tile("identf")
        self.identb = ar.alloc(128, BF16)
        self.Tidb = Tile("identb")
        self.onesb = ar.alloc(128, BF16)
        self.Tones = Tile("ones")
        self.DMA(lambda e: e.dma_start(out=self.identf, in_=cin["c_ident"][:, :]), w=[self.Tidf])
        self.V(lambda e: e.tensor_copy(out=self.identb, in_=self.identf), r=[self.Tidf], w=[self.Tidb])
        self.V(lambda e: e.memset(self.onesb, 1.0), w=[self.Tones])
        self.stage = [ar.alloc(2048, F32) for _ in range(3)]
        self.Tstage = [Tile("stage%d" % i) for i in range(3)]
        self.stage_i = 0
        self.cast_i = 0

    def wload(self, dram_ap, dst_ap, Tdst, k=None, n=None, cast="alt"):
        s = self.stage_i % 3
        self.stage_i += 1
        st = self.stage[s]
        if k is not None:
            st = st[:, 0:k * n].rearrange("p (k n) -> p k n", k=k)
        elif n is not None:
            st = st[:, 0:n]
        Ts = self.Tstage[s]
        self.DMA(lambda e: e.dma_start(out=st, in_=dram_ap), w=[Ts])
        if cast == "alt":
            cast = "gpsimd" if (self.cast_i % 2 == 0) else "vector"
            self.cast_i += 1
        self.P.op(cast, lambda e: e.tensor_copy(out=dst_ap, in_=st), [Ts], [Tdst])

    def rstd_of(self, src_ap, Tsrc, junk, Tjunk, ss, Tss):
        self.A(lambda e: e.activation(out=junk, in_=src_ap, func=AF.Square, accum_out=ss), r=[Tsrc], w=[Tjunk, Tss])
        self.A(lambda e: e.activation(out=ss, in_=ss, func=AF.Sqrt, scale=1.0 / D, bias=1e-6), r=[Tss], w=[Tss])
        self.V(lambda e: e.reciprocal(out=ss, in_=ss), r=[Tss], w=[Tss])

    def normT(self, src_fn, src_tiles, g_row_ap, hT3, ThT, ntt=NT):
        ar = self.ar
        ar.push()
        gt = ar.alloc(D, F32)
        Tg = Tile("g")
        self.DMA(lambda e: e.dma_start(out=gt, in_=g_row_ap.partition_broadcast(128)), w=[Tg])
        NB = 4 if ntt > 2 else 2
        xt = [ar.alloc(D, F32) for _ in range(NB)]
        Tx = [Tile("xt%d" % i) for i in range(NB)]
        hb = [ar.alloc(D, BF16) for _ in range(NB)]
        Th = [Tile("hb%d" % i) for i in range(NB)]
        junk = ar.alloc(D, BF16)
        Tj = Tile("junk")
        ss = [ar.alloc(1, F32) for _ in range(NB)]
        Tss = [Tile("ss%d" % i) for i in range(NB)]
        for tt in range(ntt):
            b = tt % NB
            src = src_fn(tt)
            self.DMA(lambda e, b=b, src=src: e.dma_start(out=xt[b], in_=src), r=[src_tiles[tt]], w=[Tx[b]])
            self.rstd_of(xt[b], Tx[b], junk, Tj, ss[b], Tss[b])
            self.V(lambda e, b=b: e.scalar_tensor_tensor(out=hb[b], in0=xt[b], scalar=ss[b], in1=gt, op0=ALU.mult, op1=ALU.mult),
                   r=[Tx[b], Tss[b], Tg], w=[Th[b]])
            pv = self.ps[:, 2 * b:2 * b + 2, :].bitcast(BF16).rearrange("p a b -> p (a b)")
            for k in range(KC):
                self.T(lambda e, b=b, k=k, pv=pv: e.transpose(out=pv[:, k * 128:(k + 1) * 128], in_=hb[b][:, k * 128:(k + 1) * 128], identity=self.identb),
                       r=[Th[b], self.Tidb], w=[self.PB[2 * b], self.PB[2 * b + 1]], silent=(k != KC - 1))
            self.A(lambda e, tt=tt, pv=pv: e.copy(out=hT3[:, :, tt * 128:(tt + 1) * 128], in_=pv.rearrange("p (k t) -> p k t", k=KC)),
                   r=[self.PB[2 * b], self.PB[2 * b + 1]], w=[ThT[tt]])
        ar.pop()
        self.P.barrier()

    def proj_fm(self, w3, Tw, hT3, ThT, out_ap, Tout, scale, ntok=S, banks=(4, 5)):
        nblk = (ntok + 511) // 512
        for tb in range(nblk):
            n = min(512, ntok - tb * 512)
            bk = banks[tb % 2]
            tts = list(range(tb * 4, tb * 4 + (n + 127) // 128))
            for k in range(KC):
                self.T(lambda e, k=k, tb=tb, n=n, bk=bk: e.matmul(self.ps[:, bk, 0:n], lhsT=w3[:, k, :], rhs=hT3[:, k, tb * 512:tb * 512 + n], start=(k == 0), stop=(k == KC - 1)),
                       r=[Tw] + [ThT[t] for t in tts], w=[self.PB[bk]], silent=(k != KC - 1))
            self.A(lambda e, tb=tb, n=n, bk=bk: e.mul(out_ap[:, tb * 512:tb * 512 + n], self.ps[:, bk, 0:n], scale), r=[self.PB[bk]], w=[Tout])

    def proj_tm(self, w3, Tw, hT3, ThT, out3, Tout, ntt=NT, banks=(6, 7), ncol=128):
        ngrp = (ntt + 3) // 4
        for g4 in range(ngrp):
            bk = banks[g4 % 2]
            cnt = min(4, ntt - g4 * 4)
            for i in range(cnt):
                tt = g4 * 4 + i
                for k in range(KC):
                    self.T(lambda e, k=k, tt=tt, i=i, bk=bk: e.matmul(self.ps[:, bk, i * ncol:(i + 1) * ncol], lhsT=hT3[:, k, tt * 128:(tt + 1) * 128], rhs=w3[:, k, 0:ncol], start=(k == 0), stop=(k == KC - 1)),
                           r=[Tw, ThT[tt]], w=[self.PB[bk]], silent=(k != KC - 1))
            self.V(lambda e, g4=g4, cnt=cnt, bk=bk: e.tensor_copy(out=out3[:, g4 * 4:g4 * 4 + cnt, :], in_=self.ps[:, bk, 0:cnt * ncol].rearrange("p (a d) -> p a d", a=cnt)),
                   r=[self.PB[bk]], w=[Tout])

    def attn_bufs(self):
        ar = self.ar
        self.sb = [ar.alloc(512, F32) for _ in range(4)]
        self.Tsb = [Tile("sb%d" % i) for i in range(4)]
        self.pt = [ar.alloc(512, BF16) for _ in range(4)]
        self.Tpt = [Tile("pt%d" % i) for i in range(4)]
        self.rec = ar.alloc(512, F32)
        self.Trec = Tile("rec")
        self.osb = [ar.alloc(512, BF16) for _ in range(2)]
        self.Tosb = [Tile("osb%d" % i) for i in range(2)]
        self.osb_i = 0

    def attn_units(self, units, q_ap, Tq, imp=None):
        nu = len(units)
        OB, DB = 2, 3

        def s_stage(i):
            u = units[i]
            b = i % 2
            pen = u.get("pen")
            self.T(lambda e, u=u, b=b: e.matmul(self.ps[:, b, :], lhsT=u["kT"], rhs=q_ap, start=True, stop=(u.get("pen") is None)),
                   r=list(u["rd"]) + [Tq], w=[self.PB[b]])
            if pen is not None:
                self.T(lambda e, pen=pen, b=b: e.matmul(self.ps[:, b, :], lhsT=pen[0], rhs=pen[1], start=False, stop=True),
                       r=list(pen[2]), w=[self.PB[b]])

        def mid(i):
            u = units[i]
            b = i % 2
            bias = float(u.get("bias", 0.0))
            if u.get("tab") is not None:
                self.V(lambda e, u=u, b=b: e.scalar_tensor_tensor(out=self.sb[b], in0=u["tab"], scalar=float(u["slope"]), in1=self.ps[:, b, :], op0=ALU.mult, op1=ALU.add),
                       r=[self.PB[b]] + list(u.get("tabrd", [])), w=[self.Tsb[b]])
                src, rd = self.sb[b], [self.Tsb[b]]
            else:
                src, rd = self.ps[:, b, :], [self.PB[b]]
            if bias != 0.0:
                self.A(lambda e, b=b, src=src, bias=bias: e.activation(out=self.pt[b], in_=src, func=AF.Exp, bias=bias), r=rd, w=[self.Tpt[b]])
            else:
                self.A(lambda e, b=b, src=src: e.activation(out=self.pt[b], in_=src, func=AF.Exp), r=rd, w=[self.Tpt[b]])

        def pv(i):
            u = units[i]
            b = i % 2
            first = (i == 0)
            last = (i == nu - 1)
            self.T(lambda e, u=u, b=b, first=first, last=last: e.matmul(self.ps[:, OB, :], lhsT=u["V"], rhs=self.pt[b], start=first, stop=last),
                   r=list(u["rd"]) + [self.Tpt[b]], w=[self.PB[OB]])
            self.T(lambda e, b=b, first=first, last=last: e.matmul(self.ps[:, DB, :], lhsT=self.onesb, rhs=self.pt[b], start=first, stop=last),
                   r=[self.Tones, self.Tpt[b]], w=[self.PB[DB]])
            if imp is not None:
                self.T(lambda e, b=b, first=first, last=last: e.matmul(self.ps[0:32, 7, :], lhsT=imp[0], rhs=self.pt[b], start=first, stop=last),
                       r=[imp[1], self.Tpt[b]], w=[self.PB[7]])

        s_stage(0)
        for i in range(nu):
            if i + 1 < nu:
                s_stage(i + 1)
            mid(i)
            pv(i)

    def recip_den(self):
        rec = self.rec
        self.V(lambda e: e.tensor_scalar(out=rec, in0=self.ps[:, 3, :], scalar1=1e-30, scalar2=None, op0=ALU.max), r=[self.PB[3]], w=[self.Trec])
        self.V(lambda e: e.reciprocal(out=rec, in_=rec), r=[self.Trec], w=[self.Trec])

    def finalize_plain(self, dst_dram, Tdst):
        self.recip_den()
        ob = self.osb_i % 2
        self.osb_i += 1
        rec = self.rec
        osb = self.osb[ob]
        self.V(lambda e: e.tensor_tensor(out=osb, in0=self.ps[:, 2, :], in1=rec, op=ALU.mult), r=[self.PB[2], self.Trec], w=[self.Tosb[ob]])
        self.DMA(lambda e: e.dma_start(out=dst_dram, in_=osb.rearrange("p (t c) -> p t c", t=4)), r=[self.Tosb[ob]], w=[Tdst], q=STQ)

    def mem_kv(self, mem_ap, Tmem, g_row, wkv, hT3mem, ThTm, kTm, TkTm, Vm, TVm, wbuf, Twbuf):
        self.normT(lambda tt: mem_ap[tt * 128:(tt + 1) * 128, :], [Tmem, Tmem], g_row, hT3mem, ThTm, ntt=2)
        wcols = wkv.rearrange("(k p) n -> p k n", p=128)
        for h in range(4):
            b = h % 2
            self.wload(wcols[:, :, h * 128:(h + 1) * 128], wbuf[b], Twbuf[b], k=KC, n=128)
            self.proj_fm(wbuf[b], Twbuf[b], hT3mem, ThTm, kTm[:, h, :], TkTm, 1.0, ntok=256)
        for h in range(4):
            b = h % 2
            self.wload(wcols[:, :, 512 + h * 128:512 + (h + 1) * 128], wbuf[b], Twbuf[b], k=KC, n=128)
            self.proj_tm(wbuf[b], Twbuf[b], hT3mem, ThTm, Vm[:, h, :, :], TVm, ntt=2)

    def resid_update(self, y_ap, Ty, x_src_ap, Tsrc, x_dst_ap, Tdst, gt, Tg, bufs):
        xt, Tx, tmp, Ttmp, junk, Tj, ss, Tss = bufs
        self.DMA(lambda e: e.dma_start(out=xt, in_=x_src_ap), r=[Tsrc], w=[Tx])
        self.rstd_of(y_ap, Ty, junk, Tj, ss, Tss)
        self.V(lambda e: e.scalar_tensor_tensor(out=tmp, in0=y_ap, scalar=ss, in1=gt, op0=ALU.mult, op1=ALU.mult), r=[Ty, Tss, Tg], w=[Ttmp])
        self.V(lambda e: e.tensor_tensor(out=xt, in0=xt, in1=tmp, op=ALU.add), r=[Ttmp, Tx], w=[Tx])
        self.DMA(lambda e: e.dma_start(out=x_dst_ap, in_=xt), r=[Tx], w=[Tdst], q=STQ)


def alibi(n, i):
    return float(2.0 ** (-8.0 * (i + 1) / n))


def build(n_layers=4, stop_after_mixer=False, l_start=0, dbg=False, decl=None, allow_silent=True, pe_selfwait=False, rep=1, dummy=(), seq=None):
    nc = bass.Bass("TRN2", target_bir_lowering=False)

    def din(name, shape):
        if decl is not None and name not in decl:
            return nc.dram_tensor(name, [1] * len(shape), F32).ap()
        return nc.dram_tensor(name, list(shape), F32, kind="ExternalInput").ap()

    x_in = din("x", [S, D])
    mem = din("mem", [256, D])
    norm_g = din("norm_g", [4, 5, D])
    a_w_in = din("a_w_in", [2, D, 9728])
    a_w_out = din("a_w_out", [2, 1536, D])
    b_w_in = din("b_w_in", [2, D, 2084])
    b_w_out = din("b_w_out", [2, 2048, D])
    mem_w_kv = din("mem_w_kv", [4, D, 1024])
    ffn_w_gu = din("ffn_w_gu", [4, D, 2 * DFF])
    ffn_w_down = din("ffn_w_down", [4, DFF, D])
    kv_norm_g = din("kv_norm_g", [1, D])
    kv_w = din("kv_w", [D, 3072])
    cmp_pe = din("cmp_pe", [2, 32, 128])
    cmp_wk1 = din("cmp_wk1", [4096, 512])
    cmp_wk2 = din("cmp_wk2", [512, 128])
    cmp_wv1 = din("cmp_wv1", [4096, 512])
    cmp_wv2 = din("cmp_wv2", [512, 128])
    cshapes = {"c_ident": [128, 128], "c_tabA": [128, 3, TABW], "c_tabB": [128, 2, TABW], "c_cmp": [128, S],
               "c_ov": [128, 32], "c_E": [128, 16, 128], "c_keep": [128, 16, 32], "c_force": [128, 16, 32],
               "c_onehot": [128, 36, 128]}
    cin = {k: din(k, v) for k, v in cshapes.items()}
    out = nc.dram_tensor("out", [S, D], F32, kind="ExternalOutput").ap()
    oT_d = nc.dram_tensor("oT_d", [NT, 128, 16, 128], BF16).ap()
    qT_d = nc.dram_tensor("qT_d", [16, 128, S], BF16).ap()
    actT_d = nc.dram_tensor("actT_d", [NT, 128, NFC, 128], BF16).ap()
    y_d = nc.dram_tensor("y_d", [S, D], F32).ap()
    sh_d = nc.dram_tensor("sh_d", [4, 4, 128, S], BF16).ap()
    kc_d = nc.dram_tensor("kc_d", [2, 128, 4, 128], BF16).ap()

    dbg_t = {}
    if dbg:
        for l_ in range(4):
            for nm in ("xm", "x"):
                dbg_t[nm + str(l_)] = nc.dram_tensor("dbg_" + nm + str(l_), [S, D], F32, kind="ExternalOutput").ap()

    with ExitStack() as es:
        kb = KB(nc, es)
        P = kb.P
        P.allow_silent = allow_silent
        P.pe_selfwait = pe_selfwait
        ar = kb.ar
        ps = kb.ps
        PB = kb.PB
        kb.setup_consts(cin)

        Tmem = Tile("mem")
        Tin = Tile("x_in")
        Tout = [Tile("out%d" % i) for i in range(NT)]
        ToT = [Tile("oT%d" % i) for i in range(16)]
        TqT = [Tile("qT%d" % i) for i in range(16)]
        Tact = [Tile("act%d" % i) for i in range(NFC)]
        Ty = [Tile("y%d" % i) for i in range(NT)]
        Tsh = [Tile("sh%d" % i) for i in range(4)]
        Tkc = Tile("kc")

        state = {"first": True}

        def xsrc(tt):
            if state["first"]:
                return x_in[tt * 128:(tt + 1) * 128, :], Tin
            return out[tt * 128:(tt + 1) * 128, :], Tout[tt]

        def w_out_phase(w_out_l, nheads, g_row):
            P.barrier()
            ar.push()
            wo = ar.alloc(nheads * D, BF16).rearrange("p (h n) -> p h n", h=nheads)
            Two = [Tile("wo%d" % i) for i in range(nheads)]
            for h in range(nheads):
                kb.wload(w_out_l[h * 128:(h + 1) * 128, :], wo[:, h, :], Two[h], n=D)
            gt = ar.alloc(D, F32)
            Tg = Tile("g")
            kb.DMA(lambda e: e.dma_start(out=gt, in_=g_row.partition_broadcast(128)), w=[Tg])
            ot = [ar.alloc(nheads * 128, BF16).rearrange("p (h t) -> p h t", h=nheads) for _ in range(4)]
            Tot = [Tile("ot%d" % i) for i in range(4)]
            xts = [ar.alloc(D, F32) for _ in range(4)]
            Txs = [Tile("xt%d" % i) for i in range(4)]
            sss = [ar.alloc(1, F32) for _ in range(4)]
            Tsss = [Tile("ss%d" % i) for i in range(4)]
            tmp = ar.alloc(D, F32)
            Ttmp = Tile("tmp")
            junk = ar.alloc(D, BF16)
            Tj = Tile("junk")
            g4 = gt.rearrange("p (a b) -> p a b", a=4)
            t4 = tmp.rearrange("p (a b) -> p a b", a=4)
            j4 = junk.rearrange("p (a b) -> p a b", a=4)
            for tt in range(NT):
                b = tt % 4
                pb0 = 4 * (tt % 2)
                kb.DMA(lambda e, b=b, tt=tt: e.dma_start(out=ot[b], in_=oT_d[tt, :, 0:nheads, :]), r=ToT[:nheads], w=[Tot[b]])
                for c4 in range(4):
                    for h in range(nheads):
                        kb.T(lambda e, b=b, c4=c4, h=h, pb0=pb0: e.matmul(ps[:, pb0 + c4, :], lhsT=ot[b][:, h, :], rhs=wo[:, h, c4 * 512:(c4 + 1) * 512], start=(h == 0), stop=(h == nheads - 1)),
                             r=[Tot[b], Two[h]], w=[PB[pb0 + c4]], silent=(h != nheads - 1))
                src, Tsrc = xsrc(tt)
                xt, Tx, ss, Tss = xts[b], Txs[b], sss[b], Tsss[b]
                y_ap = ps[:, pb0:pb0 + 4, :]
                pbs = PB[pb0:pb0 + 4]
                kb.DMA(lambda e, xt=xt, src=src: e.dma_start(out=xt, in_=src), r=[Tsrc], w=[Tx])
                kb.A(lambda e, ss=ss, y_ap=y_ap: e.activation(out=j4, in_=y_ap, func=AF.Square, accum_out=ss), r=pbs, w=[Tj, Tss])
                kb.A(lambda e, ss=ss: e.activation(out=ss, in_=ss, func=AF.Sqrt, scale=1.0 / D, bias=1e-6), r=[Tss], w=[Tss])
                kb.V(lambda e, ss=ss: e.reciprocal(out=ss, in_=ss), r=[Tss], w=[Tss])
                kb.V(lambda e, ss=ss, y_ap=y_ap: e.scalar_tensor_tensor(out=t4, in0=y_ap, scalar=ss, in1=g4, op0=ALU.mult, op1=ALU.mult),
                     r=pbs + [Tss, Tg], w=[Ttmp])
                kb.V(lambda e, xt=xt: e.tensor_tensor(out=xt, in0=xt, in1=tmp, op=ALU.add), r=[Ttmp, Tx], w=[Tx])
                kb.DMA(lambda e, xt=xt, tt=tt: e.dma_start(out=out[tt * 128:(tt + 1) * 128, :], in_=xt), r=[Tx], w=[Tout[tt]], q=STQ)
            ar.pop()
            P.barrier()
            state["first"] = False

        def ffn_phase(l):
            P.barrier()
            ar.push()
            wd = [ar.alloc(NFC * 512, BF16).rearrange("p (f n) -> p f n", f=NFC), None]
            Twd = [[Tile("wd%d_%d" % (i, q)) for q in range(11)] for i in range(2)]
            wdv = ffn_w_down[l].rearrange("(f p) n -> p f n", p=128)

            def load_wd_unit(c4, q):
                b = c4 % 2
                kb.wload(wdv[:, q * 4:(q + 1) * 4, c4 * 512:(c4 + 1) * 512], wd[b][:, q * 4:(q + 1) * 4, :], Twd[b][q], k=4, n=512)

            ar.push()
            hT = ar.alloc(KC * S, BF16)
            hT3 = hT.rearrange("p (k t) -> p k t", k=KC)
            ThT = [Tile("hT%d" % i) for i in range(NT)]
            kb.normT(lambda tt: out[tt * 128:(tt + 1) * 128, :], Tout, norm_g[l, 2:3, :], hT3, ThT)
            wg = [ar.alloc(KC * 128, BF16).rearrange("p (k n) -> p k n", k=KC) for _ in range(2)]
            wu = [ar.alloc(KC * 128, BF16).rearrange("p (k n) -> p k n", k=KC) for _ in range(2)]
            Twg = [Tile("wg%d" % i) for i in range(2)]
            Twu = [Tile("wu%d" % i) for i in range(2)]
            sg = [ar.alloc(512, F32) for _ in range(2)]
            Tsg = [Tile("sg%d" % i) for i in range(2)]
            ao = [ar.alloc(512, BF16) for _ in range(2)]
            Tao = [Tile("ao%d" % i) for i in range(2)]
            wgu = ffn_w_gu[l].rearrange("(k p) n -> p k n", p=128)

            def load_fc(fc):
                b = fc % 2
                kb.wload(wgu[:, :, fc * 128:(fc + 1) * 128], wg[b], Twg[b], k=KC, n=128)
                kb.wload(wgu[:, :, DFF + fc * 128:DFF + (fc + 1) * 128], wu[b], Twu[b], k=KC, n=128)

            load_fc(0)
            it = 0
            for fc in range(NFC):
                if fc + 1 < NFC:
                    load_fc(fc + 1)
                if 30 <= fc < 41:
                    load_wd_unit(0, fc - 30)
                b = fc % 2
                for tb in range(4):
                    gb, ub = 4 + 2 * (it % 2), 5 + 2 * (it % 2)
                    i2 = it % 2
                    it += 1
                    tts = [ThT[t] for t in range(tb * 4, tb * 4 + 4)]
                    for k in range(KC):
                        kb.T(lambda e, k=k, b=b, tb=tb, gb=gb: e.matmul(ps[:, gb, :], lhsT=wg[b][:, k, :], rhs=hT3[:, k, tb * 512:(tb + 1) * 512], start=(k == 0), stop=(k == KC - 1)),
                             r=[Twg[b]] + tts, w=[PB[gb]], silent=(k != KC - 1))
                    for k in range(KC):
                        kb.T(lambda e, k=k, b=b, tb=tb, ub=ub: e.matmul(ps[:, ub, :], lhsT=wu[b][:, k, :], rhs=hT3[:, k, tb * 512:(tb + 1) * 512], start=(k == 0), stop=(k == KC - 1)),
                             r=[Twu[b]] + tts, w=[PB[ub]], silent=(k != KC - 1))
                    kb.A(lambda e, i2=i2, gb=gb: e.activation(out=sg[i2], in_=ps[:, gb, :], func=AF.Silu), r=[PB[gb]], w=[Tsg[i2]])
                    kb.V(lambda e, i2=i2, ub=ub: e.tensor_tensor(out=ao[i2], in0=ps[:, ub, :], in1=sg[i2], op=ALU.mult), r=[PB[ub], Tsg[i2]], w=[Tao[i2]])
                    kb.DMA(lambda e, i2=i2, fc=fc, tb=tb: e.dma_start(out=actT_d[tb * 4:(tb + 1) * 4, :, fc, :].rearrange("t p c -> p t c"), in_=ao[i2].rearrange("p (t c) -> p t c", t=4)),
                           r=[Tao[i2]], w=[Tact[fc]], q=STQ)
            ar.pop()
            P.barrier()
            wd[1] = ar.alloc(NFC * 512, BF16).rearrange("p (f n) -> p f n", f=NFC)
            at = [ar.alloc(NFC * 128, BF16).rearrange("p (f t) -> p f t", f=NFC) for _ in range(4)]
            Tat = [Tile("at%d" % i) for i in range(4)]
            yo = [ar.alloc(512, F32) for _ in range(2)]
            Tyo = [Tile("yo%d" % i) for i in range(2)]
            it = 0
            for c4 in range(4):
                b = c4 % 2
                for tt in range(NT):
                    if c4 + 1 < 4 and 1 <= tt < 12:
                        load_wd_unit(c4 + 1, tt - 1)
                    i2 = it % 2
                    i4 = it % 4
                    bk = 4 + (it % 2)
                    it += 1
                    kb.DMA(lambda e, i4=i4, tt=tt: e.dma_start(out=at[i4], in_=actT_d[tt]), r=Tact, w=[Tat[i4]])
                    for f in range(NFC):
                        kb.T(lambda e, f=f, i4=i4, b=b, bk=bk: e.matmul(ps[:, bk, :], lhsT=at[i4][:, f, :], rhs=wd[b][:, f, :], start=(f == 0), stop=(f == NFC - 1)),
                             r=[Tat[i4], Twd[b][f // 4]], w=[PB[bk]], silent=(f != NFC - 1))
                    kb.A(lambda e, i2=i2, bk=bk: e.copy(out=yo[i2], in_=ps[:, bk, :]), r=[PB[bk]], w=[Tyo[i2]])
                    kb.DMA(lambda e, i2=i2, tt=tt, c4=c4: e.dma_start(out=y_d[tt * 128:(tt + 1) * 128, c4 * 512:(c4 + 1) * 512], in_=yo[i2]), r=[Tyo[i2]], w=[Ty[tt]], q=STQ)
            ar.pop()
            P.barrier()
            ar.push()
            gt = ar.alloc(D, F32)
            Tg = Tile("g")
            kb.DMA(lambda e: e.dma_start(out=gt, in_=norm_g[l, 3:4, :].partition_broadcast(128)), w=[Tg])
            yt = [ar.alloc(D, F32) for _ in range(4)]
            Tyt = [Tile("yt%d" % i) for i in range(4)]
            xt = [ar.alloc(D, F32) for _ in range(4)]
            Tx = [Tile("xt%d" % i) for i in range(4)]
            ss = [ar.alloc(1, F32) for _ in range(4)]
            Tss = [Tile("ss%d" % i) for i in range(4)]
            tmp = ar.alloc(D, F32)
            Ttmp = Tile("tmp")
            junk = ar.alloc(D, BF16)
            Tj = Tile("junk")
            for tt in range(NT):
                b = tt % 4
                kb.DMA(lambda e, b=b, tt=tt: e.dma_start(out=yt[b], in_=y_d[tt * 128:(tt + 1) * 128, :]), r=[Ty[tt]], w=[Tyt[b]])
                kb.resid_update(yt[b], Tyt[b], out[tt * 128:(tt + 1) * 128, :], Tout[tt], out[tt * 128:(tt + 1) * 128, :], Tout[tt], gt, Tg,
                                (xt[b], Tx[b], tmp, Ttmp, junk, Tj, ss[b], Tss[b]))
            ar.pop()
            P.barrier()

        def layer_a(l):
            P.barrier()
            ar.push()
            hT = ar.alloc(KC * S, BF16)
            hT3 = hT.rearrange("p (k t) -> p k t", k=KC)
            ThT = [Tile("hT%d" % i) for i in range(NT)]
            wb = [ar.alloc(KC * 128, BF16).rearrange("p (k n) -> p k n", k=KC) for _ in range(3)]
            Twb = [Tile("wb%d" % i) for i in range(3)]
            hTm = ar.alloc(KC * 256, BF16).rearrange("p (k t) -> p k t", k=KC)
            ThTm = [Tile("hTm0"), Tile("hTm1")]
            kTm = ar.alloc(4 * 256, BF16).rearrange("p (h t) -> p h t", h=4)
            TkTm = Tile("kTm")
            Vm = ar.alloc(4 * 2 * 128, BF16).rearrange("p (h a d) -> p h a d", h=4, a=2)
            TVm = Tile("Vm")
            kb.mem_kv(mem, Tmem, norm_g[l, 4:5, :], mem_w_kv[l], hTm, ThTm, kTm, TkTm, Vm, TVm, wb, Twb)
            if state["first"]:
                kb.normT(lambda tt: x_in[tt * 128:(tt + 1) * 128, :], [Tin] * NT, norm_g[l, 0:1, :], hT3, ThT)
            else:
                kb.normT(lambda tt: out[tt * 128:(tt + 1) * 128, :], Tout, norm_g[l, 0:1, :], hT3, ThT)
            tab = ar.alloc(3 * TABW, F32).rearrange("p (g c) -> p g c", g=3)
            Ttab = Tile("tabA")
            kb.DMA(lambda e: e.dma_start(out=tab, in_=cin["c_tabA"]), w=[Ttab])
            kb.attn_bufs()
            qT = [ar.alloc(S, BF16) for _ in range(3)]
            kT = [ar.alloc(S, BF16) for _ in range(3)]
            Vt = [ar.alloc(NT * 128, BF16).rearrange("p (t d) -> p t d", t=NT) for _ in range(3)]
            Tq = [Tile("q%d" % i) for i in range(3)]
            Tk = [Tile("k%d" % i) for i in range(3)]
            Tv = [Tile("v%d" % i) for i in range(3)]
            win = a_w_in[l].rearrange("(k p) n -> p k n", p=128)
            for j in range(8):
                for g in range(3):
                    hq = g * 8 + j
                    kb.wload(win[:, :, hq * 128:(hq + 1) * 128], wb[0], Twb[0], k=KC, n=128)
                    kb.proj_fm(wb[0], Twb[0], hT3, ThT, qT[g], Tq[g], ISQ)
                    kb.wload(win[:, :, (24 + hq) * 128:(24 + hq + 1) * 128], wb[1], Twb[1], k=KC, n=128)
                    kb.proj_fm(wb[1], Twb[1], hT3, ThT, kT[g], Tk[g], 1.0)
                    kb.wload(win[:, :, (48 + hq) * 128:(48 + hq + 1) * 128], wb[2], Twb[2], k=KC, n=128)
                    kb.proj_tm(wb[2], Twb[2], hT3, ThT, Vt[g], Tv[g])
                for tb in range(4):
                    first = True
                    for g in range(3):
                        slope = alibi(24, g * 8 + j)
                        lo = {0: 4 * tb - 1, 1: 4 * tb - 4, 2: 0}[g]
                        units = []
                        for kt in range(max(0, lo), 4 * tb + 4):
                            delta = 512 * tb - 128 * kt
                            de = min(delta, 128) if g == 2 else delta
                            bias = -slope * (delta - de)
                            units.append(dict(kT=kT[g][:, kt * 128:(kt + 1) * 128], V=Vt[g][:, kt, :], rd=[Tk[g], Tv[g]],
                                              tab=tab[:, g, de + 511:de + 511 + 512], tabrd=[Ttab], slope=slope, bias=bias))
                        kb._chain_first = first
                        run_units_chain(units, qT[g][:, tb * 512:(tb + 1) * 512], Tq[g], first, g == 2)
                        first = False
                    kb.finalize_plain(oT_d[tb * 4:(tb + 1) * 4, :, j, :].rearrange("t p c -> p t c"), ToT[j])
            for h in range(4):
                kb.wload(win[:, :, 9216 + h * 128:9216 + (h + 1) * 128], wb[0], Twb[0], k=KC, n=128)
                kb.proj_fm(wb[0], Twb[0], hT3, ThT, qT[0], Tq[0], ISQ)
                for tb in range(4):
                    units = [dict(kT=kTm[:, h, kt * 128:(kt + 1) * 128], V=Vm[:, h, kt, :], rd=[TkTm, TVm], tab=None) for kt in range(2)]
                    run_units_chain(units, qT[0][:, tb * 512:(tb + 1) * 512], Tq[0], True, True)
                    kb.finalize_plain(oT_d[tb * 4:(tb + 1) * 4, :, 8 + h, :].rearrange("t p c -> p t c"), ToT[8 + h])
            ar.pop()
            w_out_phase(a_w_out[l], 12, norm_g[l, 1:2, :])

        def run_units_chain(units, q_ap, Tq, first, last):
            nu = len(units)
            OB, DB = 2, 3
            kbx = kb
            sbL, ptL, onesL = kb.sb, kb.pt, kb.onesb
            SBK = (0, 1, 4, 5)
            LA = 3

            def s_stage(i):
                u = units[i]
                b = i % 4
                pen = u.get("pen")
                kbx.T(lambda e, u=u, b=b: e.matmul(ps[:, SBK[b], :], lhsT=u["kT"], rhs=q_ap, start=True, stop=(u.get("pen") is None)),
                      r=list(u["rd"]) + [Tq], w=[PB[SBK[b]]], silent=(pen is not None))
                if pen is not None:
                    kbx.T(lambda e, pen=pen, b=b: e.matmul(ps[:, SBK[b], :], lhsT=pen[0], rhs=pen[1], start=False, stop=True),
                          r=list(pen[2]), w=[PB[SBK[b]]])

            def mid(i):
                u = units[i]
                b = i % 4
                bias = float(u.get("bias", 0.0))
                if u.get("tab") is not None:
                    kbx.V(lambda e, u=u, b=b: e.scalar_tensor_tensor(out=sbL[b], in0=u["tab"], scalar=float(u["slope"]), in1=ps[:, SBK[b], :], op0=ALU.mult, op1=ALU.add),
                          r=[PB[SBK[b]]] + list(u.get("tabrd", [])), w=[kbx.Tsb[b]])
                    src, rd = sbL[b], [kbx.Tsb[b]]
                else:
                    src, rd = ps[:, SBK[b], :], [PB[SBK[b]]]
                if bias != 0.0:
                    kbx.A(lambda e, b=b, src=src, bias=bias: e.activation(out=ptL[b], in_=src, func=AF.Exp, bias=bias), r=rd, w=[kbx.Tpt[b]])
                else:
                    kbx.A(lambda e, b=b, src=src: e.activation(out=ptL[b], in_=src, func=AF.Exp), r=rd, w=[kbx.Tpt[b]])

            def pv(i):
                u = units[i]
                b = i % 4
                st = first and (i == 0)
                sp = last and (i == nu - 1)
                imp = u.get("imp")
                kbx.T(lambda e, u=u, b=b: e.matmul(ps[:, OB, :], lhsT=u["V"], rhs=ptL[b], start=st, stop=sp),
                      r=list(u["rd"]) + [kbx.Tpt[b]], w=[PB[OB]], silent=True)
                kbx.T(lambda e, b=b: e.matmul(ps[:, DB, :], lhsT=onesL, rhs=ptL[b], start=st, stop=sp),
                      r=[kbx.Tones, kbx.Tpt[b]], w=[PB[DB]], silent=(imp is not None))
                if imp is not None:
                    kbx.T(lambda e, b=b, imp=imp: e.matmul(ps[0:32, 7, :], lhsT=imp[0], rhs=ptL[b], start=st, stop=sp),
                          r=[imp[1], kbx.Tpt[b]], w=[PB[7]])

            for i in range(min(LA, nu)):
                s_stage(i)
            for i in range(nu):
                if i + LA < nu:
                    s_stage(i + LA)
                mid(i)
                pv(i)

        def shared_kv_phase():
            P.barrier()
            ar.push()
            hT = ar.alloc(KC * S, BF16)
            hT3 = hT.rearrange("p (k t) -> p k t", k=KC)
            ThT = [Tile("hT%d" % i) for i in range(NT)]
            if state["first"]:
                kb.normT(lambda tt: x_in[tt * 128:(tt + 1) * 128, :], [Tin] * NT, kv_norm_g[0:1, :], hT3, ThT)
            else:
                kb.normT(lambda tt: out[tt * 128:(tt + 1) * 128, :], Tout, kv_norm_g[0:1, :], hT3, ThT)
            wb = [ar.alloc(KC * 128, BF16).rearrange("p (k n) -> p k n", k=KC) for _ in range(2)]
            Twb = [Tile("wb%d" % i) for i in range(2)]
            tmpT = [ar.alloc(S, BF16) for _ in range(2)]
            Ttmp = [Tile("tmpT%d" % i) for i in range(2)]
            kvw = kv_w.rearrange("(k p) n -> p k n", p=128)
            it = 0
            for g in range(4):
                for which, slot, fm in ((2, 0, True), (3, 1, False), (4, 2, True), (5, 3, False)):
                    b = it % 2
                    it += 1
                    col = which * 512 + g * 128
                    kb.wload(kvw[:, :, col:col + 128], wb[b], Twb[b], k=KC, n=128)
                    if fm:
                        kb.proj_fm(wb[b], Twb[b], hT3, ThT, tmpT[b], Ttmp[b], 1.0)
                    else:
                        kb.proj_tm(wb[b], Twb[b], hT3, ThT, tmpT[b].rearrange("p (t d) -> p t d", t=NT), Ttmp[b])
                    kb.DMA(lambda e, b=b, g=g, slot=slot: e.dma_start(out=sh_d[g, slot], in_=tmpT[b]), r=[Ttmp[b]], w=[Tsh[g]], q=STQ)
            w1sb = ar.alloc(32 * 512, BF16).rearrange("p (l n) -> p l n", l=32)
            Tw1 = [Tile("w1_%d" % q) for q in range(8)]
            w2sb = ar.alloc(4 * 128, BF16).rearrange("p (c n) -> p c n", c=4)
            Tw2 = Tile("w2")
            pef = ar.alloc(128, F32)
            Tpef = Tile("pef")
            peT = ar.alloc(32, BF16)
            TpeT = Tile("peT")
            b1 = ar.alloc(4, F32)
            Tb1 = Tile("b1")
            hx = ar.alloc(128, F32)
            Thx = Tile("hx")
            x2 = ar.alloc(128, F32)
            Tx2 = Tile("x2")
            sgm = ar.alloc(128, F32)
            Tsgm = Tile("sgm")
            gel = ar.alloc(4 * 128, BF16).rearrange("p (c n) -> p c n", c=4)
            Tgel = Tile("gel")
            csb = ar.alloc(4 * 128, BF16).rearrange("p (g n) -> p g n", g=4)
            Tcsb = Tile("csb")
            for which, w1, w2 in ((0, cmp_wk1, cmp_wk2), (1, cmp_wv1, cmp_wv2)):
                w1v = w1.rearrange("(l p) n -> p l n", p=128)
                for q in range(8):
                    kb.wload(w1v[:, q * 4:(q + 1) * 4, :], w1sb[:, q * 4:(q + 1) * 4, :], Tw1[q], k=4, n=512)
                kb.wload(w2.rearrange("(c p) n -> p c n", p=128), w2sb, Tw2, k=4, n=128)
                kb.DMA(lambda e, which=which: e.dma_start(out=pef[0:32, :], in_=cmp_pe[which]), w=[Tpef])
                kb.T(lambda e: e.transpose(out=ps[:, 6, 0:32], in_=pef[0:32, :], identity=kb.identf[0:32, 0:32]), r=[Tpef, kb.Tidf], w=[PB[6]])
                kb.V(lambda e: e.tensor_copy(out=peT, in_=ps[:, 6, 0:32]), r=[PB[6]], w=[TpeT])
                for hc in range(4):
                    for l_ in range(32):
                        kb.T(lambda e, hc=hc, l_=l_: e.matmul(ps[:, 7, hc:hc + 1], lhsT=w1sb[:, l_, hc * 128:(hc + 1) * 128], rhs=peT[:, l_:l_ + 1], start=(l_ == 0), stop=(l_ == 31)),
                             r=[Tw1[l_ // 4], TpeT], w=[PB[7]], silent=(l_ != 31))
                kb.V(lambda e: e.tensor_copy(out=b1, in_=ps[:, 7, 0:4]), r=[PB[7]], w=[Tb1])
                kb.V(lambda e: e.memset(csb, 0.0), w=[Tcsb])
                for g in range(4):
                    b = it % 2
                    it += 1
                    col = which * 512 + g * 128
                    kb.wload(kvw[:, :, col:col + 128], wb[b], Twb[b], k=KC, n=128)
                    kb.proj_fm(wb[b], Twb[b], hT3, ThT, tmpT[b], Ttmp[b], 1.0)
                    kr3 = tmpT[b].rearrange("p (n s) -> p n s", s=16)
                    for hc in range(4):
                        bk = 4 + hc % 2
                        for l_ in range(32):
                            rhs = kr3[:, 0:127, l_] if l_ < 16 else kr3[:, 1:128, l_ - 16]
                            kb.T(lambda e, hc=hc, l_=l_, rhs=rhs, bk=bk: e.matmul(ps[:, bk, 0:127], lhsT=w1sb[:, l_, hc * 128:(hc + 1) * 128], rhs=rhs, start=(l_ == 0), stop=(l_ == 31)),
                                 r=[Tw1[l_ // 4], Ttmp[b]], w=[PB[bk]], silent=(l_ != 31))
                        kb.V(lambda e, hc=hc, bk=bk: e.tensor_scalar(out=hx[:, 0:127], in0=ps[:, bk, 0:127], scalar1=b1[:, hc:hc + 1], scalar2=None, op0=ALU.add), r=[PB[bk], Tb1], w=[Thx])
                        kb.V(lambda e: e.tensor_tensor(out=x2[:, 0:127], in0=hx[:, 0:127], in1=hx[:, 0:127], op=ALU.mult), r=[Thx], w=[Tx2])
                        kb.V(lambda e: e.tensor_scalar(out=x2[:, 0:127], in0=x2[:, 0:127], scalar1=0.044715, scalar2=1.0, op0=ALU.mult, op1=ALU.add), r=[Tx2], w=[Tx2])
                        kb.V(lambda e: e.tensor_tensor(out=x2[:, 0:127], in0=x2[:, 0:127], in1=hx[:, 0:127], op=ALU.mult), r=[Tx2, Thx], w=[Tx2])
                        kb.A(lambda e: e.activation(out=sgm[:, 0:127], in_=x2[:, 0:127], func=AF.Sigmoid, scale=1.5957691216057308), r=[Tx2], w=[Tsgm])
                        kb.V(lambda e, hc=hc: e.tensor_tensor(out=gel[:, hc, 0:127], in0=hx[:, 0:127], in1=sgm[:, 0:127], op=ALU.mult), r=[Thx, Tsgm], w=[Tgel])
                    if which == 0:
                        for hc in range(4):
                            kb.T(lambda e, hc=hc: e.matmul(ps[:, 6, 0:127], lhsT=w2sb[:, hc, :], rhs=gel[:, hc, 0:127], start=(hc == 0), stop=(hc == 3)), r=[Tw2, Tgel], w=[PB[6]], silent=(hc != 3))
                        kb.V(lambda e, g=g: e.tensor_copy(out=csb[:, g, 0:127], in_=ps[:, 6, 0:127]), r=[PB[6]], w=[Tcsb])
                    else:
                        for hc in range(4):
                            kb.T(lambda e, hc=hc: e.matmul(ps[0:127, 6, 0:128], lhsT=gel[:, hc, 0:127], rhs=w2sb[:, hc, :], start=(hc == 0), stop=(hc == 3)), r=[Tw2, Tgel], w=[PB[6]], silent=(hc != 3))
                        kb.V(lambda e, g=g: e.tensor_copy(out=csb[0:127, g, :], in_=ps[0:127, 6, 0:128]), r=[PB[6]], w=[Tcsb])
                kb.DMA(lambda e, which=which: e.dma_start(out=kc_d[which], in_=csb), r=[Tcsb], w=[Tkc], q=STQ)
            ar.pop()
            P.barrier()

        def layer_b(l):
            lb = l - 2
            P.barrier()
            ar.push()
            kTm = ar.alloc(4 * 256, BF16).rearrange("p (h t) -> p h t", h=4)
            TkTm = Tile("kTm")
            Vm = ar.alloc(4 * 2 * 128, BF16).rearrange("p (h a d) -> p h a d", h=4, a=2)
            TVm = Tile("Vm")
            ghi = ar.alloc(S, BF16)
            glo = ar.alloc(S, BF16)
            Tgh = Tile("ghi")
            Tgl = Tile("glo")
            win = b_w_in[lb].rearrange("(k p) n -> p k n", p=128)
            ar.push()
            hT = ar.alloc(KC * S, BF16)
            hT3 = hT.rearrange("p (k t) -> p k t", k=KC)
            ThT = [Tile("hT%d" % i) for i in range(NT)]
            wb = [ar.alloc(KC * 128, BF16).rearrange("p (k n) -> p k n", k=KC) for _ in range(2)]
            Twb = [Tile("wb%d" % i) for i in range(2)]
            hTm = ar.alloc(KC * 256, BF16).rearrange("p (k t) -> p k t", k=KC)
            ThTm = [Tile("hTm0"), Tile("hTm1")]
            kb.mem_kv(mem, Tmem, norm_g[l, 4:5, :], mem_w_kv[l], hTm, ThTm, kTm, TkTm, Vm, TVm, wb, Twb)
            if state["first"]:
                kb.normT(lambda tt: x_in[tt * 128:(tt + 1) * 128, :], [Tin] * NT, norm_g[l, 0:1, :], hT3, ThT)
            else:
                kb.normT(lambda tt: out[tt * 128:(tt + 1) * 128, :], Tout, norm_g[l, 0:1, :], hT3, ThT)
            qtmp = [ar.alloc(S, BF16) for _ in range(2)]
            Tqtmp = [Tile("qtmp%d" % i) for i in range(2)]
            for h in range(16):
                b = h % 2
                col = h * 128 if h < 12 else 1572 + (h - 12) * 128
                kb.wload(win[:, :, col:col + 128], wb[b], Twb[b], k=KC, n=128)
                kb.proj_fm(wb[b], Twb[b], hT3, ThT, qtmp[b], Tqtmp[b], ISQ)
                kb.DMA(lambda e, b=b, h=h: e.dma_start(out=qT_d[h], in_=qtmp[b]), r=[Tqtmp[b]], w=[TqT[h]], q=STQ)
            wgt = ar.alloc(KC * 36, BF16).rearrange("p (k n) -> p k n", k=KC)
            Twgt = Tile("wgt")
            kb.wload(win[:, :, 1536:1572], wgt, Twgt, k=KC, n=36)
            gtok = [ar.alloc(36, F32) for _ in range(2)]
            Tgtok = [Tile("gtok%d" % i) for i in range(2)]
            gTf = ar.alloc(S, F32)
            TgTf = Tile("gTf")
            kb.V(lambda e: e.memset(ghi, 0.0), w=[Tgh])
            kb.V(lambda e: e.memset(glo, 0.0), w=[Tgl])
            for tt in range(NT):
                b = tt % 2
                bk = 6 + b
                bk2 = 4 + b
                for k in range(KC):
                    kb.T(lambda e, k=k, tt=tt, bk=bk: e.matmul(ps[:, bk, 0:36], lhsT=hT3[:, k, tt * 128:(tt + 1) * 128], rhs=wgt[:, k, :], start=(k == 0), stop=(k == KC - 1)),
                         r=[Twgt, ThT[tt]], w=[PB[bk]], silent=(k != KC - 1))
                kb.A(lambda e, b=b, bk=bk: e.activation(out=gtok[b], in_=ps[:, bk, 0:36], func=AF.Sigmoid), r=[PB[bk]], w=[Tgtok[b]])
                kb.T(lambda e, b=b, bk2=bk2: e.transpose(out=ps[0:36, bk2, 0:128], in_=gtok[b], identity=kb.identf), r=[Tgtok[b], kb.Tidf], w=[PB[bk2]])
                kb.V(lambda e, tt=tt, bk2=bk2: e.tensor_copy(out=gTf[0:36, tt * 128:(tt + 1) * 128], in_=ps[0:36, bk2, 0:128]), r=[PB[bk2]], w=[TgTf])
            kb.V(lambda e: e.tensor_copy(out=ghi[0:36, :], in_=gTf[0:36, :]), r=[TgTf], w=[Tgh])
            kb.V(lambda e: e.tensor_tensor(out=gTf[0:36, :], in0=gTf[0:36, :], in1=ghi[0:36, :], op=ALU.subtract), r=[TgTf, Tgh], w=[TgTf])
            kb.V(lambda e: e.tensor_copy(out=glo[0:36, :], in_=gTf[0:36, :]), r=[TgTf], w=[Tgl])
            ar.pop()
            P.barrier()
            ar.push()
            tabB = ar.alloc(2 * TABW, F32).rearrange("p (g c) -> p g c", g=2)
            TtabB = Tile("tabB")
            kb.DMA(lambda e: e.dma_start(out=tabB, in_=cin["c_tabB"]), w=[TtabB])
            cmpT = ar.alloc(S, F32)
            Tcmp = Tile("cmpT")
            kb.DMA(lambda e: e.dma_start(out=cmpT, in_=cin["c_cmp"]), w=[Tcmp])
            keep = ar.alloc(16 * 32, F32).rearrange("p (t j) -> p t j", t=16)
            force = ar.alloc(16 * 32, F32).rearrange("p (t j) -> p t j", t=16)
            Tkeep = Tile("keep")
            Tforce = Tile("force")
            kb.DMA(lambda e: e.dma_start(out=keep, in_=cin["c_keep"]), w=[Tkeep])
            kb.DMA(lambda e: e.dma_start(out=force, in_=cin["c_force"]), w=[Tforce])
            ovf = ar.alloc(32, F32)
            Tovf = Tile("ovf")
            ov_b = ar.alloc(32, BF16)
            Tov = Tile("ov")
            kb.DMA(lambda e: e.dma_start(out=ovf, in_=cin["c_ov"]), w=[Tovf])
            kb.V(lambda e: e.tensor_copy(out=ov_b, in_=ovf), r=[Tovf], w=[Tov])
            E_b = ar.alloc(16 * 128, BF16).rearrange("p (k s) -> p k s", k=16)
            TE = Tile("E")
            kb.wload(cin["c_E"], E_b, TE, k=16, n=128)
            oh_b = ar.alloc(36 * 128, BF16).rearrange("p (r m) -> p r m", r=36)
            Toh = Tile("oh")
            for q in range(3):
                kb.wload(cin["c_onehot"][:, q * 12:(q + 1) * 12, :], oh_b[:, q * 12:(q + 1) * 12, :], Toh, k=12, n=128)
            kcs = ar.alloc(4 * 128, BF16).rearrange("p (g n) -> p g n", g=4)
            vcs = ar.alloc(4 * 128, BF16).rearrange("p (g n) -> p g n", g=4)
            Tkcs = Tile("kcs")
            kb.DMA(lambda e: e.dma_start(out=kcs, in_=kc_d[0]), r=[Tkc], w=[Tkcs])
            kb.DMA(lambda e: e.dma_start(out=vcs, in_=kc_d[1]), r=[Tkc], w=[Tkcs])
            kb.attn_bufs()
            shg = [ar.alloc(4 * S, BF16).rearrange("p (s t) -> p s t", s=4) for _ in range(2)]
            Tshg = [Tile("shg%d" % i) for i in range(2)]
            qb = [ar.alloc(S, BF16) for _ in range(3)]
            Tqb = [Tile("qb%d" % i) for i in range(3)]
            ocmp = [ar.alloc(S, F32) for _ in range(3)]
            Toc = [Tile("ocmp%d" % i) for i in range(3)]
            impT = ar.alloc(S, F32)
            Timp = Tile("impT")
            penT = ar.alloc(S, BF16)
            Tpen = Tile("penT")
            kb.V(lambda e: e.memset(penT, 0.0), w=[Tpen])
            rg = ar.alloc(512, F32)
            Trg = Tile("rg")
            tmpo = ar.alloc(512, F32)
            Ttmpo = Tile("tmpo")
            v1 = ar.alloc(32, F32)
            v2 = ar.alloc(32, F32)
            mxa = ar.alloc(8, F32)
            mxb = ar.alloc(8, F32)
            selp = ar.alloc(32, F32)
            Tv1, Tv2, Tmxa, Tmxb, Tselp = Tile("v1"), Tile("v2"), Tile("mxa"), Tile("mxb"), Tile("selp")

            def finalize_gated(h, branch, tb, dst, Tdst, mode, dram=None, Tdram=None):
                kb.recip_den()
                recL = kb.rec
                r_ = h * 3 + branch
                kb.T(lambda e: e.matmul(ps[:, 6, :], lhsT=oh_b[:, r_, :], rhs=ghi[:, tb * 512:(tb + 1) * 512], start=True, stop=False), r=[Toh, Tgh], w=[PB[6]], silent=True)
                kb.T(lambda e: e.matmul(ps[:, 6, :], lhsT=oh_b[:, r_, :], rhs=glo[:, tb * 512:(tb + 1) * 512], start=False, stop=True), r=[Toh, Tgl], w=[PB[6]])
                kb.V(lambda e: e.tensor_tensor(out=rg, in0=ps[:, 6, :], in1=recL, op=ALU.mult), r=[PB[6], kb.Trec], w=[Trg])
                if mode == "set":
                    kb.V(lambda e: e.tensor_tensor(out=dst, in0=ps[:, 2, :], in1=rg, op=ALU.mult), r=[PB[2], Trg], w=[Tdst])
                elif mode == "add":
                    kb.V(lambda e: e.tensor_tensor(out=tmpo, in0=ps[:, 2, :], in1=rg, op=ALU.mult), r=[PB[2], Trg], w=[Ttmpo])
                    kb.G(lambda e: e.tensor_tensor(out=dst, in0=dst, in1=tmpo, op=ALU.add), r=[Ttmpo, Tdst], w=[Tdst])
                else:
                    kb.V(lambda e: e.tensor_tensor(out=tmpo, in0=ps[:, 2, :], in1=rg, op=ALU.mult), r=[PB[2], Trg], w=[Ttmpo])
                    ob = kb.osb_i % 2
                    kb.osb_i += 1
                    osbL = kb.osb[ob]
                    kb.G(lambda e: e.tensor_tensor(out=osbL, in0=dst, in1=tmpo, op=ALU.add), r=[Ttmpo, Tdst], w=[kb.Tosb[ob]])
                    kb.DMA(lambda e: e.dma_start(out=dram, in_=osbL.rearrange("p (t c) -> p t c", t=4)), r=[kb.Tosb[ob]], w=[Tdram], q=STQ)

            for g in range(4):
                sb_ = g % 2
                kb.DMA(lambda e, g=g, sb_=sb_: e.dma_start(out=shg[sb_], in_=sh_d[g].rearrange("s p t -> p s t")), r=[Tsh[g]], w=[Tshg[sb_]])
                ksT = shg[sb_][:, 0, :]
                vs = shg[sb_][:, 1, :].rearrange("p (t d) -> p t d", t=NT)
                kwT = shg[sb_][:, 2, :]
                vw = shg[sb_][:, 3, :].rearrange("p (t d) -> p t d", t=NT)
                for hh in range(3):
                    h = 3 * g + hh
                    slope = alibi(12, h)
                    kb.DMA(lambda e, hh=hh, h=h: e.dma_start(out=qb[hh], in_=qT_d[h]), r=[TqT[h]], w=[Tqb[hh]])
                    for tb in range(4):
                        units = [dict(kT=kcs[:, g, :], V=vcs[:, g, :], rd=[Tkcs], tab=cmpT[:, tb * 512:(tb + 1) * 512], tabrd=[Tcmp], slope=slope, bias=0.0, imp=(ov_b, Tov))]
                        run_units_chain(units, qb[hh][:, tb * 512:(tb + 1) * 512], Tqb[hh], True, True)
                        finalize_gated(h, 0, tb, ocmp[hh][:, tb * 512:(tb + 1) * 512], Toc[hh], "set")
                        recI = kb.rec
                        if hh == 0:
                            kb.V(lambda e, tb=tb: e.tensor_tensor(out=impT[0:32, tb * 512:(tb + 1) * 512], in0=ps[0:32, 7, :], in1=recI[0:32, :], op=ALU.mult), r=[PB[7], kb.Trec], w=[Timp])
                        else:
                            kb.V(lambda e: e.tensor_tensor(out=tmpo[0:32, :], in0=ps[0:32, 7, :], in1=recI[0:32, :], op=ALU.mult), r=[PB[7], kb.Trec], w=[Ttmpo])
                            kb.V(lambda e, tb=tb: e.tensor_tensor(out=impT[0:32, tb * 512:(tb + 1) * 512], in0=impT[0:32, tb * 512:(tb + 1) * 512], in1=tmpo[0:32, :], op=ALU.add), r=[Ttmpo, Timp], w=[Timp])
                for tt in range(NT):
                    kb.T(lambda e, tt=tt: e.transpose(out=ps[:, 6, 0:32], in_=impT[0:32, tt * 128:(tt + 1) * 128], identity=kb.identf[0:32, 0:32]), r=[Timp, kb.Tidf], w=[PB[6]])
                    kb.V(lambda e, tt=tt: e.tensor_tensor(out=v1, in0=ps[:, 6, 0:32], in1=keep[:, tt, :], op=ALU.mult), r=[PB[6], Tkeep], w=[Tv1])
                    kb.V(lambda e, tt=tt: e.tensor_tensor(out=v1, in0=v1, in1=force[:, tt, :], op=ALU.add), r=[Tv1, Tforce], w=[Tv1])
                    kb.V(lambda e: e.max(out=mxa, in_=v1), r=[Tv1], w=[Tmxa])
                    kb.V(lambda e: e.match_replace(out=v2, in_to_replace=mxa, in_values=v1, imm_value=-1.0e9), r=[Tv1, Tmxa], w=[Tv2])
                    kb.V(lambda e: e.max(out=mxb, in_=v2), r=[Tv2], w=[Tmxb])
                    kb.V(lambda e: e.tensor_scalar(out=selp, in0=v1, scalar1=mxb[:, 7:8], scalar2=None, op0=ALU.is_ge), r=[Tv1, Tmxb], w=[Tselp])
                    kb.V(lambda e: e.tensor_scalar(out=selp, in0=selp, scalar1=1.0, scalar2=30000.0, op0=ALU.subtract, op1=ALU.mult), r=[Tselp], w=[Tselp])
                    kb.T(lambda e: e.transpose(out=ps[0:32, 7, 0:128], in_=selp, identity=kb.identf), r=[Tselp, kb.Tidf], w=[PB[7]])
                    kb.V(lambda e, tt=tt: e.tensor_copy(out=penT[0:32, tt * 128:(tt + 1) * 128], in_=ps[0:32, 7, 0:128]), r=[PB[7]], w=[Tpen])
                for hh in range(3):
                    h = 3 * g + hh
                    slope = alibi(12, h)
                    for tb in range(4):
                        q_ap = qb[hh][:, tb * 512:(tb + 1) * 512]
                        units = []
                        for kt in range(0, 4 * tb + 4):
                            delta = 512 * tb - 128 * kt
                            de = min(delta, 128)
                            units.append(dict(kT=ksT[:, kt * 128:(kt + 1) * 128], V=vs[:, kt, :], rd=[Tshg[sb_]], tab=tabB[:, 0, de + 511:de + 511 + 512], tabrd=[TtabB],
                                              slope=slope, bias=-slope * (delta - de), pen=(E_b[:, kt, :], penT[:, tb * 512:(tb + 1) * 512], [TE, Tpen])))
                        run_units_chain(units, q_ap, Tqb[hh], True, True)
                        finalize_gated(h, 1, tb, ocmp[hh][:, tb * 512:(tb + 1) * 512], Toc[hh], "add")
                        units = []
                        for kt in range(max(0, 4 * tb - 4), 4 * tb + 4):
                            delta = 512 * tb - 128 * kt
                            units.append(dict(kT=kwT[:, kt * 128:(kt + 1) * 128], V=vw[:, kt, :], rd=[Tshg[sb_]], tab=tabB[:, 1, delta + 511:delta + 511 + 512], tabrd=[TtabB],
                                              slope=slope, bias=0.0))
                        run_units_chain(units, q_ap, Tqb[hh], True, True)
                        finalize_gated(h, 2, tb, ocmp[hh][:, tb * 512:(tb + 1) * 512], Toc[hh], "final", dram=oT_d[tb * 4:(tb + 1) * 4, :, h, :].rearrange("t p c -> p t c"), Tdram=ToT[h])
            for h in range(4):
                kb.DMA(lambda e, h=h: e.dma_start(out=qb[0], in_=qT_d[12 + h]), r=[TqT[12 + h]], w=[Tqb[0]])
                for tb in range(4):
                    units = [dict(kT=kTm[:, h, kt * 128:(kt + 1) * 128], V=Vm[:, h, kt, :], rd=[TkTm, TVm], tab=None) for kt in range(2)]
                    run_units_chain(units, qb[0][:, tb * 512:(tb + 1) * 512], Tqb[0], True, True)
                    kb.finalize_plain(oT_d[tb * 4:(tb + 1) * 4, :, 12 + h, :].rearrange("t p c -> p t c"), ToT[12 + h])
            ar.pop()
            ar.pop()
            w_out_phase(b_w_out[lb], 16, norm_g[l, 1:2, :])

        if seq is not None:
            for ph in seq:
                if ph == "a0":
                    layer_a(0)
                elif ph == "skv":
                    shared_kv_phase()
                elif ph == "b2":
                    layer_b(2)
                elif ph == "f0":
                    ffn_phase(0)
                if dbg and ph == "a0":
                    P.barrier()
                    Td = Tile("dbg")
                    for q in range(4):
                        kb.DMA(lambda e, q=q: e.dma_start(out=dbg_t["xm0"][q * 512:(q + 1) * 512, :], in_=out[q * 512:(q + 1) * 512, :]), r=Tout, w=[Td])
                    P.barrier()
        for l in range(l_start, n_layers if seq is None else 0):
            if l < 2:
                for _r in range(rep):
                    layer_a(l)
            else:
                if l == 2 or l == l_start:
                    shared_kv_phase()
                layer_b(l)
            def snap(name):
                if not dbg:
                    return
                P.barrier()
                Td = Tile("dbg")
                for q in range(4):
                    kb.DMA(lambda e, q=q: e.dma_start(out=dbg_t[name][q * 512:(q + 1) * 512, :], in_=out[q * 512:(q + 1) * 512, :]), r=Tout, w=[Td])
                P.barrier()
            snap("xm%d" % l)
            if stop_after_mixer and l == n_layers - 1:
                break
            ffn_phase(l)
            snap("x%d" % l)

        if dummy:
            P.barrier()
            dz = ar.alloc(64, F32)
            Tdz = Tile("dz")
            dz8 = ar.alloc(8, F32)
            kb.V(lambda e: e.memset(dz, 0.5), w=[Tdz])
            if "sigmoid" in dummy:
                kb.A(lambda e: e.activation(out=dz, in_=dz, func=AF.Sigmoid, scale=1.5), r=[Tdz], w=[Tdz])
            if "max" in dummy:
                kb.V(lambda e: e.max(out=dz8, in_=dz[:, 0:32]), r=[Tdz], w=[Tdz])
                kb.V(lambda e: e.match_replace(out=dz[:, 32:64], in_to_replace=dz8, in_values=dz[:, 0:32], imm_value=-1.0e9), r=[Tdz], w=[Tdz])
            if "isge" in dummy:
                kb.V(lambda e: e.tensor_scalar(out=dz[:, 0:32], in0=dz[:, 0:32], scalar1=dz8[:, 7:8], scalar2=None, op0=ALU.is_ge), r=[Tdz], w=[Tdz])
        P.final_wait("sync")
        P.emit(es)
        print("arena peak bytes/partition:", ar.peak * 2, "ops:", {e: len(P.ops[e]) for e in ENGS}, "semcounts:", P.count, max(P.dma_cnt))
    return nc


CONSTS = None


def kernel(**inputs):
    global CONSTS
    if CONSTS is None:
        CONSTS = make_consts()
    nc = build()
    x = np.ascontiguousarray(inputs["x"], dtype=np.float32)
    shared = {k: np.ascontiguousarray(v, dtype=np.float32) for k, v in inputs.items() if k not in ("x", "mem")}
    shared["kv_norm_g"] = shared["kv_norm_g"].reshape(1, D)
    in_maps = []
    for c in range(8):
        b = c % 4
        m = dict(shared)
        m.update(CONSTS)
        m["x"] = x[b]
        m["mem"] = np.ascontiguousarray(inputs["mem"][b], dtype=np.float32)
        in_maps.append(m)
    res = run_bass_kernel_spmd(nc, in_maps, core_ids=list(range(8)))
    return np.stack([res.results[b]["out"] for b in range(4)], 0).astype(np.float32)
```

```python
import math
from contextlib import ExitStack

import numpy as np
import concourse.bass as bass
import concourse.mybir as mybir
from concourse.bass_utils import run_bass_kernel_spmd

F32 = mybir.dt.float32
BF16 = mybir.dt.bfloat16
AF = mybir.ActivationFunctionType
ALU = mybir.AluOpType

ENGS = ("tensor", "vector", "scalar", "gpsimd", "sync")
STQ = "gpsimd"

D = 2048
S = 2048
NT = 16
KC = 16
DFF = 5632
NFC = 44
ISQ = 1.0 / math.sqrt(128.0)
NEG = -1.0e6
TABW = 1536
ARENA_N = 102 * 1024


class Tile:
    __slots__ = ("name", "w", "r")

    def __init__(self, name=""):
        self.name = name
        self.w = None
        self.r = []


class Prog:
    def __init__(self, nc, n_dma_sems=32):
        self.nc = nc
        self.allow_silent = True
        self.pe_selfwait = False
        self.ops = {e: [] for e in ENGS}
        self.count = {e: 0 for e in ENGS}
        self.waited = {e: {} for e in ENGS}
        self.n_dma_sems = n_dma_sems
        self.dma_cnt = [0] * n_dma_sems
        self.dma_rr = 0
        self.sems = {}

    def _deps(self, eng, reads, writes):
        need = {}

        def add(tok):
            if tok is None:
                return
            k, v = tok
            if need.get(k, 0) < v:
                need[k] = v

        for t in reads:
            add(t.w)
        for t in writes:
            add(t.w)
            for tok in t.r:
                add(tok)
        waits = []
        wd = self.waited[eng]
        for k, v in need.items():
            if wd.get(k, 0) >= v:
                continue
            if k == eng and v > self.count[eng]:
                continue
            if k == eng and eng == "tensor" and not self.pe_selfwait:
                continue
            wd[k] = v
            waits.append((k, v))
        return waits

    def _mark(self, tok, reads, writes):
        for t in reads:
            t.r.append(tok)
            if len(t.r) > 48:
                m = {}
                for k, v in t.r:
                    if m.get(k, 0) < v:
                        m[k] = v
                t.r = list(m.items())
        for t in writes:
            t.w = tok
            t.r = []

    def op(self, eng, fn, reads=(), writes=(), silent=False):
        waits = self._deps(eng, reads, writes)
        if silent and self.allow_silent:
            tok = (eng, self.count[eng] + 1)
            self.ops[eng].append((waits, fn, None))
        else:
            self.count[eng] += 1
            tok = (eng, self.count[eng])
            self.ops[eng].append((waits, fn, (eng, 1)))
        self._mark(tok, reads, writes)
        return tok

    def dma(self, eng, fn, reads=(), writes=()):
        s = self.dma_rr
        self.dma_rr = (self.dma_rr + 1) % self.n_dma_sems
        key = ("dma", s)
        waits = self._deps(eng, reads, writes)
        prev = self.dma_cnt[s]
        if prev > 0 and self.waited[eng].get(key, 0) < prev:
            self.waited[eng][key] = prev
            waits.append((key, prev))
        self.dma_cnt[s] += 16
        tok = (key, self.dma_cnt[s])
        self.ops[eng].append((waits, fn, (key, 16)))
        self._mark(tok, reads, writes)
        return tok

    def barrier(self):
        toks = [(e, self.count[e]) for e in ENGS if self.count[e] > 0]
        toks += [(("dma", s), v) for s, v in enumerate(self.dma_cnt) if v > 0]
        for e in ENGS:
            waits = []
            for k, v in toks:
                if k == e:
                    continue
                if self.waited[e].get(k, 0) < v:
                    self.waited[e][k] = v
                    waits.append((k, v))
            if waits:
                self.ops[e].append((waits, None, None))

    def final_wait(self, eng="sync"):
        toks = [(e, self.count[e]) for e in ENGS if self.count[e] > 0 and e != eng]
        toks += [(("dma", s), v) for s, v in enumerate(self.dma_cnt) if v > 0]
        waits = []
        for k, v in toks:
            if self.waited[eng].get(k, 0) < v:
                self.waited[eng][k] = v
                waits.append((k, v))
        self.ops[eng].append((waits, None, None))

    def emit(self, es):
        nc = self.nc
        for e in ENGS:
            self.sems[e] = es.enter_context(nc.semaphore("s_" + e))
        for s in range(self.n_dma_sems):
            self.sems[("dma", s)] = es.enter_context(nc.semaphore("s_dma%d" % s))
        block = es.enter_context(nc.Block())
        sems = self.sems

        def run(engname):
            def body(e):
                for waits, fn, inc in self.ops[engname]:
                    for k, v in waits:
                        e.wait_ge(sems[k], v)
                    if fn is None:
                        continue
                    ins = fn(e)
                    if inc is not None:
                        ins.then_inc(sems[inc[0]], inc[1])
            return body

        block.sync(run("sync"))
        block.tensor(run("tensor"))
        block.vector(run("vector"))
        block.scalar(run("scalar"))
        block.gpsimd(run("gpsimd"))


class Arena:
    def __init__(self, base_ap_bf16, nelem_bf16):
        self.base = base_ap_bf16
        self.n = nelem_bf16
        self.top = 0
        self.stack = []
        self.peak = 0

    def push(self):
        self.stack.append(self.top)

    def pop(self):
        self.top = self.stack.pop()

    def alloc(self, nelem, dtype):
        w = 2 if dtype == F32 else 1
        n16 = (nelem * w + 31) // 32 * 32
        if self.top + n16 > self.n:
            raise MemoryError("SBUF arena overflow: need %d have %d" % (self.top + n16, self.n))
        ap = self.base[:, self.top:self.top + nelem * w]
        self.top += n16
        self.peak = max(self.peak, self.top)
        if dtype != BF16:
            ap = ap.bitcast(dtype)
        return ap


def _toeplitz(fn):
    j = np.arange(128)[:, None]
    c = np.arange(TABW)[None, :]
    dist = c - 511 - j
    valid = fn(dist)
    return np.where(valid, -dist.astype(np.float32), np.float32(NEG)).astype(np.float32)


def make_consts():
    c = {}
    c["c_ident"] = np.eye(128, dtype=np.float32)
    tabA = np.stack([
        _toeplitz(lambda d: (d >= 0) & (d <= 128)),
        _toeplitz(lambda d: (d >= 0) & (d <= 512) & (d % 4 == 0)),
        _toeplitz(lambda d: (d >= 0) & (d % 16 == 0)),
    ], 0)
    c["c_tabA"] = np.ascontiguousarray(tabA.transpose(1, 0, 2))
    tabB = np.stack([
        _toeplitz(lambda d: (d >= 0)),
        _toeplitz(lambda d: (d >= 0) & (d <= 511)),
    ], 0)
    c["c_tabB"] = np.ascontiguousarray(tabB.transpose(1, 0, 2))
    n = np.arange(128)[:, None]
    t = np.arange(S)[None, :]
    cend = 16 * n + 31
    dc = t - cend
    cm = np.where((dc >= 0) & (n < 127), -dc.astype(np.float32), np.float32(NEG)).astype(np.float32)
    c["c_cmp"] = np.ascontiguousarray(cm)
    cs = np.arange(127) * 16
    ss = np.arange(32) * 64
    ov = np.clip(np.minimum(cs[:, None] + 32, ss[None, :] + 64) - np.maximum(cs[:, None], ss[None, :]), 0, None) / 32.0
    ovp = np.zeros((128, 32), np.float32)
    ovp[:127] = ov
    c["c_ov"] = ovp
    E = np.zeros((128, 16, 128), np.float32)
    for kt in range(16):
        for s_ in range(128):
            E[2 * kt + s_ // 64, kt, s_] = 1.0
    c["c_E"] = E
    keep = np.zeros((128, 16, 32), np.float32)
    force = np.zeros((128, 16, 32), np.float32)
    for tt in range(16):
        for p in range(128):
            cur = (tt * 128 + p) // 64
            for jb in range(32):
                if jb == 0 or jb == cur or jb == cur - 1:
                    force[p, tt, jb] = 1.0e4 + (32 - jb)
                elif jb > cur:
                    force[p, tt, jb] = -1.0e4 - jb
                else:
                    keep[p, tt, jb] = 1.0
    c["c_keep"] = keep
    c["c_force"] = force
    oh = np.zeros((128, 36, 128), np.float32)
    for r in range(36):
        oh[r, r, :] = 1.0
    c["c_onehot"] = oh
    return c


class KB:
    def __init__(self, nc, es):
        self.nc = nc
        self.P = Prog(nc)
        arena_t = es.enter_context(nc.sbuf_tensor("arena", [128, ARENA_N], BF16))
        self.ar = Arena(arena_t[:, :], ARENA_N)
        self.ps = es.enter_context(nc.psum_tensor("ps", [128, 8, 512], F32))
        self.PB = [Tile("pb%d" % i) for i in range(8)]

    def V(self, fn, r=(), w=()):
        return self.P.op("vector", fn, r, w)

    def A(self, fn, r=(), w=()):
        return self.P.op("scalar", fn, r, w)

    def T(self, fn, r=(), w=(), silent=False):
        return self.P.op("tensor", fn, r, w, silent=silent)

    def G(self, fn, r=(), w=()):
        return self.P.op("gpsimd", fn, r, w)

    def DMA(self, fn, r=(), w=(), q="sync"):
        return self.P.dma(q, fn, r, w)

    def psb(self, b):
        return self.ps[:, b, :]

    def setup_consts(self, cin):
        ar = self.ar
        self.identf = ar.alloc(128, F32)
        self.Tidf = Tile("identf")
        self.identb = ar.alloc(128, BF16)
        self.Tidb = Tile("identb")
        self.onesb = ar.alloc(128, BF16)
        self.Tones = Tile("ones")
        self.DMA(lambda e: e.dma_start(out=self.identf, in_=cin["c_ident"][:, :]), w=[self.Tidf])
        self.V(lambda e: e.tensor_copy(out=self.identb, in_=self.identf), r=[self.Tidf], w=[self.Tidb])
        self.V(lambda e: e.memset(self.onesb, 1.0), w=[self.Tones])
        self.stage = [ar.alloc(2048, F32) for _ in range(3)]
        self.Tstage = [Tile("stage%d" % i) for i in range(3)]
        self.stage_i = 0
        self.cast_i = 0

    def wload(self, dram_ap, dst_ap, Tdst, k=None, n=None, cast="alt"):
        s = self.stage_i % 3
        self.stage_i += 1
        st = self.stage[s]
        if k is not None:
            st = st[:, 0:k * n].rearrange("p (k n) -> p k n", k=k)
        elif n is not None:
            st = st[:, 0:n]
        Ts = self.Tstage[s]
        self.DMA(lambda e: e.dma_start(out=st, in_=dram_ap), w=[Ts])
        if cast == "alt":
            cast = "gpsimd" if (self.cast_i % 2 == 0) else "vector"
            self.cast_i += 1
        self.P.op(cast, lambda e: e.tensor_copy(out=dst_ap, in_=st), [Ts], [Tdst])

    def rstd_of(self, src_ap, Tsrc, junk, Tjunk, ss, Tss):
        self.A(lambda e: e.activation(out=junk, in_=src_ap, func=AF.Square, accum_out=ss), r=[Tsrc], w=[Tjunk, Tss])
        self.A(lambda e: e.activation(out=ss, in_=ss, func=AF.Sqrt, scale=1.0 / D, bias=1e-6), r=[Tss], w=[Tss])
        self.V(lambda e: e.reciprocal(out=ss, in_=ss), r=[Tss], w=[Tss])

    def normT(self, src_fn, src_tiles, g_row_ap, hT3, ThT, ntt=NT):
        ar = self.ar
        ar.push()
        gt = ar.alloc(D, F32)
        Tg = Tile("g")
        self.DMA(lambda e: e.dma_start(out=gt, in_=g_row_ap.partition_broadcast(128)), w=[Tg])
        NB = 4 if ntt > 2 else 2
        xt = [ar.alloc(D, F32) for _ in range(NB)]
        Tx = [Tile("xt%d" % i) for i in range(NB)]
        hb = [ar.alloc(D, BF16) for _ in range(NB)]
        Th = [Tile("hb%d" % i) for i in range(NB)]
        junk = ar.alloc(D, BF16)
        Tj = Tile("junk")
        ss = [ar.alloc(1, F32) for _ in range(NB)]
        Tss = [Tile("ss%d" % i) for i in range(NB)]
        for tt in range(ntt):
            b = tt % NB
            src = src_fn(tt)
            self.DMA(lambda e, b=b, src=src: e.dma_start(out=xt[b], in_=src), r=[src_tiles[tt]], w=[Tx[b]])
            self.rstd_of(xt[b], Tx[b], junk, Tj, ss[b], Tss[b])
            self.V(lambda e, b=b: e.scalar_tensor_tensor(out=hb[b], in0=xt[b], scalar=ss[b], in1=gt, op0=ALU.mult, op1=ALU.mult),
                   r=[Tx[b], Tss[b], Tg], w=[Th[b]])
            pv = self.ps[:, 2 * b:2 * b + 2, :].bitcast(BF16).rearrange("p a b -> p (a b)")
            for k in range(KC):
                self.T(lambda e, b=b, k=k, pv=pv: e.transpose(out=pv[:, k * 128:(k + 1) * 128], in_=hb[b][:, k * 128:(k + 1) * 128], identity=self.identb),
                       r=[Th[b], self.Tidb], w=[self.PB[2 * b], self.PB[2 * b + 1]], silent=(k != KC - 1))
            self.A(lambda e, tt=tt, pv=pv: e.copy(out=hT3[:, :, tt * 128:(tt + 1) * 128], in_=pv.rearrange("p (k t) -> p k t", k=KC)),
                   r=[self.PB[2 * b], self.PB[2 * b + 1]], w=[ThT[tt]])
        ar.pop()
        self.P.barrier()

    def proj_fm(self, w3, Tw, hT3, ThT, out_ap, Tout, scale, ntok=S, banks=(4, 5)):
        nblk = (ntok + 511) // 512
        for tb in range(nblk):
            n = min(512, ntok - tb * 512)
            bk = banks[tb % 2]
            tts = list(range(tb * 4, tb * 4 + (n + 127) // 128))
            for k in range(KC):
                self.T(lambda e, k=k, tb=tb, n=n, bk=bk: e.matmul(self.ps[:, bk, 0:n], lhsT=w3[:, k, :], rhs=hT3[:, k, tb * 512:tb * 512 + n], start=(k == 0), stop=(k == KC - 1)),
                       r=[Tw] + [ThT[t] for t in tts], w=[self.PB[bk]], silent=(k != KC - 1))
            self.A(lambda e, tb=tb, n=n, bk=bk: e.mul(out_ap[:, tb * 512:tb * 512 + n], self.ps[:, bk, 0:n], scale), r=[self.PB[bk]], w=[Tout])

    def proj_tm(self, w3, Tw, hT3, ThT, out3, Tout, ntt=NT, banks=(6, 7), ncol=128):
        ngrp = (ntt + 3) // 4
        for g4 in range(ngrp):
            bk = banks[g4 % 2]
            cnt = min(4, ntt - g4 * 4)
            for i in range(cnt):
                tt = g4 * 4 + i
                for k in range(KC):
                    self.T(lambda e, k=k, tt=tt, i=i, bk=bk: e.matmul(self.ps[:, bk, i * ncol:(i + 1) * ncol], lhsT=hT3[:, k, tt * 128:(tt + 1) * 128], rhs=w3[:, k, 0:ncol], start=(k == 0), stop=(k == KC - 1)),
                           r=[Tw, ThT[tt]], w=[self.PB[bk]], silent=(k != KC - 1))
            self.V(lambda e, g4=g4, cnt=cnt, bk=bk: e.tensor_copy(out=out3[:, g4 * 4:g4 * 4 + cnt, :], in_=self.ps[:, bk, 0:cnt * ncol].rearrange("p (a d) -> p a d", a=cnt)),
                   r=[self.PB[bk]], w=[Tout])

    def attn_bufs(self):
        ar = self.ar
        self.sb = [ar.alloc(512, F32) for _ in range(4)]
        self.Tsb = [Tile("sb%d" % i) for i in range(4)]
        self.pt = [ar.alloc(512, BF16) for _ in range(4)]
        self.Tpt = [Tile("pt%d" % i) for i in range(4)]
        self.rec = ar.alloc(512, F32)
        self.Trec = Tile("rec")
        self.osb = [ar.alloc(512, BF16) for _ in range(2)]
        self.Tosb = [Tile("osb%d" % i) for i in range(2)]
        self.osb_i = 0

    def attn_units(self, units, q_ap, Tq, imp=None):
        nu = len(units)
        OB, DB = 2, 3

        def s_stage(i):
            u = units[i]
            b = i % 2
            pen = u.get("pen")
            self.T(lambda e, u=u, b=b: e.matmul(self.ps[:, b, :], lhsT=u["kT"], rhs=q_ap, start=True, stop=(u.get("pen") is None)),
                   r=list(u["rd"]) + [Tq], w=[self.PB[b]])
            if pen is not None:
                self.T(lambda e, pen=pen, b=b: e.matmul(self.ps[:, b, :], lhsT=pen[0], rhs=pen[1], start=False, stop=True),
                       r=list(pen[2]), w=[self.PB[b]])

        def mid(i):
            u = units[i]
            b = i % 2
            bias = float(u.get("bias", 0.0))
            if u.get("tab") is not None:
                self.V(lambda e, u=u, b=b: e.scalar_tensor_tensor(out=self.sb[b], in0=u["tab"], scalar=float(u["slope"]), in1=self.ps[:, b, :], op0=ALU.mult, op1=ALU.add),
                       r=[self.PB[b]] + list(u.get("tabrd", [])), w=[self.Tsb[b]])
                src, rd = self.sb[b], [self.Tsb[b]]
            else:
                src, rd = self.ps[:, b, :], [self.PB[b]]
            if bias != 0.0:
                self.A(lambda e, b=b, src=src, bias=bias: e.activation(out=self.pt[b], in_=src, func=AF.Exp, bias=bias), r=rd, w=[self.Tpt[b]])
            else:
                self.A(lambda e, b=b, src=src: e.activation(out=self.pt[b], in_=src, func=AF.Exp), r=rd, w=[self.Tpt[b]])

        def pv(i):
            u = units[i]
            b = i % 2
            first = (i == 0)
            last = (i == nu - 1)
            self.T(lambda e, u=u, b=b, first=first, last=last: e.matmul(self.ps[:, OB, :], lhsT=u["V"], rhs=self.pt[b], start=first, stop=last),
                   r=list(u["rd"]) + [self.Tpt[b]], w=[self.PB[OB]])
            self.T(lambda e, b=b, first=first, last=last: e.matmul(self.ps[:, DB, :], lhsT=self.onesb, rhs=self.pt[b], start=first, stop=last),
                   r=[self.Tones, self.Tpt[b]], w=[self.PB[DB]])
            if imp is not None:
                self.T(lambda e, b=b, first=first, last=last: e.matmul(self.ps[0:32, 7, :], lhsT=imp[0], rhs=self.pt[b], start=first, stop=last),
                       r=[imp[1], self.Tpt[b]], w=[self.PB[7]])

        s_stage(0)
        for i in range(nu):
            if i + 1 < nu:
                s_stage(i + 1)
            mid(i)
            pv(i)

    def recip_den(self):
        rec = self.rec
        self.V(lambda e: e.tensor_scalar(out=rec, in0=self.ps[:, 3, :], scalar1=1e-30, scalar2=None, op0=ALU.max), r=[self.PB[3]], w=[self.Trec])
        self.V(lambda e: e.reciprocal(out=rec, in_=rec), r=[self.Trec], w=[self.Trec])

    def finalize_plain(self, dst_dram, Tdst):
        self.recip_den()
        ob = self.osb_i % 2
        self.osb_i += 1
        rec = self.rec
        osb = self.osb[ob]
        self.V(lambda e: e.tensor_tensor(out=osb, in0=self.ps[:, 2, :], in1=rec, op=ALU.mult), r=[self.PB[2], self.Trec], w=[self.Tosb[ob]])
        self.DMA(lambda e: e.dma_start(out=dst_dram, in_=osb.rearrange("p (t c) -> p t c", t=4)), r=[self.Tosb[ob]], w=[Tdst], q=STQ)

    def mem_kv(self, mem_ap, Tmem, g_row, wkv, hT3mem, ThTm, kTm, TkTm, Vm, TVm, wbuf, Twbuf):
        self.normT(lambda tt: mem_ap[tt * 128:(tt + 1) * 128, :], [Tmem, Tmem], g_row, hT3mem, ThTm, ntt=2)
        wcols = wkv.rearrange("(k p) n -> p k n", p=128)
        for h in range(4):
            b = h % 2
            self.wload(wcols[:, :, h * 128:(h + 1) * 128], wbuf[b], Twbuf[b], k=KC, n=128)
            self.proj_fm(wbuf[b], Twbuf[b], hT3mem, ThTm, kTm[:, h, :], TkTm, 1.0, ntok=256)
        for h in range(4):
            b = h % 2
            self.wload(wcols[:, :, 512 + h * 128:512 + (h + 1) * 128], wbuf[b], Twbuf[b], k=KC, n=128)
            self.proj_tm(wbuf[b], Twbuf[b], hT3mem, ThTm, Vm[:, h, :, :], TVm, ntt=2)

    def resid_update(self, y_ap, Ty, x_src_ap, Tsrc, x_dst_ap, Tdst, gt, Tg, bufs):
        xt, Tx, tmp, Ttmp, junk, Tj, ss, Tss = bufs
        self.DMA(lambda e: e.dma_start(out=xt, in_=x_src_ap), r=[Tsrc], w=[Tx])
        self.rstd_of(y_ap, Ty, junk, Tj, ss, Tss)
        self.V(lambda e: e.scalar_tensor_tensor(out=tmp, in0=y_ap, scalar=ss, in1=gt, op0=ALU.mult, op1=ALU.mult), r=[Ty, Tss, Tg], w=[Ttmp])
        self.V(lambda e: e.tensor_tensor(out=xt, in0=xt, in1=tmp, op=ALU.add), r=[Ttmp, Tx], w=[Tx])
        self.DMA(lambda e: e.dma_start(out=x_dst_ap, in_=xt), r=[Tx], w=[Tdst], q=STQ)


def alibi(n, i):
    return float(2.0 ** (-8.0 * (i + 1) / n))


def build(n_layers=4, stop_after_mixer=False, l_start=0, dbg=False, decl=None, allow_silent=True, pe_selfwait=False, rep=1, dummy=(), seq=None):
    nc = bass.Bass("TRN2", target_bir_lowering=False)

    def din(name, shape):
        if decl is not None and name not in decl:
            return nc.dram_tensor(name, [1] * len(shape), F32).ap()
        return nc.dram_tensor(name, list(shape), F32, kind="ExternalInput").ap()

    x_in = din("x", [S, D])
    mem = din("mem", [256, D])
    norm_g = din("norm_g", [4, 5, D])
    a_w_in = din("a_w_in", [2, D, 9728])
    a_w_out = din("a_w_out", [2, 1536, D])
    b_w_in = din("b_w_in", [2, D, 2084])
    b_w_out = din("b_w_out", [2, 2048, D])
    mem_w_kv = din("mem_w_kv", [4, D, 1024])
    ffn_w_gu = din("ffn_w_gu", [4, D, 2 * DFF])
    ffn_w_down = din("ffn_w_down", [4, DFF, D])
    kv_norm_g = din("kv_norm_g", [1, D])
    kv_w = din("kv_w", [D, 3072])
    cmp_pe = din("cmp_pe", [2, 32, 128])
    cmp_wk1 = din("cmp_wk1", [4096, 512])
    cmp_wk2 = din("cmp_wk2", [512, 128])
    cmp_wv1 = din("cmp_wv1", [4096, 512])
    cmp_wv2 = din("cmp_wv2", [512, 128])
    cshapes = {"c_ident": [128, 128], "c_tabA": [128, 3, TABW], "c_tabB": [128, 2, TABW], "c_cmp": [128, S],
               "c_ov": [128, 32], "c_E": [128, 16, 128], "c_keep": [128, 16, 32], "c_force": [128, 16, 32],
               "c_onehot": [128, 36, 128]}
    cin = {k: din(k, v) for k, v in cshapes.items()}
    out = nc.dram_tensor("out", [S, D], F32, kind="ExternalOutput").ap()
    oT_d = nc.dram_tensor("oT_d", [NT, 128, 16, 128], BF16).ap()
    qT_d = nc.dram_tensor("qT_d", [16, 128, S], BF16).ap()
    actT_d = nc.dram_tensor("actT_d", [NT, 128, NFC, 128], BF16).ap()
    y_d = nc.dram_tensor("y_d", [S, D], F32).ap()
    sh_d = nc.dram_tensor("sh_d", [4, 4, 128, S], BF16).ap()
    kc_d = nc.dram_tensor("kc_d", [2, 128, 4, 128], BF16).ap()

    dbg_t = {}
    if dbg:
        for l_ in range(4):
            for nm in ("xm", "x"):
                dbg_t[nm + str(l_)] = nc.dram_tensor("dbg_" + nm + str(l_), [S, D], F32, kind="ExternalOutput").ap()

    with ExitStack() as es:
        kb = KB(nc, es)
        P = kb.P
        P.allow_silent = allow_silent
        P.pe_selfwait = pe_selfwait
        ar = kb.ar
        ps = kb.ps
        PB = kb.PB
        kb.setup_consts(cin)

        Tmem = Tile("mem")
        Tin = Tile("x_in")
        Tout = [Tile("out%d" % i) for i in range(NT)]
        ToT = [Tile("oT%d" % i) for i in range(16)]
        TqT = [Tile("qT%d" % i) for i in range(16)]
        Tact = [Tile("act%d" % i) for i in range(NFC)]
        Ty = [Tile("y%d" % i) for i in range(NT)]
        Tsh = [Tile("sh%d" % i) for i in range(4)]
        Tkc = Tile("kc")

        state = {"first": True}

        def xsrc(tt):
            if state["first"]:
                return x_in[tt * 128:(tt + 1) * 128, :], Tin
            return out[tt * 128:(tt + 1) * 128, :], Tout[tt]

        def w_out_phase(w_out_l, nheads, g_row):
            P.barrier()
            ar.push()
            wo = ar.alloc(nheads * D, BF16).rearrange("p (h n) -> p h n", h=nheads)
            Two = [Tile("wo%d" % i) for i in range(nheads)]
            for h in range(nheads):
                kb.wload(w_out_l[h * 128:(h + 1) * 128, :], wo[:, h, :], Two[h], n=D)
            gt = ar.alloc(D, F32)
            Tg = Tile("g")
            kb.DMA(lambda e: e.dma_start(out=gt, in_=g_row.partition_broadcast(128)), w=[Tg])
            ot = [ar.alloc(nheads * 128, BF16).rearrange("p (h t) -> p h t", h=nheads) for _ in range(4)]
            Tot = [Tile("ot%d" % i) for i in range(4)]
            xts = [ar.alloc(D, F32) for _ in range(4)]
            Txs = [Tile("xt%d" % i) for i in range(4)]
            sss = [ar.alloc(1, F32) for _ in range(4)]
            Tsss = [Tile("ss%d" % i) for i in range(4)]
            tmp = ar.alloc(D, F32)
            Ttmp = Tile("tmp")
            junk = ar.alloc(D, BF16)
            Tj = Tile("junk")
            g4 = gt.rearrange("p (a b) -> p a b", a=4)
            t4 = tmp.rearrange("p (a b) -> p a b", a=4)
            j4 = junk.rearrange("p (a b) -> p a b", a=4)
            for tt in range(NT):
                b = tt % 4
                pb0 = 4 * (tt % 2)
                kb.DMA(lambda e, b=b, tt=tt: e.dma_start(out=ot[b], in_=oT_d[tt, :, 0:nheads, :]), r=ToT[:nheads], w=[Tot[b]])
                for c4 in range(4):
                    for h in range(nheads):
                        kb.T(lambda e, b=b, c4=c4, h=h, pb0=pb0: e.matmul(ps[:, pb0 + c4, :], lhsT=ot[b][:, h, :], rhs=wo[:, h, c4 * 512:(c4 + 1) * 512], start=(h == 0), stop=(h == nheads - 1)),
                             r=[Tot[b], Two[h]], w=[PB[pb0 + c4]], silent=(h != nheads - 1))
                src, Tsrc = xsrc(tt)
                xt, Tx, ss, Tss = xts[b], Txs[b], sss[b], Tsss[b]
                y_ap = ps[:, pb0:pb0 + 4, :]
                pbs = PB[pb0:pb0 + 4]
                kb.DMA(lambda e, xt=xt, src=src: e.dma_start(out=xt, in_=src), r=[Tsrc], w=[Tx])
                kb.A(lambda e, ss=ss, y_ap=y_ap: e.activation(out=j4, in_=y_ap, func=AF.Square, accum_out=ss), r=pbs, w=[Tj, Tss])
                kb.A(lambda e, ss=ss: e.activation(out=ss, in_=ss, func=AF.Sqrt, scale=1.0 / D, bias=1e-6), r=[Tss], w=[Tss])
                kb.V(lambda e, ss=ss: e.reciprocal(out=ss, in_=ss), r=[Tss], w=[Tss])
                kb.V(lambda e, ss=ss, y_ap=y_ap: e.scalar_tensor_tensor(out=t4, in0=y_ap, scalar=ss, in1=g4, op0=ALU.mult, op1=ALU.mult),
                     r=pbs + [Tss, Tg], w=[Ttmp])
                kb.V(lambda e, xt=xt: e.tensor_tensor(out=xt, in0=xt, in1=tmp, op=ALU.add), r=[Ttmp, Tx], w=[Tx])
                kb.DMA(lambda e, xt=xt, tt=tt: e.dma_start(out=out[tt * 128:(tt + 1) * 128, :], in_=xt), r=[Tx], w=[Tout[tt]], q=STQ)
            ar.pop()
            P.barrier()
            state["first"] = False

        def ffn_phase(l):
            P.barrier()
            ar.push()
            wd = [ar.alloc(NFC * 512, BF16).rearrange("p (f n) -> p f n", f=NFC), None]
            Twd = [[Tile("wd%d_%d" % (i, q)) for q in range(11)] for i in range(2)]
            wdv = ffn_w_down[l].rearrange("(f p) n -> p f n", p=128)

            def load_wd_unit(c4, q):
                b = c4 % 2
                kb.wload(wdv[:, q * 4:(q + 1) * 4, c4 * 512:(c4 + 1) * 512], wd[b][:, q * 4:(q + 1) * 4, :], Twd[b][q], k=4, n=512)

            ar.push()
            hT = ar.alloc(KC * S, BF16)
            hT3 = hT.rearrange("p (k t) -> p k t", k=KC)
            ThT = [Tile("hT%d" % i) for i in range(NT)]
            kb.normT(lambda tt: out[tt * 128:(tt + 1) * 128, :], Tout, norm_g[l, 2:3, :], hT3, ThT)
            wg = [ar.alloc(KC * 128, BF16).rearrange("p (k n) -> p k n", k=KC) for _ in range(2)]
            wu = [ar.alloc(KC * 128, BF16).rearrange("p (k n) -> p k n", k=KC) for _ in range(2)]
            Twg = [Tile("wg%d" % i) for i in range(2)]
            Twu = [Tile("wu%d" % i) for i in range(2)]
            sg = [ar.alloc(512, F32) for _ in range(2)]
            Tsg = [Tile("sg%d" % i) for i in range(2)]
            ao = [ar.alloc(512, BF16) for _ in range(2)]
            Tao = [Tile("ao%d" % i) for i in range(2)]
            wgu = ffn_w_gu[l].rearrange("(k p) n -> p k n", p=128)

            def load_fc(fc):
                b = fc % 2
                kb.wload(wgu[:, :, fc * 128:(fc + 1) * 128], wg[b], Twg[b], k=KC, n=128)
                kb.wload(wgu[:, :, DFF + fc * 128:DFF + (fc + 1) * 128], wu[b], Twu[b], k=KC, n=128)

            load_fc(0)
            it = 0
            for fc in range(NFC):
                if fc + 1 < NFC:
                    load_fc(fc + 1)
                if 30 <= fc < 41:
                    load_wd_unit(0, fc - 30)
                b = fc % 2
                for tb in range(4):
                    gb, ub = 4 + 2 * (it % 2), 5 + 2 * (it % 2)
                    i2 = it % 2
                    it += 1
                    tts = [ThT[t] for t in range(tb * 4, tb * 4 + 4)]
                    for k in range(KC):
                        kb.T(lambda e, k=k, b=b, tb=tb, gb=gb: e.matmul(ps[:, gb, :], lhsT=wg[b][:, k, :], rhs=hT3[:, k, tb * 512:(tb + 1) * 512], start=(k == 0), stop=(k == KC - 1)),
                             r=[Twg[b]] + tts, w=[PB[gb]], silent=(k != KC - 1))
                    for k in range(KC):
                        kb.T(lambda e, k=k, b=b, tb=tb, ub=ub: e.matmul(ps[:, ub, :], lhsT=wu[b][:, k, :], rhs=hT3[:, k, tb * 512:(tb + 1) * 512], start=(k == 0), stop=(k == KC - 1)),
                             r=[Twu[b]] + tts, w=[PB[ub]], silent=(k != KC - 1))
                    kb.A(lambda e, i2=i2, gb=gb: e.activation(out=sg[i2], in_=ps[:, gb, :], func=AF.Silu), r=[PB[gb]], w=[Tsg[i2]])
                    kb.V(lambda e, i2=i2, ub=ub: e.tensor_tensor(out=ao[i2], in0=ps[:, ub, :], in1=sg[i2], op=ALU.mult), r=[PB[ub], Tsg[i2]], w=[Tao[i2]])
                    kb.DMA(lambda e, i2=i2, fc=fc, tb=tb: e.dma_start(out=actT_d[tb * 4:(tb + 1) * 4, :, fc, :].rearrange("t p c -> p t c"), in_=ao[i2].rearrange("p (t c) -> p t c", t=4)),
                           r=[Tao[i2]], w=[Tact[fc]], q=STQ)
            ar.pop()
            P.barrier()
            wd[1] = ar.alloc(NFC * 512, BF16).rearrange("p (f n) -> p f n", f=NFC)
            at = [ar.alloc(NFC * 128, BF16).rearrange("p (f t) -> p f t", f=NFC) for _ in range(4)]
            Tat = [Tile("at%d" % i) for i in range(4)]
            yo = [ar.alloc(512, F32) for _ in range(2)]
            Tyo = [Tile("yo%d" % i) for i in range(2)]
            it = 0
            for c4 in range(4):
                b = c4 % 2
                for tt in range(NT):
                    if c4 + 1 < 4 and 1 <= tt < 12:
                        load_wd_unit(c4 + 1, tt - 1)
                    i2 = it % 2
                    i4 = it % 4
                    bk = 4 + (it % 2)
                    it += 1
                    kb.DMA(lambda e, i4=i4, tt=tt: e.dma_start(out=at[i4], in_=actT_d[tt]), r=Tact, w=[Tat[i4]])
                    for f in range(NFC):
                        kb.T(lambda e, f=f, i4=i4, b=b, bk=bk: e.matmul(ps[:, bk, :], lhsT=at[i4][:, f, :], rhs=wd[b][:, f, :], start=(f == 0), stop=(f == NFC - 1)),
                             r=[Tat[i4], Twd[b][f // 4]], w=[PB[bk]], silent=(f != NFC - 1))
                    kb.A(lambda e, i2=i2, bk=bk: e.copy(out=yo[i2], in_=ps[:, bk, :]), r=[PB[bk]], w=[Tyo[i2]])
                    kb.DMA(lambda e, i2=i2, tt=tt, c4=c4: e.dma_start(out=y_d[tt * 128:(tt + 1) * 128, c4 * 512:(c4 + 1) * 512], in_=yo[i2]), r=[Tyo[i2]], w=[Ty[tt]], q=STQ)
            ar.pop()
            P.barrier()
            ar.push()
            gt = ar.alloc(D, F32)
            Tg = Tile("g")
            kb.DMA(lambda e: e.dma_start(out=gt, in_=norm_g[l, 3:4, :].partition_broadcast(128)), w=[Tg])
            yt = [ar.alloc(D, F32) for _ in range(4)]
            Tyt = [Tile("yt%d" % i) for i in range(4)]
            xt = [ar.alloc(D, F32) for _ in range(4)]
            Tx = [Tile("xt%d" % i) for i in range(4)]
            ss = [ar.alloc(1, F32) for _ in range(4)]
            Tss = [Tile("ss%d" % i) for i in range(4)]
            tmp = ar.alloc(D, F32)
            Ttmp = Tile("tmp")
            junk = ar.alloc(D, BF16)
            Tj = Tile("junk")
            for tt in range(NT):
                b = tt % 4
                kb.DMA(lambda e, b=b, tt=tt: e.dma_start(out=yt[b], in_=y_d[tt * 128:(tt + 1) * 128, :]), r=[Ty[tt]], w=[Tyt[b]])
                kb.resid_update(yt[b], Tyt[b], out[tt * 128:(tt + 1) * 128, :], Tout[tt], out[tt * 128:(tt + 1) * 128, :], Tout[tt], gt, Tg,
                                (xt[b], Tx[b], tmp, Ttmp, junk, Tj, ss[b], Tss[b]))
            ar.pop()
            P.barrier()

        def layer_a(l):
            P.barrier()
            ar.push()
            hT = ar.alloc(KC * S, BF16)
            hT3 = hT.rearrange("p (k t) -> p k t", k=KC)
            ThT = [Tile("hT%d" % i) for i in range(NT)]
            wb = [ar.alloc(KC * 128, BF16).rearrange("p (k n) -> p k n", k=KC) for _ in range(3)]
            Twb = [Tile("wb%d" % i) for i in range(3)]
            hTm = ar.alloc(KC * 256, BF16).rearrange("p (k t) -> p k t", k=KC)
            ThTm = [Tile("hTm0"), Tile("hTm1")]
            kTm = ar.alloc(4 * 256, BF16).rearrange("p (h t) -> p h t", h=4)
            TkTm = Tile("kTm")
            Vm = ar.alloc(4 * 2 * 128, BF16).rearrange("p (h a d) -> p h a d", h=4, a=2)
            TVm = Tile("Vm")
            kb.mem_kv(mem, Tmem, norm_g[l, 4:5, :], mem_w_kv[l], hTm, ThTm, kTm, TkTm, Vm, TVm, wb, Twb)
            if state["first"]:
                kb.normT(lambda tt: x_in[tt * 128:(tt + 1) * 128, :], [Tin] * NT, norm_g[l, 0:1, :], hT3, ThT)
            else:
                kb.normT(lambda tt: out[tt * 128:(tt + 1) * 128, :], Tout, norm_g[l, 0:1, :], hT3, ThT)
            tab = ar.alloc(3 * TABW, F32).rearrange("p (g c) -> p g c", g=3)
            Ttab = Tile("tabA")
            kb.DMA(lambda e: e.dma_start(out=tab, in_=cin["c_tabA"]), w=[Ttab])
            kb.attn_bufs()
            qT = [ar.alloc(S, BF16) for _ in range(3)]
            kT = [ar.alloc(S, BF16) for _ in range(3)]
            Vt = [ar.alloc(NT * 128, BF16).rearrange("p (t d) -> p t d", t=NT) for _ in range(3)]
            Tq = [Tile("q%d" % i) for i in range(3)]
            Tk = [Tile("k%d" % i) for i in range(3)]
            Tv = [Tile("v%d" % i) for i in range(3)]
            win = a_w_in[l].rearrange("(k p) n -> p k n", p=128)
            for j in range(8):
                for g in range(3):
                    hq = g * 8 + j
                    kb.wload(win[:, :, hq * 128:(hq + 1) * 128], wb[0], Twb[0], k=KC, n=128)
                    kb.proj_fm(wb[0], Twb[0], hT3, ThT, qT[g], Tq[g], ISQ)
                    kb.wload(win[:, :, (24 + hq) * 128:(24 + hq + 1) * 128], wb[1], Twb[1], k=KC, n=128)
                    kb.proj_fm(wb[1], Twb[1], hT3, ThT, kT[g], Tk[g], 1.0)
                    kb.wload(win[:, :, (48 + hq) * 128:(48 + hq + 1) * 128], wb[2], Twb[2], k=KC, n=128)
                    kb.proj_tm(wb[2], Twb[2], hT3, ThT, Vt[g], Tv[g])
                for tb in range(4):
                    first = True
                    for g in range(3):
                        slope = alibi(24, g * 8 + j)
                        lo = {0: 4 * tb - 1, 1: 4 * tb - 4, 2: 0}[g]
                        units = []
                        for kt in range(max(0, lo), 4 * tb + 4):
                            delta = 512 * tb - 128 * kt
                            de = min(delta, 128) if g == 2 else delta
                            bias = -slope * (delta - de)
                            units.append(dict(kT=kT[g][:, kt * 128:(kt + 1) * 128], V=Vt[g][:, kt, :], rd=[Tk[g], Tv[g]],
                                              tab=tab[:, g, de + 511:de + 511 + 512], tabrd=[Ttab], slope=slope, bias=bias))
                        kb._chain_first = first
                        run_units_chain(units, qT[g][:, tb * 512:(tb + 1) * 512], Tq[g], first, g == 2)
                        first = False
                    kb.finalize_plain(oT_d[tb * 4:(tb + 1) * 4, :, j, :].rearrange("t p c -> p t c"), ToT[j])
            for h in range(4):
                kb.wload(win[:, :, 9216 + h * 128:9216 + (h + 1) * 128], wb[0], Twb[0], k=KC, n=128)
                kb.proj_fm(wb[0], Twb[0], hT3, ThT, qT[0], Tq[0], ISQ)
                for tb in range(4):
                    units = [dict(kT=kTm[:, h, kt * 128:(kt + 1) * 128], V=Vm[:, h, kt, :], rd=[TkTm, TVm], tab=None) for kt in range(2)]
                    run_units_chain(units, qT[0][:, tb * 512:(tb + 1) * 512], Tq[0], True, True)
                    kb.finalize_plain(oT_d[tb * 4:(tb + 1) * 4, :, 8 + h, :].rearrange("t p c -> p t c"), ToT[8 + h])
            ar.pop()
            w_out_phase(a_w_out[l], 12, norm_g[l, 1:2, :])

        def run_units_chain(units, q_ap, Tq, first, last):
            nu = len(units)
            OB, DB = 2, 3
            kbx = kb
            sbL, ptL, onesL = kb.sb, kb.pt, kb.onesb
            SBK = (0, 1, 4, 5)
            LA = 3

            def s_stage(i):
                u = units[i]
                b = i % 4
                pen = u.get("pen")
                kbx.T(lambda e, u=u, b=b: e.matmul(ps[:, SBK[b], :], lhsT=u["kT"], rhs=q_ap, start=True, stop=(u.get("pen") is None)),
                      r=list(u["rd"]) + [Tq], w=[PB[SBK[b]]], silent=(pen is not None))
                if pen is not None:
                    kbx.T(lambda e, pen=pen, b=b: e.matmul(ps[:, SBK[b], :], lhsT=pen[0], rhs=pen[1], start=False, stop=True),
                          r=list(pen[2]), w=[PB[SBK[b]]])

            def mid(i):
                u = units[i]
                b = i % 4
                bias = float(u.get("bias", 0.0))
                if u.get("tab") is not None:
                    kbx.V(lambda e, u=u, b=b: e.scalar_tensor_tensor(out=sbL[b], in0=u["tab"], scalar=float(u["slope"]), in1=ps[:, SBK[b], :], op0=ALU.mult, op1=ALU.add),
                          r=[PB[SBK[b]]] + list(u.get("tabrd", [])), w=[kbx.Tsb[b]])
                    src, rd = sbL[b], [kbx.Tsb[b]]
                else:
                    src, rd = ps[:, SBK[b], :], [PB[SBK[b]]]
                if bias != 0.0:
                    kbx.A(lambda e, b=b, src=src, bias=bias: e.activation(out=ptL[b], in_=src, func=AF.Exp, bias=bias), r=rd, w=[kbx.Tpt[b]])
                else:
                    kbx.A(lambda e, b=b, src=src: e.activation(out=ptL[b], in_=src, func=AF.Exp), r=rd, w=[kbx.Tpt[b]])

            def pv(i):
                u = units[i]
                b = i % 4
                st = first and (i == 0)
                sp = last and (i == nu - 1)
                imp = u.get("imp")
                kbx.T(lambda e, u=u, b=b: e.matmul(ps[:, OB, :], lhsT=u["V"], rhs=ptL[b], start=st, stop=sp),
                      r=list(u["rd"]) + [kbx.Tpt[b]], w=[PB[OB]], silent=True)
                kbx.T(lambda e, b=b: e.matmul(ps[:, DB, :], lhsT=onesL, rhs=ptL[b], start=st, stop=sp),
                      r=[kbx.Tones, kbx.Tpt[b]], w=[PB[DB]], silent=(imp is not None))
                if imp is not None:
                    kbx.T(lambda e, b=b, imp=imp: e.matmul(ps[0:32, 7, :], lhsT=imp[0], rhs=ptL[b], start=st, stop=sp),
                          r=[imp[1], kbx.Tpt[b]], w=[PB[7]])

            for i in range(min(LA, nu)):
                s_stage(i)
            for i in range(nu):
                if i + LA < nu:
                    s_stage(i + LA)
                mid(i)
                pv(i)

        def shared_kv_phase():
            P.barrier()
            ar.push()
            hT = ar.alloc(KC * S, BF16)
            hT3 = hT.rearrange("p (k t) -> p k t", k=KC)
            ThT = [Tile("hT%d" % i) for i in range(NT)]
            if state["first"]:
                kb.normT(lambda tt: x_in[tt * 128:(tt + 1) * 128, :], [Tin] * NT, kv_norm_g[0:1, :], hT3, ThT)
            else:
                kb.normT(lambda tt: out[tt * 128:(tt + 1) * 128, :], Tout, kv_norm_g[0:1, :], hT3, ThT)
            wb = [ar.alloc(KC * 128, BF16).rearrange("p (k n) -> p k n", k=KC) for _ in range(2)]
            Twb = [Tile("wb%d" % i) for i in range(2)]
            tmpT = [ar.alloc(S, BF16) for _ in range(2)]
            Ttmp = [Tile("tmpT%d" % i) for i in range(2)]
            kvw = kv_w.rearrange("(k p) n -> p k n", p=128)
            it = 0
            for g in range(4):
                for which, slot, fm in ((2, 0, True), (3, 1, False), (4, 2, True), (5, 3, False)):
                    b = it % 2
                    it += 1
                    col = which * 512 + g * 128
                    kb.wload(kvw[:, :, col:col + 128], wb[b], Twb[b], k=KC, n=128)
                    if fm:
                        kb.proj_fm(wb[b], Twb[b], hT3, ThT, tmpT[b], Ttmp[b], 1.0)
                    else:
                        kb.proj_tm(wb[b], Twb[b], hT3, ThT, tmpT[b].rearrange("p (t d) -> p t d", t=NT), Ttmp[b])
                    kb.DMA(lambda e, b=b, g=g, slot=slot: e.dma_start(out=sh_d[g, slot], in_=tmpT[b]), r=[Ttmp[b]], w=[Tsh[g]], q=STQ)
            w1sb = ar.alloc(32 * 512, BF16).rearrange("p (l n) -> p l n", l=32)
            Tw1 = [Tile("w1_%d" % q) for q in range(8)]
            w2sb = ar.alloc(4 * 128, BF16).rearrange("p (c n) -> p c n", c=4)
            Tw2 = Tile("w2")
            pef = ar.alloc(128, F32)
            Tpef = Tile("pef")
            peT = ar.alloc(32, BF16)
            TpeT = Tile("peT")
            b1 = ar.alloc(4, F32)
            Tb1 = Tile("b1")
            hx = ar.alloc(128, F32)
            Thx = Tile("hx")
            x2 = ar.alloc(128, F32)
            Tx2 = Tile("x2")
            sgm = ar.alloc(128, F32)
            Tsgm = Tile("sgm")
            gel = ar.alloc(4 * 128, BF16).rearrange("p (c n) -> p c n", c=4)
            Tgel = Tile("gel")
            csb = ar.alloc(4 * 128, BF16).rearrange("p (g n) -> p g n", g=4)
            Tcsb = Tile("csb")
            for which, w1, w2 in ((0, cmp_wk1, cmp_wk2), (1, cmp_wv1, cmp_wv2)):
                w1v = w1.rearrange("(l p) n -> p l n", p=128)
                for q in range(8):
                    kb.wload(w1v[:, q * 4:(q + 1) * 4, :], w1sb[:, q * 4:(q + 1) * 4, :], Tw1[q], k=4, n=512)
                kb.wload(w2.rearrange("(c p) n -> p c n", p=128), w2sb, Tw2, k=4, n=128)
                kb.DMA(lambda e, which=which: e.dma_start(out=pef[0:32, :], in_=cmp_pe[which]), w=[Tpef])
                kb.T(lambda e: e.transpose(out=ps[:, 6, 0:32], in_=pef[0:32, :], identity=kb.identf[0:32, 0:32]), r=[Tpef, kb.Tidf], w=[PB[6]])
                kb.V(lambda e: e.tensor_copy(out=peT, in_=ps[:, 6, 0:32]), r=[PB[6]], w=[TpeT])
                for hc in range(4):
                    for l_ in range(32):
                        kb.T(lambda e, hc=hc, l_=l_: e.matmul(ps[:, 7, hc:hc + 1], lhsT=w1sb[:, l_, hc * 128:(hc + 1) * 128], rhs=peT[:, l_:l_ + 1], start=(l_ == 0), stop=(l_ == 31)),
                             r=[Tw1[l_ // 4], TpeT], w=[PB[7]], silent=(l_ != 31))
                kb.V(lambda e: e.tensor_copy(out=b1, in_=ps[:, 7, 0:4]), r=[PB[7]], w=[Tb1])
                kb.V(lambda e: e.memset(csb, 0.0), w=[Tcsb])
                for g in range(4):
                    b = it % 2
                    it += 1
                    col = which * 512 + g * 128
                    kb.wload(kvw[:, :, col:col + 128], wb[b], Twb[b], k=KC, n=128)
                    kb.proj_fm(wb[b], Twb[b], hT3, ThT, tmpT[b], Ttmp[b], 1.0)
                    kr3 = tmpT[b].rearrange("p (n s) -> p n s", s=16)
                    for hc in range(4):
                        bk = 4 + hc % 2
                        for l_ in range(32):
                            rhs = kr3[:, 0:127, l_] if l_ < 16 else kr3[:, 1:128, l_ - 16]
                            kb.T(lambda e, hc=hc, l_=l_, rhs=rhs, bk=bk: e.matmul(ps[:, bk, 0:127], lhsT=w1sb[:, l_, hc * 128:(hc + 1) * 128], rhs=rhs, start=(l_ == 0), stop=(l_ == 31)),
                                 r=[Tw1[l_ // 4], Ttmp[b]], w=[PB[bk]], silent=(l_ != 31))
                        kb.V(lambda e, hc=hc, bk=bk: e.tensor_scalar(out=hx[:, 0:127], in0=ps[:, bk, 0:127], scalar1=b1[:, hc:hc + 1], scalar2=None, op0=ALU.add), r=[PB[bk], Tb1], w=[Thx])
                        kb.V(lambda e: e.tensor_tensor(out=x2[:, 0:127], in0=hx[:, 0:127], in1=hx[:, 0:127], op=ALU.mult), r=[Thx], w=[Tx2])
                        kb.V(lambda e: e.tensor_scalar(out=x2[:, 0:127], in0=x2[:, 0:127], scalar1=0.044715, scalar2=1.0, op0=ALU.mult, op1=ALU.add), r=[Tx2], w=[Tx2])
                        kb.V(lambda e: e.tensor_tensor(out=x2[:, 0:127], in0=x2[:, 0:127], in1=hx[:, 0:127], op=ALU.mult), r=[Tx2, Thx], w=[Tx2])
                        kb.A(lambda e: e.activation(out=sgm[:, 0:127], in_=x2[:, 0:127], func=AF.Sigmoid, scale=1.5957691216057308), r=[Tx2], w=[Tsgm])
                        kb.V(lambda e, hc=hc: e.tensor_tensor(out=gel[:, hc, 0:127], in0=hx[:, 0:127], in1=sgm[:, 0:127], op=ALU.mult), r=[Thx, Tsgm], w=[Tgel])
                    if which == 0:
                        for hc in range(4):
                            kb.T(lambda e, hc=hc: e.matmul(ps[:, 6, 0:127], lhsT=w2sb[:, hc, :], rhs=gel[:, hc, 0:127], start=(hc == 0), stop=(hc == 3)), r=[Tw2, Tgel], w=[PB[6]], silent=(hc != 3))
                        kb.V(lambda e, g=g: e.tensor_copy(out=csb[:, g, 0:127], in_=ps[:, 6, 0:127]), r=[PB[6]], w=[Tcsb])
                    else:
                        for hc in range(4):
                            kb.T(lambda e, hc=hc: e.matmul(ps[0:127, 6, 0:128], lhsT=gel[:, hc, 0:127], rhs=w2sb[:, hc, :], start=(hc == 0), stop=(hc == 3)), r=[Tw2, Tgel], w=[PB[6]], silent=(hc != 3))
                        kb.V(lambda e, g=g: e.tensor_copy(out=csb[0:127, g, :], in_=ps[0:127, 6, 0:128]), r=[PB[6]], w=[Tcsb])
                kb.DMA(lambda e, which=which: e.dma_start(out=kc_d[which], in_=csb), r=[Tcsb], w=[Tkc], q=STQ)
            ar.pop()
            P.barrier()

        def layer_b(l):
            lb = l - 2
            P.barrier()
            ar.push()
            kTm = ar.alloc(4 * 256, BF16).rearrange("p (h t) -> p h t", h=4)
            TkTm = Tile("kTm")
            Vm = ar.alloc(4 * 2 * 128, BF16).rearrange("p (h a d) -> p h a d", h=4, a=2)
            TVm = Tile("Vm")
            ghi = ar.alloc(S, BF16)
            glo = ar.alloc(S, BF16)
            Tgh = Tile("ghi")
            Tgl = Tile("glo")
            win = b_w_in[lb].rearrange("(k p) n -> p k n", p=128)
            ar.push()
            hT = ar.alloc(KC * S, BF16)
            hT3 = hT.rearrange("p (k t) -> p k t", k=KC)
            ThT = [Tile("hT%d" % i) for i in range(NT)]
            wb = [ar.alloc(KC * 128, BF16).rearrange("p (k n) -> p k n", k=KC) for _ in range(2)]
            Twb = [Tile("wb%d" % i) for i in range(2)]
            hTm = ar.alloc(KC * 256, BF16).rearrange("p (k t) -> p k t", k=KC)
            ThTm = [Tile("hTm0"), Tile("hTm1")]
            kb.mem_kv(mem, Tmem, norm_g[l, 4:5, :], mem_w_kv[l], hTm, ThTm, kTm, TkTm, Vm, TVm, wb, Twb)
            if state["first"]:
                kb.normT(lambda tt: x_in[tt * 128:(tt + 1) * 128, :], [Tin] * NT, norm_g[l, 0:1, :], hT3, ThT)
            else:
                kb.normT(lambda tt: out[tt * 128:(tt + 1) * 128, :], Tout, norm_g[l, 0:1, :], hT3, ThT)
            qtmp = [ar.alloc(S, BF16) for _ in range(2)]
            Tqtmp = [Tile("qtmp%d" % i) for i in range(2)]
            for h in range(16):
                b = h % 2
                col = h * 128 if h < 12 else 1572 + (h - 12) * 128
                kb.wload(win[:, :, col:col + 128], wb[b], Twb[b], k=KC, n=128)
                kb.proj_fm(wb[b], Twb[b], hT3, ThT, qtmp[b], Tqtmp[b], ISQ)
                kb.DMA(lambda e, b=b, h=h: e.dma_start(out=qT_d[h], in_=qtmp[b]), r=[Tqtmp[b]], w=[TqT[h]], q=STQ)
            wgt = ar.alloc(KC * 36, BF16).rearrange("p (k n) -> p k n", k=KC)
            Twgt = Tile("wgt")
            kb.wload(win[:, :, 1536:1572], wgt, Twgt, k=KC, n=36)
            gtok = [ar.alloc(36, F32) for _ in range(2)]
            Tgtok = [Tile("gtok%d" % i) for i in range(2)]
            gTf = ar.alloc(S, F32)
            TgTf = Tile("gTf")
            kb.V(lambda e: e.memset(ghi, 0.0), w=[Tgh])
            kb.V(lambda e: e.memset(glo, 0.0), w=[Tgl])
            for tt in range(NT):
                b = tt % 2
                bk = 6 + b
                bk2 = 4 + b
                for k in range(KC):
                    kb.T(lambda e, k=k, tt=tt, bk=bk: e.matmul(ps[:, bk, 0:36], lhsT=hT3[:, k, tt * 128:(tt + 1) * 128], rhs=wgt[:, k, :], start=(k == 0), stop=(k == KC - 1)),
                         r=[Twgt, ThT[tt]], w=[PB[bk]], silent=(k != KC - 1))
                kb.A(lambda e, b=b, bk=bk: e.activation(out=gtok[b], in_=ps[:, bk, 0:36], func=AF.Sigmoid), r=[PB[bk]], w=[Tgtok[b]])
                kb.T(lambda e, b=b, bk2=bk2: e.transpose(out=ps[0:36, bk2, 0:128], in_=gtok[b], identity=kb.identf), r=[Tgtok[b], kb.Tidf], w=[PB[bk2]])
                kb.V(lambda e, tt=tt, bk2=bk2: e.tensor_copy(out=gTf[0:36, tt * 128:(tt + 1) * 128], in_=ps[0:36, bk2, 0:128]), r=[PB[bk2]], w=[TgTf])
            kb.V(lambda e: e.tensor_copy(out=ghi[0:36, :], in_=gTf[0:36, :]), r=[TgTf], w=[Tgh])
            kb.V(lambda e: e.tensor_tensor(out=gTf[0:36, :], in0=gTf[0:36, :], in1=ghi[0:36, :], op=ALU.subtract), r=[TgTf, Tgh], w=[TgTf])
            kb.V(lambda e: e.tensor_copy(out=glo[0:36, :], in_=gTf[0:36, :]), r=[TgTf], w=[Tgl])
            ar.pop()
            P.barrier()
            ar.push()
            tabB = ar.alloc(2 * TABW, F32).rearrange("p (g c) -> p g c", g=2)
            TtabB = Tile("tabB")
            kb.DMA(lambda e: e.dma_start(out=tabB, in_=cin["c_tabB"]), w=[TtabB])
            cmpT = ar.alloc(S, F32)
            Tcmp = Tile("cmpT")
            kb.DMA(lambda e: e.dma_start(out=cmpT, in_=cin["c_cmp"]), w=[Tcmp])
            keep = ar.alloc(16 * 32, F32).rearrange("p (t j) -> p t j", t=16)
            force = ar.alloc(16 * 32, F32).rearrange("p (t j) -> p t j", t=16)
            Tkeep = Tile("keep")
            Tforce = Tile("force")
            kb.DMA(lambda e: e.dma_start(out=keep, in_=cin["c_keep"]), w=[Tkeep])
            kb.DMA(lambda e: e.dma_start(out=force, in_=cin["c_force"]), w=[Tforce])
            ovf = ar.alloc(32, F32)
            Tovf = Tile("ovf")
            ov_b = ar.alloc(32, BF16)
            Tov = Tile("ov")
            kb.DMA(lambda e: e.dma_start(out=ovf, in_=cin["c_ov"]), w=[Tovf])
            kb.V(lambda e: e.tensor_copy(out=ov_b, in_=ovf), r=[Tovf], w=[Tov])
            E_b = ar.alloc(16 * 128, BF16).rearrange("p (k s) -> p k s", k=16)
            TE = Tile("E")
            kb.wload(cin["c_E"], E_b, TE, k=16, n=128)
            oh_b = ar.alloc(36 * 128, BF16).rearrange("p (r m) -> p r m", r=36)
            Toh = Tile("oh")
            for q in range(3):
                kb.wload(cin["c_onehot"][:, q * 12:(q + 1) * 12, :], oh_b[:, q * 12:(q + 1) * 12, :], Toh, k=12, n=128)
            kcs = ar.alloc(4 * 128, BF16).rearrange("p (g n) -> p g n", g=4)
            vcs = ar.alloc(4 * 128, BF16).rearrange("p (g n) -> p g n", g=4)
            Tkcs = Tile("kcs")
            kb.DMA(lambda e: e.dma_start(out=kcs, in_=kc_d[0]), r=[Tkc], w=[Tkcs])
            kb.DMA(lambda e: e.dma_start(out=vcs, in_=kc_d[1]), r=[Tkc], w=[Tkcs])
            kb.attn_bufs()
            shg = [ar.alloc(4 * S, BF16).rearrange("p (s t) -> p s t", s=4) for _ in range(2)]
            Tshg = [Tile("shg%d" % i) for i in range(2)]
            qb = [ar.alloc(S, BF16) for _ in range(3)]
            Tqb = [Tile("qb%d" % i) for i in range(3)]
            ocmp = [ar.alloc(S, F32) for _ in range(3)]
            Toc = [Tile("ocmp%d" % i) for i in range(3)]
            impT = ar.alloc(S, F32)
            Timp = Tile("impT")
            penT = ar.alloc(S, BF16)
            Tpen = Tile("penT")
            kb.V(lambda e: e.memset(penT, 0.0), w=[Tpen])
            rg = ar.alloc(512, F32)
            Trg = Tile("rg")
            tmpo = ar.alloc(512, F32)
            Ttmpo = Tile("tmpo")
            v1 = ar.alloc(32, F32)
            v2 = ar.alloc(32, F32)
            mxa = ar.alloc(8, F32)
            mxb = ar.alloc(8, F32)
            selp = ar.alloc(32, F32)
            Tv1, Tv2, Tmxa, Tmxb, Tselp = Tile("v1"), Tile("v2"), Tile("mxa"), Tile("mxb"), Tile("selp")

            def finalize_gated(h, branch, tb, dst, Tdst, mode, dram=None, Tdram=None):
                kb.recip_den()
                recL = kb.rec
                r_ = h * 3 + branch
                kb.T(lambda e: e.matmul(ps[:, 6, :], lhsT=oh_b[:, r_, :], rhs=ghi[:, tb * 512:(tb + 1) * 512], start=True, stop=False), r=[Toh, Tgh], w=[PB[6]], silent=True)
                kb.T(lambda e: e.matmul(ps[:, 6, :], lhsT=oh_b[:, r_, :], rhs=glo[:, tb * 512:(tb + 1) * 512], start=False, stop=True), r=[Toh, Tgl], w=[PB[6]])
                kb.V(lambda e: e.tensor_tensor(out=rg, in0=ps[:, 6, :], in1=recL, op=ALU.mult), r=[PB[6], kb.Trec], w=[Trg])
                if mode == "set":
                    kb.V(lambda e: e.tensor_tensor(out=dst, in0=ps[:, 2, :], in1=rg, op=ALU.mult), r=[PB[2], Trg], w=[Tdst])
                elif mode == "add":
                    kb.V(lambda e: e.tensor_tensor(out=tmpo, in0=ps[:, 2, :], in1=rg, op=ALU.mult), r=[PB[2], Trg], w=[Ttmpo])
                    kb.G(lambda e: e.tensor_tensor(out=dst, in0=dst, in1=tmpo, op=ALU.add), r=[Ttmpo, Tdst], w=[Tdst])
                else:
                    kb.V(lambda e: e.tensor_tensor(out=tmpo, in0=ps[:, 2, :], in1=rg, op=ALU.mult), r=[PB[2], Trg], w=[Ttmpo])
                    ob = kb.osb_i % 2
                    kb.osb_i += 1
                    osbL = kb.osb[ob]
                    kb.G(lambda e: e.tensor_tensor(out=osbL, in0=dst, in1=tmpo, op=ALU.add), r=[Ttmpo, Tdst], w=[kb.Tosb[ob]])
                    kb.DMA(lambda e: e.dma_start(out=dram, in_=osbL.rearrange("p (t c) -> p t c", t=4)), r=[kb.Tosb[ob]], w=[Tdram], q=STQ)

            for g in range(4):
                sb_ = g % 2
                kb.DMA(lambda e, g=g, sb_=sb_: e.dma_start(out=shg[sb_], in_=sh_d[g].rearrange("s p t -> p s t")), r=[Tsh[g]], w=[Tshg[sb_]])
                ksT = shg[sb_][:, 0, :]
                vs = shg[sb_][:, 1, :].rearrange("p (t d) -> p t d", t=NT)
                kwT = shg[sb_][:, 2, :]
                vw = shg[sb_][:, 3, :].rearrange("p (t d) -> p t d", t=NT)
                for hh in range(3):
                    h = 3 * g + hh
                    slope = alibi(12, h)
                    kb.DMA(lambda e, hh=hh, h=h: e.dma_start(out=qb[hh], in_=qT_d[h]), r=[TqT[h]], w=[Tqb[hh]])
                    for tb in range(4):
                        units = [dict(kT=kcs[:, g, :], V=vcs[:, g, :], rd=[Tkcs], tab=cmpT[:, tb * 512:(tb + 1) * 512], tabrd=[Tcmp], slope=slope, bias=0.0, imp=(ov_b, Tov))]
                        run_units_chain(units, qb[hh][:, tb * 512:(tb + 1) * 512], Tqb[hh], True, True)
                        finalize_gated(h, 0, tb, ocmp[hh][:, tb * 512:(tb + 1) * 512], Toc[hh], "set")
                        recI = kb.rec
                        if hh == 0:
                            kb.V(lambda e, tb=tb: e.tensor_tensor(out=impT[0:32, tb * 512:(tb + 1) * 512], in0=ps[0:32, 7, :], in1=recI[0:32, :], op=ALU.mult), r=[PB[7], kb.Trec], w=[Timp])
                        else:
                            kb.V(lambda e: e.tensor_tensor(out=tmpo[0:32, :], in0=ps[0:32, 7, :], in1=recI[0:32, :], op=ALU.mult), r=[PB[7], kb.Trec], w=[Ttmpo])
                            kb.V(lambda e, tb=tb: e.tensor_tensor(out=impT[0:32, tb * 512:(tb + 1) * 512], in0=impT[0:32, tb * 512:(tb + 1) * 512], in1=tmpo[0:32, :], op=ALU.add), r=[Ttmpo, Timp], w=[Timp])
                for tt in range(NT):
                    kb.T(lambda e, tt=tt: e.transpose(out=ps[:, 6, 0:32], in_=impT[0:32, tt * 128:(tt + 1) * 128], identity=kb.identf[0:32, 0:32]), r=[Timp, kb.Tidf], w=[PB[6]])
                    kb.V(lambda e, tt=tt: e.tensor_tensor(out=v1, in0=ps[:, 6, 0:32], in1=keep[:, tt, :], op=ALU.mult), r=[PB[6], Tkeep], w=[Tv1])
                    kb.V(lambda e, tt=tt: e.tensor_tensor(out=v1, in0=v1, in1=force[:, tt, :], op=ALU.add), r=[Tv1, Tforce], w=[Tv1])
                    kb.V(lambda e: e.max(out=mxa, in_=v1), r=[Tv1], w=[Tmxa])
                    kb.V(lambda e: e.match_replace(out=v2, in_to_replace=mxa, in_values=v1, imm_value=-1.0e9), r=[Tv1, Tmxa], w=[Tv2])
                    kb.V(lambda e: e.max(out=mxb, in_=v2), r=[Tv2], w=[Tmxb])
                    kb.V(lambda e: e.tensor_scalar(out=selp, in0=v1, scalar1=mxb[:, 7:8], scalar2=None, op0=ALU.is_ge), r=[Tv1, Tmxb], w=[Tselp])
                    kb.V(lambda e: e.tensor_scalar(out=selp, in0=selp, scalar1=1.0, scalar2=30000.0, op0=ALU.subtract, op1=ALU.mult), r=[Tselp], w=[Tselp])
                    kb.T(lambda e: e.transpose(out=ps[0:32, 7, 0:128], in_=selp, identity=kb.identf), r=[Tselp, kb.Tidf], w=[PB[7]])
                    kb.V(lambda e, tt=tt: e.tensor_copy(out=penT[0:32, tt * 128:(tt + 1) * 128], in_=ps[0:32, 7, 0:128]), r=[PB[7]], w=[Tpen])
                for hh in range(3):
                    h = 3 * g + hh
                    slope = alibi(12, h)
                    for tb in range(4):
                        q_ap = qb[hh][:, tb * 512:(tb + 1) * 512]
                        units = []
                        for kt in range(0, 4 * tb + 4):
                            delta = 512 * tb - 128 * kt
                            de = min(delta, 128)
                            units.append(dict(kT=ksT[:, kt * 128:(kt + 1) * 128], V=vs[:, kt, :], rd=[Tshg[sb_]], tab=tabB[:, 0, de + 511:de + 511 + 512], tabrd=[TtabB],
                                              slope=slope, bias=-slope * (delta - de), pen=(E_b[:, kt, :], penT[:, tb * 512:(tb + 1) * 512], [TE, Tpen])))
                        run_units_chain(units, q_ap, Tqb[hh], True, True)
                        finalize_gated(h, 1, tb, ocmp[hh][:, tb * 512:(tb + 1) * 512], Toc[hh], "add")
                        units = []
                        for kt in range(max(0, 4 * tb - 4), 4 * tb + 4):
                            delta = 512 * tb - 128 * kt
                            units.append(dict(kT=kwT[:, kt * 128:(kt + 1) * 128], V=vw[:, kt, :], rd=[Tshg[sb_]], tab=tabB[:, 1, delta + 511:delta + 511 + 512], tabrd=[TtabB],
                                              slope=slope, bias=0.0))
                        run_units_chain(units, q_ap, Tqb[hh], True, True)
                        finalize_gated(h, 2, tb, ocmp[hh][:, tb * 512:(tb + 1) * 512], Toc[hh], "final", dram=oT_d[tb * 4:(tb + 1) * 4, :, h, :].rearrange("t p c -> p t c"), Tdram=ToT[h])
            for h in range(4):
                kb.DMA(lambda e, h=h: e.dma_start(out=qb[0], in_=qT_d[12 + h]), r=[TqT[12 + h]], w=[Tqb[0]])
                for tb in range(4):
                    units = [dict(kT=kTm[:, h, kt * 128:(kt + 1) * 128], V=Vm[:, h, kt, :], rd=[TkTm, TVm], tab=None) for kt in range(2)]
                    run_units_chain(units, qb[0][:, tb * 512:(tb + 1) * 512], Tqb[0], True, True)
                    kb.finalize_plain(oT_d[tb * 4:(tb + 1) * 4, :, 12 + h, :].rearrange("t p c -> p t c"), ToT[12 + h])
            ar.pop()
            ar.pop()
            w_out_phase(b_w_out[lb], 16, norm_g[l, 1:2, :])

        if seq is not None:
            for ph in seq:
                if ph == "a0":
                    layer_a(0)
                elif ph == "skv":
                    shared_kv_phase()
                elif ph == "b2":
                    layer_b(2)
                elif ph == "f0":
                    ffn_phase(0)
                if dbg and ph == "a0":
                    P.barrier()
                    Td = Tile("dbg")
                    for q in range(4):
                        kb.DMA(lambda e, q=q: e.dma_start(out=dbg_t["xm0"][q * 512:(q + 1) * 512, :], in_=out[q * 512:(q + 1) * 512, :]), r=Tout, w=[Td])
                    P.barrier()
        for l in range(l_start, n_layers if seq is None else 0):
            if l < 2:
                for _r in range(rep):
                    layer_a(l)
            else:
                if l == 2 or l == l_start:
                    shared_kv_phase()
                layer_b(l)
            def snap(name):
                if not dbg:
                    return
                P.barrier()
                Td = Tile("dbg")
                for q in range(4):
                    kb.DMA(lambda e, q=q: e.dma_start(out=dbg_t[name][q * 512:(q + 1) * 512, :], in_=out[q * 512:(q + 1) * 512, :]), r=Tout, w=[Td])
                P.barrier()
            snap("xm%d" % l)
            if stop_after_mixer and l == n_layers - 1:
                break
            ffn_phase(l)
            snap("x%d" % l)

        if dummy:
            P.barrier()
            dz = ar.alloc(64, F32)
            Tdz = Tile("dz")
            dz8 = ar.alloc(8, F32)
            kb.V(lambda e: e.memset(dz, 0.5), w=[Tdz])
            if "sigmoid" in dummy:
                kb.A(lambda e: e.activation(out=dz, in_=dz, func=AF.Sigmoid, scale=1.5), r=[Tdz], w=[Tdz])
            if "max" in dummy:
                kb.V(lambda e: e.max(out=dz8, in_=dz[:, 0:32]), r=[Tdz], w=[Tdz])
                kb.V(lambda e: e.match_replace(out=dz[:, 32:64], in_to_replace=dz8, in_values=dz[:, 0:32], imm_value=-1.0e9), r=[Tdz], w=[Tdz])
            if "isge" in dummy:
                kb.V(lambda e: e.tensor_scalar(out=dz[:, 0:32], in0=dz[:, 0:32], scalar1=dz8[:, 7:8], scalar2=None, op0=ALU.is_ge), r=[Tdz], w=[Tdz])
        P.final_wait("sync")
        P.emit(es)
        print("arena peak bytes/partition:", ar.peak * 2, "ops:", {e: len(P.ops[e]) for e in ENGS}, "semcounts:", P.count, max(P.dma_cnt))
    return nc


CONSTS = None


def kernel(**inputs):
    global CONSTS
    if CONSTS is None:
        CONSTS = make_consts()
    nc = build()
    x = np.ascontiguousarray(inputs["x"], dtype=np.float32)
    shared = {k: np.ascontiguousarray(v, dtype=np.float32) for k, v in inputs.items() if k not in ("x", "mem")}
    shared["kv_norm_g"] = shared["kv_norm_g"].reshape(1, D)
    active = [0, 1, 4, 5]
    zeros = {k: np.zeros_like(v) for k, v in shared.items()}
    zx = np.zeros_like(x[0])
    zm = np.zeros((256, D), np.float32)
    in_maps = []
    for c in range(8):
        if c in active:
            b = active.index(c)
            m = dict(shared)
            m.update(CONSTS)
            m["x"] = x[b]
            m["mem"] = np.ascontiguousarray(inputs["mem"][b], dtype=np.float32)
        else:
            m = dict(zeros)
            m.update(CONSTS)
            m["x"] = zx
            m["mem"] = zm
        in_maps.append(m)
    res = run_bass_kernel_spmd(nc, in_maps, core_ids=list(range(8)))
    return np.stack([res.results[c]["out"] for c in active], 0).astype(np.float32)
```

```python
import math
from contextlib import ExitStack

import numpy as np
import concourse.bass as bass
import concourse.mybir as mybir
from concourse.bass_utils import run_bass_kernel_spmd

F32 = mybir.dt.float32
BF16 = mybir.dt.bfloat16
AF = mybir.ActivationFunctionType
ALU = mybir.AluOpType

ENGS = ("tensor", "vector", "scalar", "gpsimd", "sync")
STQ = "gpsimd"

D = 2048
S = 2048
NT = 16
KC = 16
DFF = 5632
NFC = 44
ISQ = 1.0 / math.sqrt(128.0)
NEG = -1.0e6
TABW = 1536
ARENA_N = 102 * 1024


class Tile:
    __slots__ = ("name", "w", "r")

    def __init__(self, name=""):
        self.name = name
        self.w = None
        self.r = []


class Prog:
    def __init__(self, nc, n_dma_sems=40):
        self.nc = nc
        self.allow_silent = True
        self.pe_selfwait = False
        self.ops = {e: [] for e in ENGS}
        self.count = {e: 0 for e in ENGS}
        self.waited = {e: {} for e in ENGS}
        self.n_dma_sems = n_dma_sems
        self.dma_cnt = [0] * n_dma_sems
        self.dma_rr = 0
        self.sdma_rr = 0
        self.sems = {}

    def _deps(self, eng, reads, writes):
        need = {}

        def add(tok):
            if tok is None:
                return
            k, v = tok
            if need.get(k, 0) < v:
                need[k] = v

        for t in reads:
            add(t.w)
        for t in writes:
            add(t.w)
            for tok in t.r:
                add(tok)
        waits = []
        wd = self.waited[eng]
        for k, v in need.items():
            if wd.get(k, 0) >= v:
                continue
            if k == eng and v > self.count[eng]:
                continue
            if k == eng and eng == "tensor" and not self.pe_selfwait:
                continue
            wd[k] = v
            waits.append((k, v))
        return waits

    def _mark(self, tok, reads, writes):
        for t in reads:
            t.r.append(tok)
            if len(t.r) > 48:
                m = {}
                for k, v in t.r:
                    if m.get(k, 0) < v:
                        m[k] = v
                t.r = list(m.items())
        for t in writes:
            t.w = tok
            t.r = []

    def op(self, eng, fn, reads=(), writes=(), silent=False):
        waits = self._deps(eng, reads, writes)
        if silent and self.allow_silent:
            tok = (eng, self.count[eng] + 1)
            self.ops[eng].append((waits, fn, None))
        else:
            self.count[eng] += 1
            tok = (eng, self.count[eng])
            self.ops[eng].append((waits, fn, (eng, 1)))
        self._mark(tok, reads, writes)
        return tok

    def dma(self, eng, fn, reads=(), writes=()):
        if eng == "gpsimd":
            s = self.n_dma_sems - 8 + self.sdma_rr
            self.sdma_rr = (self.sdma_rr + 1) % 8
        else:
            s = self.dma_rr
            self.dma_rr = (self.dma_rr + 1) % (self.n_dma_sems - 8)
        key = ("dma", s)
        waits = self._deps(eng, reads, writes)
        prev = self.dma_cnt[s]
        if prev > 0 and self.waited[eng].get(key, 0) < prev:
            self.waited[eng][key] = prev
            waits.append((key, prev))
        self.dma_cnt[s] += 16
        tok = (key, self.dma_cnt[s])
        self.ops[eng].append((waits, fn, (key, 16)))
        self._mark(tok, reads, writes)
        return tok

    def barrier(self):
        toks = [(e, self.count[e]) for e in ENGS if self.count[e] > 0]
        toks += [(("dma", s), v) for s, v in enumerate(self.dma_cnt) if v > 0]
        for e in ENGS:
            waits = []
            for k, v in toks:
                if k == e:
                    continue
                if self.waited[e].get(k, 0) < v:
                    self.waited[e][k] = v
                    waits.append((k, v))
            if waits:
                self.ops[e].append((waits, None, None))

    def final_wait(self, eng="sync"):
        toks = [(e, self.count[e]) for e in ENGS if self.count[e] > 0 and e != eng]
        toks += [(("dma", s), v) for s, v in enumerate(self.dma_cnt) if v > 0]
        waits = []
        for k, v in toks:
            if self.waited[eng].get(k, 0) < v:
                self.waited[eng][k] = v
                waits.append((k, v))
        self.ops[eng].append((waits, None, None))

    def emit(self, es):
        nc = self.nc
        for e in ENGS:
            self.sems[e] = es.enter_context(nc.semaphore("s_" + e))
        for s in range(self.n_dma_sems):
            self.sems[("dma", s)] = es.enter_context(nc.semaphore("s_dma%d" % s))
        block = es.enter_context(nc.Block())
        sems = self.sems

        def run(engname):
            def body(e):
                for waits, fn, inc in self.ops[engname]:
                    for k, v in waits:
                        e.wait_ge(sems[k], v)
                    if fn is None:
                        continue
                    ins = fn(e)
                    if inc is not None:
                        ins.then_inc(sems[inc[0]], inc[1])
            return body

        block.sync(run("sync"))
        block.tensor(run("tensor"))
        block.vector(run("vector"))
        block.scalar(run("scalar"))
        block.gpsimd(run("gpsimd"))


class Arena:
    def __init__(self, base_ap_bf16, nelem_bf16):
        self.base = base_ap_bf16
        self.n = nelem_bf16
        self.top = 0
        self.stack = []
        self.peak = 0

    def push(self):
        self.stack.append(self.top)

    def pop(self):
        self.top = self.stack.pop()

    def alloc(self, nelem, dtype):
        w = 2 if dtype == F32 else 1
        n16 = (nelem * w + 31) // 32 * 32
        if self.top + n16 > self.n:
            raise MemoryError("SBUF arena overflow: need %d have %d" % (self.top + n16, self.n))
        ap = self.base[:, self.top:self.top + nelem * w]
        self.top += n16
        self.peak = max(self.peak, self.top)
        if dtype != BF16:
            ap = ap.bitcast(dtype)
        return ap


def _toeplitz(fn):
    j = np.arange(128)[:, None]
    c = np.arange(TABW)[None, :]
    dist = c - 511 - j
    valid = fn(dist)
    return np.where(valid, -dist.astype(np.float32), np.float32(NEG)).astype(np.float32)


def make_consts():
    c = {}
    c["c_ident"] = np.eye(128, dtype=np.float32)
    tabA = np.stack([
        _toeplitz(lambda d: (d >= 0) & (d <= 128)),
        _toeplitz(lambda d: (d >= 0) & (d <= 512) & (d % 4 == 0)),
        _toeplitz(lambda d: (d >= 0) & (d % 16 == 0)),
    ], 0)
    c["c_tabA"] = np.ascontiguousarray(tabA.transpose(1, 0, 2))
    tabB = np.stack([
        _toeplitz(lambda d: (d >= 0)),
        _toeplitz(lambda d: (d >= 0) & (d <= 511)),
    ], 0)
    c["c_tabB"] = np.ascontiguousarray(tabB.transpose(1, 0, 2))
    n = np.arange(128)[:, None]
    t = np.arange(S)[None, :]
    cend = 16 * n + 31
    dc = t - cend
    cm = np.where((dc >= 0) & (n < 127), -dc.astype(np.float32), np.float32(NEG)).astype(np.float32)
    c["c_cmp"] = np.ascontiguousarray(cm)
    cs = np.arange(127) * 16
    ss = np.arange(32) * 64
    ov = np.clip(np.minimum(cs[:, None] + 32, ss[None, :] + 64) - np.maximum(cs[:, None], ss[None, :]), 0, None) / 32.0
    ovp = np.zeros((128, 32), np.float32)
    ovp[:127] = ov
    c["c_ov"] = ovp
    E = np.zeros((128, 16, 128), np.float32)
    for kt in range(16):
        for s_ in range(128):
            E[2 * kt + s_ // 64, kt, s_] = 1.0
    c["c_E"] = E
    keep = np.zeros((128, 16, 32), np.float32)
    force = np.zeros((128, 16, 32), np.float32)
    for tt in range(16):
        for p in range(128):
            cur = (tt * 128 + p) // 64
            for jb in range(32):
                if jb == 0 or jb == cur or jb == cur - 1:
                    force[p, tt, jb] = 1.0e4 + (32 - jb)
                elif jb > cur:
                    force[p, tt, jb] = -1.0e4 - jb
                else:
                    keep[p, tt, jb] = 1.0
    c["c_keep"] = keep
    c["c_force"] = force
    oh = np.zeros((128, 36, 128), np.float32)
    for r in range(36):
        oh[r, r, :] = 1.0
    c["c_onehot"] = oh
    return c


class KB:
    def __init__(self, nc, es):
        self.nc = nc
        self.P = Prog(nc)
        arena_t = es.enter_context(nc.sbuf_tensor("arena", [128, ARENA_N], BF16))
        self.ar = Arena(arena_t[:, :], ARENA_N)
        self.ps = es.enter_context(nc.psum_tensor("ps", [128, 8, 512], F32))
        self.PB = [Tile("pb%d" % i) for i in range(8)]

    def V(self, fn, r=(), w=()):
        return self.P.op("vector", fn, r, w)

    def A(self, fn, r=(), w=()):
        return self.P.op("scalar", fn, r, w)

    def T(self, fn, r=(), w=(), silent=False):
        return self.P.op("tensor", fn, r, w, silent=silent)

    def G(self, fn, r=(), w=()):
        return self.P.op("gpsimd", fn, r, w)

    def DMA(self, fn, r=(), w=(), q="sync"):
        return self.P.dma(q, fn, r, w)

    def psb(self, b):
        return self.ps[:, b, :]

    def setup_consts(self, cin):
        ar = self.ar
        self.identf = ar.alloc(128, F32)
        self.Tidf = Tile("identf")
        self.identb = ar.alloc(128, BF16)
        self.Tidb = Tile("identb")
        self.onesb = ar.alloc(128, BF16)
        self.Tones = Tile("ones")
        self.DMA(lambda e: e.dma_start(out=self.identf, in_=cin["c_ident"][:, :]), w=[self.Tidf])
        self.V(lambda e: e.tensor_copy(out=self.identb, in_=self.identf), r=[self.Tidf], w=[self.Tidb])
        self.V(lambda e: e.memset(self.onesb, 1.0), w=[self.Tones])
        self.stage = [ar.alloc(2048, F32) for _ in range(3)]
        self.Tstage = [Tile("stage%d" % i) for i in range(3)]
        self.stage_i = 0
        self.cast_i = 0

    def wload(self, dram_ap, dst_ap, Tdst, k=None, n=None, cast="alt"):
        s = self.stage_i % 3
        self.stage_i += 1
        st = self.stage[s]
        if k is not None:
            st = st[:, 0:k * n].rearrange("p (k n) -> p k n", k=k)
        elif n is not None:
            st = st[:, 0:n]
        Ts = self.Tstage[s]
        self.DMA(lambda e: e.dma_start(out=st, in_=dram_ap), w=[Ts])
        if cast == "alt":
            cast = "gpsimd" if (self.cast_i % 2 == 0) else "vector"
            self.cast_i += 1
        self.P.op(cast, lambda e: e.tensor_copy(out=dst_ap, in_=st), [Ts], [Tdst])

    def rstd_of(self, src_ap, Tsrc, junk, Tjunk, ss, Tss):
        self.A(lambda e: e.activation(out=junk, in_=src_ap, func=AF.Square, accum_out=ss), r=[Tsrc], w=[Tjunk, Tss])
        self.A(lambda e: e.activation(out=ss, in_=ss, func=AF.Sqrt, scale=1.0 / D, bias=1e-6), r=[Tss], w=[Tss])
        self.V(lambda e: e.reciprocal(out=ss, in_=ss), r=[Tss], w=[Tss])

    def normT(self, src_fn, src_tiles, g_row_ap, hT3, ThT, ntt=NT):
        ar = self.ar
        ar.push()
        gt = ar.alloc(D, F32)
        Tg = Tile("g")
        self.DMA(lambda e: e.dma_start(out=gt, in_=g_row_ap.partition_broadcast(128)), w=[Tg])
        NB = 4 if ntt > 2 else 2
        xt = [ar.alloc(D, F32) for _ in range(NB)]
        Tx = [Tile("xt%d" % i) for i in range(NB)]
        hb = [ar.alloc(D, BF16) for _ in range(NB)]
        Th = [Tile("hb%d" % i) for i in range(NB)]
        junk = ar.alloc(D, BF16)
        Tj = Tile("junk")
        ss = [ar.alloc(1, F32) for _ in range(NB)]
        Tss = [Tile("ss%d" % i) for i in range(NB)]
        for tt in range(ntt):
            b = tt % NB
            src = src_fn(tt)
            self.DMA(lambda e, b=b, src=src: e.dma_start(out=xt[b], in_=src), r=[src_tiles[tt]], w=[Tx[b]])
            self.rstd_of(xt[b], Tx[b], junk, Tj, ss[b], Tss[b])
            self.V(lambda e, b=b: e.scalar_tensor_tensor(out=hb[b], in0=xt[b], scalar=ss[b], in1=gt, op0=ALU.mult, op1=ALU.mult),
                   r=[Tx[b], Tss[b], Tg], w=[Th[b]])
            pv = self.ps[:, 2 * b:2 * b + 2, :].bitcast(BF16).rearrange("p a b -> p (a b)")
            for k in range(KC):
                self.T(lambda e, b=b, k=k, pv=pv: e.transpose(out=pv[:, k * 128:(k + 1) * 128], in_=hb[b][:, k * 128:(k + 1) * 128], identity=self.identb),
                       r=[Th[b], self.Tidb], w=[self.PB[2 * b], self.PB[2 * b + 1]], silent=(k != KC - 1))
            self.A(lambda e, tt=tt, pv=pv: e.copy(out=hT3[:, :, tt * 128:(tt + 1) * 128], in_=pv.rearrange("p (k t) -> p k t", k=KC)),
                   r=[self.PB[2 * b], self.PB[2 * b + 1]], w=[ThT[tt]])
        ar.pop()
        self.P.barrier()

    def proj_fm(self, w3, Tw, hT3, ThT, out_ap, Tout, scale, ntok=S, banks=(4, 5)):
        nblk = (ntok + 511) // 512
        for tb in range(nblk):
            n = min(512, ntok - tb * 512)
            bk = banks[tb % 2]
            tts = list(range(tb * 4, tb * 4 + (n + 127) // 128))
            for k in range(KC):
                self.T(lambda e, k=k, tb=tb, n=n, bk=bk: e.matmul(self.ps[:, bk, 0:n], lhsT=w3[:, k, :], rhs=hT3[:, k, tb * 512:tb * 512 + n], start=(k == 0), stop=(k == KC - 1)),
                       r=[Tw] + [ThT[t] for t in tts], w=[self.PB[bk]], silent=(k != KC - 1))
            self.A(lambda e, tb=tb, n=n, bk=bk: e.mul(out_ap[:, tb * 512:tb * 512 + n], self.ps[:, bk, 0:n], scale), r=[self.PB[bk]], w=[Tout])

    def proj_tm(self, w3, Tw, hT3, ThT, out3, Tout, ntt=NT, banks=(6, 7), ncol=128):
        ngrp = (ntt + 3) // 4
        for g4 in range(ngrp):
            bk = banks[g4 % 2]
            cnt = min(4, ntt - g4 * 4)
            for i in range(cnt):
                tt = g4 * 4 + i
                for k in range(KC):
                    self.T(lambda e, k=k, tt=tt, i=i, bk=bk: e.matmul(self.ps[:, bk, i * ncol:(i + 1) * ncol], lhsT=hT3[:, k, tt * 128:(tt + 1) * 128], rhs=w3[:, k, 0:ncol], start=(k == 0), stop=(k == KC - 1)),
                           r=[Tw, ThT[tt]], w=[self.PB[bk]], silent=(k != KC - 1))
            self.V(lambda e, g4=g4, cnt=cnt, bk=bk: e.tensor_copy(out=out3[:, g4 * 4:g4 * 4 + cnt, :], in_=self.ps[:, bk, 0:cnt * ncol].rearrange("p (a d) -> p a d", a=cnt)),
                   r=[self.PB[bk]], w=[Tout])

    def attn_bufs(self):
        ar = self.ar
        self.sb = [ar.alloc(512, F32) for _ in range(4)]
        self.Tsb = [Tile("sb%d" % i) for i in range(4)]
        self.pt = [ar.alloc(512, BF16) for _ in range(4)]
        self.Tpt = [Tile("pt%d" % i) for i in range(4)]
        self.rec = ar.alloc(512, F32)
        self.Trec = Tile("rec")
        self.osb = [ar.alloc(512, BF16) for _ in range(2)]
        self.Tosb = [Tile("osb%d" % i) for i in range(2)]
        self.osb_i = 0

    def attn_units(self, units, q_ap, Tq, imp=None):
        nu = len(units)
        OB, DB = 2, 3

        def s_stage(i):
            u = units[i]
            b = i % 2
            pen = u.get("pen")
            self.T(lambda e, u=u, b=b: e.matmul(self.ps[:, b, :], lhsT=u["kT"], rhs=q_ap, start=True, stop=(u.get("pen") is None)),
                   r=list(u["rd"]) + [Tq], w=[self.PB[b]])
            if pen is not None:
                self.T(lambda e, pen=pen, b=b: e.matmul(self.ps[:, b, :], lhsT=pen[0], rhs=pen[1], start=False, stop=True),
                       r=list(pen[2]), w=[self.PB[b]])

        def mid(i):
            u = units[i]
            b = i % 2
            bias = float(u.get("bias", 0.0))
            if u.get("tab") is not None:
                self.V(lambda e, u=u, b=b: e.scalar_tensor_tensor(out=self.sb[b], in0=u["tab"], scalar=float(u["slope"]), in1=self.ps[:, b, :], op0=ALU.mult, op1=ALU.add),
                       r=[self.PB[b]] + list(u.get("tabrd", [])), w=[self.Tsb[b]])
                src, rd = self.sb[b], [self.Tsb[b]]
            else:
                src, rd = self.ps[:, b, :], [self.PB[b]]
            if bias != 0.0:
                self.A(lambda e, b=b, src=src, bias=bias: e.activation(out=self.pt[b], in_=src, func=AF.Exp, bias=bias), r=rd, w=[self.Tpt[b]])
            else:
                self.A(lambda e, b=b, src=src: e.activation(out=self.pt[b], in_=src, func=AF.Exp), r=rd, w=[self.Tpt[b]])

        def pv(i):
            u = units[i]
            b = i % 2
            first = (i == 0)
            last = (i == nu - 1)
            self.T(lambda e, u=u, b=b, first=first, last=last: e.matmul(self.ps[:, OB, :], lhsT=u["V"], rhs=self.pt[b], start=first, stop=last),
                   r=list(u["rd"]) + [self.Tpt[b]], w=[self.PB[OB]])
            self.T(lambda e, b=b, first=first, last=last: e.matmul(self.ps[:, DB, :], lhsT=self.onesb, rhs=self.pt[b], start=first, stop=last),
                   r=[self.Tones, self.Tpt[b]], w=[self.PB[DB]])
            if imp is not None:
                self.T(lambda e, b=b, first=first, last=last: e.matmul(self.ps[0:32, 7, :], lhsT=imp[0], rhs=self.pt[b], start=first, stop=last),
                       r=[imp[1], self.Tpt[b]], w=[self.PB[7]])

        s_stage(0)
        for i in range(nu):
            if i + 1 < nu:
                s_stage(i + 1)
            mid(i)
            pv(i)

    def recip_den(self):
        rec = self.rec
        self.V(lambda e: e.tensor_scalar(out=rec, in0=self.ps[:, 3, :], scalar1=1e-30, scalar2=None, op0=ALU.max), r=[self.PB[3]], w=[self.Trec])
        self.V(lambda e: e.reciprocal(out=rec, in_=rec), r=[self.Trec], w=[self.Trec])

    def finalize_plain(self, dst_dram, Tdst):
        self.recip_den()
        ob = self.osb_i % 2
        self.osb_i += 1
        rec = self.rec
        osb = self.osb[ob]
        self.V(lambda e: e.tensor_tensor(out=osb, in0=self.ps[:, 2, :], in1=rec, op=ALU.mult), r=[self.PB[2], self.Trec], w=[self.Tosb[ob]])
        self.DMA(lambda e: e.dma_start(out=dst_dram, in_=osb.rearrange("p (t c) -> p t c", t=4)), r=[self.Tosb[ob]], w=[Tdst], q=STQ)

    def mem_kv(self, mem_ap, Tmem, g_row, wkv, hT3mem, ThTm, kTm, TkTm, Vm, TVm, wbuf, Twbuf):
        self.normT(lambda tt: mem_ap[tt * 128:(tt + 1) * 128, :], [Tmem, Tmem], g_row, hT3mem, ThTm, ntt=2)
        wcols = wkv.rearrange("(k p) n -> p k n", p=128)
        for h in range(4):
            b = h % 2
            self.wload(wcols[:, :, h * 128:(h + 1) * 128], wbuf[b], Twbuf[b], k=KC, n=128)
            self.proj_fm(wbuf[b], Twbuf[b], hT3mem, ThTm, kTm[:, h, :], TkTm, 1.0, ntok=256)
        for h in range(4):
            b = h % 2
            self.wload(wcols[:, :, 512 + h * 128:512 + (h + 1) * 128], wbuf[b], Twbuf[b], k=KC, n=128)
            self.proj_tm(wbuf[b], Twbuf[b], hT3mem, ThTm, Vm[:, h, :, :], TVm, ntt=2)

    def resid_update(self, y_ap, Ty, x_src_ap, Tsrc, x_dst_ap, Tdst, gt, Tg, bufs):
        xt, Tx, tmp, Ttmp, junk, Tj, ss, Tss = bufs
        self.DMA(lambda e: e.dma_start(out=xt, in_=x_src_ap), r=[Tsrc], w=[Tx])
        self.rstd_of(y_ap, Ty, junk, Tj, ss, Tss)
        self.V(lambda e: e.scalar_tensor_tensor(out=tmp, in0=y_ap, scalar=ss, in1=gt, op0=ALU.mult, op1=ALU.mult), r=[Ty, Tss, Tg], w=[Ttmp])
        self.V(lambda e: e.tensor_tensor(out=xt, in0=xt, in1=tmp, op=ALU.add), r=[Ttmp, Tx], w=[Tx])
        self.DMA(lambda e: e.dma_start(out=x_dst_ap, in_=xt), r=[Tx], w=[Tdst], q=STQ)


def alibi(n, i):
    return float(2.0 ** (-8.0 * (i + 1) / n))


def build(n_layers=4, stop_after_mixer=False, l_start=0, dbg=False, decl=None, allow_silent=True, pe_selfwait=False, rep=1, dummy=(), seq=None):
    nc = bass.Bass("TRN2", target_bir_lowering=False)

    def din(name, shape):
        if decl is not None and name not in decl:
            return nc.dram_tensor(name, [1] * len(shape), F32).ap()
        return nc.dram_tensor(name, list(shape), F32, kind="ExternalInput").ap()

    x_in = din("x", [S, D])
    mem = din("mem", [256, D])
    norm_g = din("norm_g", [4, 5, D])
    a_w_in = din("a_w_in", [2, D, 9728])
    a_w_out = din("a_w_out", [2, 1536, D])
    b_w_in = din("b_w_in", [2, D, 2084])
    b_w_out = din("b_w_out", [2, 2048, D])
    mem_w_kv = din("mem_w_kv", [4, D, 1024])
    ffn_w_gu = din("ffn_w_gu", [4, D, 2 * DFF])
    ffn_w_down = din("ffn_w_down", [4, DFF, D])
    kv_norm_g = din("kv_norm_g", [1, D])
    kv_w = din("kv_w", [D, 3072])
    cmp_pe = din("cmp_pe", [2, 32, 128])
    cmp_wk1 = din("cmp_wk1", [4096, 512])
    cmp_wk2 = din("cmp_wk2", [512, 128])
    cmp_wv1 = din("cmp_wv1", [4096, 512])
    cmp_wv2 = din("cmp_wv2", [512, 128])
    cshapes = {"c_ident": [128, 128], "c_tabA": [128, 3, TABW], "c_tabB": [128, 2, TABW], "c_cmp": [128, S],
               "c_ov": [128, 32], "c_E": [128, 16, 128], "c_keep": [128, 16, 32], "c_force": [128, 16, 32],
               "c_onehot": [128, 36, 128]}
    cin = {k: din(k, v) for k, v in cshapes.items()}
    out = nc.dram_tensor("out", [S, D], F32, kind="ExternalOutput").ap()
    oT_d = nc.dram_tensor("oT_d", [NT, 128, 16, 128], BF16).ap()
    qT_d = nc.dram_tensor("qT_d", [16, 128, S], BF16).ap()
    actT_d = nc.dram_tensor("actT_d", [NT, 128, NFC, 128], BF16).ap()
    y_d = nc.dram_tensor("y_d", [S, D], F32).ap()
    sh_d = nc.dram_tensor("sh_d", [4, 4, 128, S], BF16).ap()
    kc_d = nc.dram_tensor("kc_d", [2, 128, 4, 128], BF16).ap()

    dbg_t = {}
    if dbg:
        for l_ in range(4):
            for nm in ("xm", "x"):
                dbg_t[nm + str(l_)] = nc.dram_tensor("dbg_" + nm + str(l_), [S, D], F32, kind="ExternalOutput").ap()

    with ExitStack() as es:
        kb = KB(nc, es)
        P = kb.P
        P.allow_silent = allow_silent
        P.pe_selfwait = pe_selfwait
        ar = kb.ar
        ps = kb.ps
        PB = kb.PB
        kb.setup_consts(cin)

        Tmem = Tile("mem")
        Tin = Tile("x_in")
        Tout = [Tile("out%d" % i) for i in range(NT)]
        ToT = [Tile("oT%d" % i) for i in range(16)]
        TqT = [Tile("qT%d" % i) for i in range(16)]
        Tact = [Tile("act%d" % i) for i in range(NFC)]
        Ty = [Tile("y%d" % i) for i in range(NT)]
        Tsh = [Tile("sh%d" % i) for i in range(4)]
        Tkc = Tile("kc")

        state = {"first": True}

        def xsrc(tt):
            if state["first"]:
                return x_in[tt * 128:(tt + 1) * 128, :], Tin
            return out[tt * 128:(tt + 1) * 128, :], Tout[tt]

        def w_out_phase(w_out_l, nheads, g_row):
            P.barrier()
            ar.push()
            wo = ar.alloc(nheads * D, BF16).rearrange("p (h n) -> p h n", h=nheads)
            Two = [Tile("wo%d" % i) for i in range(nheads)]
            for h in range(nheads):
                kb.wload(w_out_l[h * 128:(h + 1) * 128, :], wo[:, h, :], Two[h], n=D)
            gt = ar.alloc(D, F32)
            Tg = Tile("g")
            kb.DMA(lambda e: e.dma_start(out=gt, in_=g_row.partition_broadcast(128)), w=[Tg])
            ot = [ar.alloc(nheads * 128, BF16).rearrange("p (h t) -> p h t", h=nheads) for _ in range(4)]
            Tot = [Tile("ot%d" % i) for i in range(4)]
            xts = [ar.alloc(D, F32) for _ in range(4)]
            Txs = [Tile("xt%d" % i) for i in range(4)]
            sss = [ar.alloc(1, F32) for _ in range(4)]
            Tsss = [Tile("ss%d" % i) for i in range(4)]
            tmp = ar.alloc(D, F32)
            Ttmp = Tile("tmp")
            junk = ar.alloc(D, BF16)
            Tj = Tile("junk")
            g4 = gt.rearrange("p (a b) -> p a b", a=4)
            t4 = tmp.rearrange("p (a b) -> p a b", a=4)
            j4 = junk.rearrange("p (a b) -> p a b", a=4)
            for tt in range(NT):
                b = tt % 4
                pb0 = 4 * (tt % 2)
                kb.DMA(lambda e, b=b, tt=tt: e.dma_start(out=ot[b], in_=oT_d[tt, :, 0:nheads, :]), r=ToT[:nheads], w=[Tot[b]])
                for c4 in range(4):
                    for h in range(nheads):
                        kb.T(lambda e, b=b, c4=c4, h=h, pb0=pb0: e.matmul(ps[:, pb0 + c4, :], lhsT=ot[b][:, h, :], rhs=wo[:, h, c4 * 512:(c4 + 1) * 512], start=(h == 0), stop=(h == nheads - 1)),
                             r=[Tot[b], Two[h]], w=[PB[pb0 + c4]], silent=(h != nheads - 1))
                src, Tsrc = xsrc(tt)
                xt, Tx, ss, Tss = xts[b], Txs[b], sss[b], Tsss[b]
                y_ap = ps[:, pb0:pb0 + 4, :]
                pbs = PB[pb0:pb0 + 4]
                kb.DMA(lambda e, xt=xt, src=src: e.dma_start(out=xt, in_=src), r=[Tsrc], w=[Tx])
                kb.A(lambda e, ss=ss, y_ap=y_ap: e.activation(out=j4, in_=y_ap, func=AF.Square, accum_out=ss), r=pbs, w=[Tj, Tss])
                kb.A(lambda e, ss=ss: e.activation(out=ss, in_=ss, func=AF.Sqrt, scale=1.0 / D, bias=1e-6), r=[Tss], w=[Tss])
                kb.V(lambda e, ss=ss: e.reciprocal(out=ss, in_=ss), r=[Tss], w=[Tss])
                kb.V(lambda e, ss=ss, y_ap=y_ap: e.scalar_tensor_tensor(out=t4, in0=y_ap, scalar=ss, in1=g4, op0=ALU.mult, op1=ALU.mult),
                     r=pbs + [Tss, Tg], w=[Ttmp])
                kb.V(lambda e, xt=xt: e.tensor_tensor(out=xt, in0=xt, in1=tmp, op=ALU.add), r=[Ttmp, Tx], w=[Tx])
                kb.DMA(lambda e, xt=xt, tt=tt: e.dma_start(out=out[tt * 128:(tt + 1) * 128, :], in_=xt), r=[Tx], w=[Tout[tt]], q=STQ)
            ar.pop()
            P.barrier()
            state["first"] = False

        def ffn_phase(l):
            P.barrier()
            ar.push()
            wd = [ar.alloc(NFC * 512, BF16).rearrange("p (f n) -> p f n", f=NFC), None]
            Twd = [[Tile("wd%d_%d" % (i, q)) for q in range(11)] for i in range(2)]
            wdv = ffn_w_down[l].rearrange("(f p) n -> p f n", p=128)

            def load_wd_unit(c4, q):
                b = c4 % 2
                kb.wload(wdv[:, q * 4:(q + 1) * 4, c4 * 512:(c4 + 1) * 512], wd[b][:, q * 4:(q + 1) * 4, :], Twd[b][q], k=4, n=512)

            ar.push()
            hT = ar.alloc(KC * S, BF16)
            hT3 = hT.rearrange("p (k t) -> p k t", k=KC)
            ThT = [Tile("hT%d" % i) for i in range(NT)]
            kb.normT(lambda tt: out[tt * 128:(tt + 1) * 128, :], Tout, norm_g[l, 2:3, :], hT3, ThT)
            wg = [ar.alloc(KC * 128, BF16).rearrange("p (k n) -> p k n", k=KC) for _ in range(2)]
            wu = [ar.alloc(KC * 128, BF16).rearrange("p (k n) -> p k n", k=KC) for _ in range(2)]
            Twg = [Tile("wg%d" % i) for i in range(2)]
            Twu = [Tile("wu%d" % i) for i in range(2)]
            sg = [ar.alloc(512, F32) for _ in range(2)]
            Tsg = [Tile("sg%d" % i) for i in range(2)]
            ao = [ar.alloc(512, BF16) for _ in range(2)]
            Tao = [Tile("ao%d" % i) for i in range(2)]
            wgu = ffn_w_gu[l].rearrange("(k p) n -> p k n", p=128)

            def load_fc(fc):
                b = fc % 2
                kb.wload(wgu[:, :, fc * 128:(fc + 1) * 128], wg[b], Twg[b], k=KC, n=128)
                kb.wload(wgu[:, :, DFF + fc * 128:DFF + (fc + 1) * 128], wu[b], Twu[b], k=KC, n=128)

            load_fc(0)
            it = 0
            for fc in range(NFC):
                if fc + 1 < NFC:
                    load_fc(fc + 1)
                if 30 <= fc < 41:
                    load_wd_unit(0, fc - 30)
                b = fc % 2
                for tb in range(4):
                    gb, ub = 4 + 2 * (it % 2), 5 + 2 * (it % 2)
                    i2 = it % 2
                    it += 1
                    tts = [ThT[t] for t in range(tb * 4, tb * 4 + 4)]
                    for k in range(KC):
                        kb.T(lambda e, k=k, b=b, tb=tb, gb=gb: e.matmul(ps[:, gb, :], lhsT=wg[b][:, k, :], rhs=hT3[:, k, tb * 512:(tb + 1) * 512], start=(k == 0), stop=(k == KC - 1)),
                             r=[Twg[b]] + tts, w=[PB[gb]], silent=(k != KC - 1))
                    for k in range(KC):
                        kb.T(lambda e, k=k, b=b, tb=tb, ub=ub: e.matmul(ps[:, ub, :], lhsT=wu[b][:, k, :], rhs=hT3[:, k, tb * 512:(tb + 1) * 512], start=(k == 0), stop=(k == KC - 1)),
                             r=[Twu[b]] + tts, w=[PB[ub]], silent=(k != KC - 1))
                    kb.A(lambda e, i2=i2, gb=gb: e.activation(out=sg[i2], in_=ps[:, gb, :], func=AF.Silu), r=[PB[gb]], w=[Tsg[i2]])
                    kb.V(lambda e, i2=i2, ub=ub: e.tensor_tensor(out=ao[i2], in0=ps[:, ub, :], in1=sg[i2], op=ALU.mult), r=[PB[ub], Tsg[i2]], w=[Tao[i2]])
                    kb.DMA(lambda e, i2=i2, fc=fc, tb=tb: e.dma_start(out=actT_d[tb * 4:(tb + 1) * 4, :, fc, :].rearrange("t p c -> p t c"), in_=ao[i2].rearrange("p (t c) -> p t c", t=4)),
                           r=[Tao[i2]], w=[Tact[fc]], q=STQ)
            ar.pop()
            P.barrier()
            wd[1] = ar.alloc(NFC * 512, BF16).rearrange("p (f n) -> p f n", f=NFC)
            at = [ar.alloc(NFC * 128, BF16).rearrange("p (f t) -> p f t", f=NFC) for _ in range(4)]
            Tat = [Tile("at%d" % i) for i in range(4)]
            yo = [ar.alloc(512, F32) for _ in range(2)]
            Tyo = [Tile("yo%d" % i) for i in range(2)]
            it = 0
            for c4 in range(4):
                b = c4 % 2
                for tt in range(NT):
                    if c4 + 1 < 4 and 1 <= tt < 12:
                        load_wd_unit(c4 + 1, tt - 1)
                    i2 = it % 2
                    i4 = it % 4
                    bk = 4 + (it % 2)
                    it += 1
                    kb.DMA(lambda e, i4=i4, tt=tt: e.dma_start(out=at[i4], in_=actT_d[tt]), r=Tact, w=[Tat[i4]])
                    for f in range(NFC):
                        kb.T(lambda e, f=f, i4=i4, b=b, bk=bk: e.matmul(ps[:, bk, :], lhsT=at[i4][:, f, :], rhs=wd[b][:, f, :], start=(f == 0), stop=(f == NFC - 1)),
                             r=[Tat[i4], Twd[b][f // 4]], w=[PB[bk]], silent=(f != NFC - 1))
                    kb.A(lambda e, i2=i2, bk=bk: e.copy(out=yo[i2], in_=ps[:, bk, :]), r=[PB[bk]], w=[Tyo[i2]])
                    kb.DMA(lambda e, i2=i2, tt=tt, c4=c4: e.dma_start(out=y_d[tt * 128:(tt + 1) * 128, c4 * 512:(c4 + 1) * 512], in_=yo[i2]), r=[Tyo[i2]], w=[Ty[tt]], q=STQ)
            ar.pop()
            P.barrier()
            ar.push()
            gt = ar.alloc(D, F32)
            Tg = Tile("g")
            kb.DMA(lambda e: e.dma_start(out=gt, in_=norm_g[l, 3:4, :].partition_broadcast(128)), w=[Tg])
            yt = [ar.alloc(D, F32) for _ in range(4)]
            Tyt = [Tile("yt%d" % i) for i in range(4)]
            xt = [ar.alloc(D, F32) for _ in range(4)]
            Tx = [Tile("xt%d" % i) for i in range(4)]
            ss = [ar.alloc(1, F32) for _ in range(4)]
            Tss = [Tile("ss%d" % i) for i in range(4)]
            tmp = ar.alloc(D, F32)
            Ttmp = Tile("tmp")
            junk = ar.alloc(D, BF16)
            Tj = Tile("junk")
            for tt in range(NT):
                b = tt % 4
                kb.DMA(lambda e, b=b, tt=tt: e.dma_start(out=yt[b], in_=y_d[tt * 128:(tt + 1) * 128, :]), r=[Ty[tt]], w=[Tyt[b]])
                kb.resid_update(yt[b], Tyt[b], out[tt * 128:(tt + 1) * 128, :], Tout[tt], out[tt * 128:(tt + 1) * 128, :], Tout[tt], gt, Tg,
                                (xt[b], Tx[b], tmp, Ttmp, junk, Tj, ss[b], Tss[b]))
            ar.pop()
            P.barrier()

        def layer_a(l):
            P.barrier()
            ar.push()
            hT = ar.alloc(KC * S, BF16)
            hT3 = hT.rearrange("p (k t) -> p k t", k=KC)
            ThT = [Tile("hT%d" % i) for i in range(NT)]
            wb = [ar.alloc(KC * 128, BF16).rearrange("p (k n) -> p k n", k=KC) for _ in range(3)]
            Twb = [Tile("wb%d" % i) for i in range(3)]
            hTm = ar.alloc(KC * 256, BF16).rearrange("p (k t) -> p k t", k=KC)
            ThTm = [Tile("hTm0"), Tile("hTm1")]
            kTm = ar.alloc(4 * 256, BF16).rearrange("p (h t) -> p h t", h=4)
            TkTm = Tile("kTm")
            Vm = ar.alloc(4 * 2 * 128, BF16).rearrange("p (h a d) -> p h a d", h=4, a=2)
            TVm = Tile("Vm")
            kb.mem_kv(mem, Tmem, norm_g[l, 4:5, :], mem_w_kv[l], hTm, ThTm, kTm, TkTm, Vm, TVm, wb, Twb)
            if state["first"]:
                kb.normT(lambda tt: x_in[tt * 128:(tt + 1) * 128, :], [Tin] * NT, norm_g[l, 0:1, :], hT3, ThT)
            else:
                kb.normT(lambda tt: out[tt * 128:(tt + 1) * 128, :], Tout, norm_g[l, 0:1, :], hT3, ThT)
            tab = ar.alloc(3 * TABW, F32).rearrange("p (g c) -> p g c", g=3)
            Ttab = Tile("tabA")
            kb.DMA(lambda e: e.dma_start(out=tab, in_=cin["c_tabA"]), w=[Ttab])
            kb.attn_bufs()
            qT = [ar.alloc(S, BF16) for _ in range(3)]
            kT = [ar.alloc(S, BF16) for _ in range(3)]
            Vt = [ar.alloc(NT * 128, BF16).rearrange("p (t d) -> p t d", t=NT) for _ in range(3)]
            Tq = [Tile("q%d" % i) for i in range(3)]
            Tk = [Tile("k%d" % i) for i in range(3)]
            Tv = [Tile("v%d" % i) for i in range(3)]
            win = a_w_in[l].rearrange("(k p) n -> p k n", p=128)
            for j in range(8):
                for g in range(3):
                    hq = g * 8 + j
                    kb.wload(win[:, :, hq * 128:(hq + 1) * 128], wb[0], Twb[0], k=KC, n=128)
                    kb.proj_fm(wb[0], Twb[0], hT3, ThT, qT[g], Tq[g], ISQ)
                    kb.wload(win[:, :, (24 + hq) * 128:(24 + hq + 1) * 128], wb[1], Twb[1], k=KC, n=128)
                    kb.proj_fm(wb[1], Twb[1], hT3, ThT, kT[g], Tk[g], 1.0)
                    kb.wload(win[:, :, (48 + hq) * 128:(48 + hq + 1) * 128], wb[2], Twb[2], k=KC, n=128)
                    kb.proj_tm(wb[2], Twb[2], hT3, ThT, Vt[g], Tv[g])
                for tb in range(4):
                    units = []
                    for g in range(3):
                        slope = alibi(24, g * 8 + j)
                        lo = {0: 4 * tb - 1, 1: 4 * tb - 4, 2: 0}[g]
                        for kt in range(max(0, lo), 4 * tb + 4):
                            delta = 512 * tb - 128 * kt
                            de = min(delta, 128) if g == 2 else delta
                            bias = -slope * (delta - de)
                            units.append(dict(kT=kT[g][:, kt * 128:(kt + 1) * 128], V=Vt[g][:, kt, :], rd=[Tk[g], Tv[g]],
                                              tab=tab[:, g, de + 511:de + 511 + 512], tabrd=[Ttab], slope=slope, bias=bias,
                                              q=qT[g][:, tb * 512:(tb + 1) * 512], Tq=Tq[g]))
                    run_units_chain(units, None, None, True, True)
                    kb.finalize_plain(oT_d[tb * 4:(tb + 1) * 4, :, j, :].rearrange("t p c -> p t c"), ToT[j])
            for h in range(4):
                kb.wload(win[:, :, 9216 + h * 128:9216 + (h + 1) * 128], wb[0], Twb[0], k=KC, n=128)
                kb.proj_fm(wb[0], Twb[0], hT3, ThT, qT[0], Tq[0], ISQ)
                for tb in range(4):
                    units = [dict(kT=kTm[:, h, kt * 128:(kt + 1) * 128], V=Vm[:, h, kt, :], rd=[TkTm, TVm], tab=None) for kt in range(2)]
                    run_units_chain(units, qT[0][:, tb * 512:(tb + 1) * 512], Tq[0], True, True)
                    kb.finalize_plain(oT_d[tb * 4:(tb + 1) * 4, :, 8 + h, :].rearrange("t p c -> p t c"), ToT[8 + h])
            ar.pop()
            w_out_phase(a_w_out[l], 12, norm_g[l, 1:2, :])

        def run_units_chain(units, q_ap, Tq, first, last):
            nu = len(units)
            OB, DB = 2, 3
            kbx = kb
            sbL, ptL, onesL = kb.sb, kb.pt, kb.onesb
            SBK = (0, 1, 4, 5)
            LA = 3

            def s_stage(i):
                u = units[i]
                b = i % 4
                pen = u.get("pen")
                uq = u.get("q", q_ap)
                uTq = u.get("Tq", Tq)
                kbx.T(lambda e, u=u, b=b, uq=uq: e.matmul(ps[:, SBK[b], :], lhsT=u["kT"], rhs=uq, start=True, stop=(u.get("pen") is None)),
                      r=list(u["rd"]) + [uTq], w=[PB[SBK[b]]], silent=(pen is not None))
                if pen is not None:
                    kbx.T(lambda e, pen=pen, b=b: e.matmul(ps[:, SBK[b], :], lhsT=pen[0], rhs=pen[1], start=False, stop=True),
                          r=list(pen[2]), w=[PB[SBK[b]]])

            def mid(i):
                u = units[i]
                b = i % 4
                bias = float(u.get("bias", 0.0))
                if u.get("tab") is not None:
                    kbx.V(lambda e, u=u, b=b: e.scalar_tensor_tensor(out=sbL[b], in0=u["tab"], scalar=float(u["slope"]), in1=ps[:, SBK[b], :], op0=ALU.mult, op1=ALU.add),
                          r=[PB[SBK[b]]] + list(u.get("tabrd", [])), w=[kbx.Tsb[b]])
                    src, rd = sbL[b], [kbx.Tsb[b]]
                else:
                    src, rd = ps[:, SBK[b], :], [PB[SBK[b]]]
                if bias != 0.0:
                    kbx.A(lambda e, b=b, src=src, bias=bias: e.activation(out=ptL[b], in_=src, func=AF.Exp, bias=bias), r=rd, w=[kbx.Tpt[b]])
                else:
                    kbx.A(lambda e, b=b, src=src: e.activation(out=ptL[b], in_=src, func=AF.Exp), r=rd, w=[kbx.Tpt[b]])

            def pv(i):
                u = units[i]
                b = i % 4
                st = first and (i == 0)
                sp = last and (i == nu - 1)
                imp = u.get("imp")
                kbx.T(lambda e, u=u, b=b: e.matmul(ps[:, OB, :], lhsT=u["V"], rhs=ptL[b], start=st, stop=sp),
                      r=list(u["rd"]) + [kbx.Tpt[b]], w=[PB[OB]], silent=True)
                kbx.T(lambda e, b=b: e.matmul(ps[:, DB, :], lhsT=onesL, rhs=ptL[b], start=st, stop=sp),
                      r=[kbx.Tones, kbx.Tpt[b]], w=[PB[DB]], silent=(imp is not None))
                if imp is not None:
                    kbx.T(lambda e, b=b, imp=imp: e.matmul(ps[0:32, 7, :], lhsT=imp[0], rhs=ptL[b], start=st, stop=sp),
                          r=[imp[1], kbx.Tpt[b]], w=[PB[7]])

            for i in range(min(LA, nu)):
                s_stage(i)
            for i in range(nu):
                if i + LA < nu:
                    s_stage(i + LA)
                mid(i)
                pv(i)

        def shared_kv_phase():
            P.barrier()
            ar.push()
            hT = ar.alloc(KC * S, BF16)
            hT3 = hT.rearrange("p (k t) -> p k t", k=KC)
            ThT = [Tile("hT%d" % i) for i in range(NT)]
            if state["first"]:
                kb.normT(lambda tt: x_in[tt * 128:(tt + 1) * 128, :], [Tin] * NT, kv_norm_g[0:1, :], hT3, ThT)
            else:
                kb.normT(lambda tt: out[tt * 128:(tt + 1) * 128, :], Tout, kv_norm_g[0:1, :], hT3, ThT)
            wb = [ar.alloc(KC * 128, BF16).rearrange("p (k n) -> p k n", k=KC) for _ in range(2)]
            Twb = [Tile("wb%d" % i) for i in range(2)]
            tmpT = [ar.alloc(S, BF16) for _ in range(2)]
            Ttmp = [Tile("tmpT%d" % i) for i in range(2)]
            kvw = kv_w.rearrange("(k p) n -> p k n", p=128)
            it = 0
            for g in range(4):
                for which, slot, fm in ((2, 0, True), (3, 1, False), (4, 2, True), (5, 3, False)):
                    b = it % 2
                    it += 1
                    col = which * 512 + g * 128
                    kb.wload(kvw[:, :, col:col + 128], wb[b], Twb[b], k=KC, n=128)
                    if fm:
                        kb.proj_fm(wb[b], Twb[b], hT3, ThT, tmpT[b], Ttmp[b], 1.0)
                    else:
                        kb.proj_tm(wb[b], Twb[b], hT3, ThT, tmpT[b].rearrange("p (t d) -> p t d", t=NT), Ttmp[b])
                    kb.DMA(lambda e, b=b, g=g, slot=slot: e.dma_start(out=sh_d[g, slot], in_=tmpT[b]), r=[Ttmp[b]], w=[Tsh[g]], q=STQ)
            w1sb = ar.alloc(32 * 512, BF16).rearrange("p (l n) -> p l n", l=32)
            Tw1 = [Tile("w1_%d" % q) for q in range(8)]
            w2sb = ar.alloc(4 * 128, BF16).rearrange("p (c n) -> p c n", c=4)
            Tw2 = Tile("w2")
            pef = ar.alloc(128, F32)
            Tpef = Tile("pef")
            peT = ar.alloc(32, BF16)
            TpeT = Tile("peT")
            b1 = ar.alloc(4, F32)
            Tb1 = Tile("b1")
            hx = ar.alloc(128, F32)
            Thx = Tile("hx")
            x2 = ar.alloc(128, F32)
            Tx2 = Tile("x2")
            sgm = ar.alloc(128, F32)
            Tsgm = Tile("sgm")
            gel = ar.alloc(4 * 128, BF16).rearrange("p (c n) -> p c n", c=4)
            Tgel = Tile("gel")
            csb = ar.alloc(4 * 128, BF16).rearrange("p (g n) -> p g n", g=4)
            Tcsb = Tile("csb")
            for which, w1, w2 in ((0, cmp_wk1, cmp_wk2), (1, cmp_wv1, cmp_wv2)):
                w1v = w1.rearrange("(l p) n -> p l n", p=128)
                for q in range(8):
                    kb.wload(w1v[:, q * 4:(q + 1) * 4, :], w1sb[:, q * 4:(q + 1) * 4, :], Tw1[q], k=4, n=512)
                kb.wload(w2.rearrange("(c p) n -> p c n", p=128), w2sb, Tw2, k=4, n=128)
                kb.DMA(lambda e, which=which: e.dma_start(out=pef[0:32, :], in_=cmp_pe[which]), w=[Tpef])
                kb.T(lambda e: e.transpose(out=ps[:, 6, 0:32], in_=pef[0:32, :], identity=kb.identf[0:32, 0:32]), r=[Tpef, kb.Tidf], w=[PB[6]])
                kb.V(lambda e: e.tensor_copy(out=peT, in_=ps[:, 6, 0:32]), r=[PB[6]], w=[TpeT])
                for hc in range(4):
                    for l_ in range(32):
                        kb.T(lambda e, hc=hc, l_=l_: e.matmul(ps[:, 7, hc:hc + 1], lhsT=w1sb[:, l_, hc * 128:(hc + 1) * 128], rhs=peT[:, l_:l_ + 1], start=(l_ == 0), stop=(l_ == 31)),
                             r=[Tw1[l_ // 4], TpeT], w=[PB[7]], silent=(l_ != 31))
                kb.V(lambda e: e.tensor_copy(out=b1, in_=ps[:, 7, 0:4]), r=[PB[7]], w=[Tb1])
                kb.V(lambda e: e.memset(csb, 0.0), w=[Tcsb])
                for g in range(4):
                    b = it % 2
                    it += 1
                    col = which * 512 + g * 128
                    kb.wload(kvw[:, :, col:col + 128], wb[b], Twb[b], k=KC, n=128)
                    kb.proj_fm(wb[b], Twb[b], hT3, ThT, tmpT[b], Ttmp[b], 1.0)
                    kr3 = tmpT[b].rearrange("p (n s) -> p n s", s=16)
                    for hc in range(4):
                        bk = 4 + hc % 2
                        for l_ in range(32):
                            rhs = kr3[:, 0:127, l_] if l_ < 16 else kr3[:, 1:128, l_ - 16]
                            kb.T(lambda e, hc=hc, l_=l_, rhs=rhs, bk=bk: e.matmul(ps[:, bk, 0:127], lhsT=w1sb[:, l_, hc * 128:(hc + 1) * 128], rhs=rhs, start=(l_ == 0), stop=(l_ == 31)),
                                 r=[Tw1[l_ // 4], Ttmp[b]], w=[PB[bk]], silent=(l_ != 31))
                        kb.V(lambda e, hc=hc, bk=bk: e.tensor_scalar(out=hx[:, 0:127], in0=ps[:, bk, 0:127], scalar1=b1[:, hc:hc + 1], scalar2=None, op0=ALU.add), r=[PB[bk], Tb1], w=[Thx])
                        kb.V(lambda e: e.tensor_tensor(out=x2[:, 0:127], in0=hx[:, 0:127], in1=hx[:, 0:127], op=ALU.mult), r=[Thx], w=[Tx2])
                        kb.V(lambda e: e.tensor_scalar(out=x2[:, 0:127], in0=x2[:, 0:127], scalar1=0.044715, scalar2=1.0, op0=ALU.mult, op1=ALU.add), r=[Tx2], w=[Tx2])
                        kb.V(lambda e: e.tensor_tensor(out=x2[:, 0:127], in0=x2[:, 0:127], in1=hx[:, 0:127], op=ALU.mult), r=[Tx2, Thx], w=[Tx2])
                        kb.A(lambda e: e.activation(out=sgm[:, 0:127], in_=x2[:, 0:127], func=AF.Sigmoid, scale=1.5957691216057308), r=[Tx2], w=[Tsgm])
                        kb.V(lambda e, hc=hc: e.tensor_tensor(out=gel[:, hc, 0:127], in0=hx[:, 0:127], in1=sgm[:, 0:127], op=ALU.mult), r=[Thx, Tsgm], w=[Tgel])
                    if which == 0:
                        for hc in range(4):
                            kb.T(lambda e, hc=hc: e.matmul(ps[:, 6, 0:127], lhsT=w2sb[:, hc, :], rhs=gel[:, hc, 0:127], start=(hc == 0), stop=(hc == 3)), r=[Tw2, Tgel], w=[PB[6]], silent=(hc != 3))
                        kb.V(lambda e, g=g: e.tensor_copy(out=csb[:, g, 0:127], in_=ps[:, 6, 0:127]), r=[PB[6]], w=[Tcsb])
                    else:
                        for hc in range(4):
                            kb.T(lambda e, hc=hc: e.matmul(ps[0:127, 6, 0:128], lhsT=gel[:, hc, 0:127], rhs=w2sb[:, hc, :], start=(hc == 0), stop=(hc == 3)), r=[Tw2, Tgel], w=[PB[6]], silent=(hc != 3))
                        kb.V(lambda e, g=g: e.tensor_copy(out=csb[0:127, g, :], in_=ps[0:127, 6, 0:128]), r=[PB[6]], w=[Tcsb])
                kb.DMA(lambda e, which=which: e.dma_start(out=kc_d[which], in_=csb), r=[Tcsb], w=[Tkc], q=STQ)
            ar.pop()
            P.barrier()

        def layer_b(l):
            lb = l - 2
            P.barrier()
            ar.push()
            kTm = ar.alloc(4 * 256, BF16).rearrange("p (h t) -> p h t", h=4)
            TkTm = Tile("kTm")
            Vm = ar.alloc(4 * 2 * 128, BF16).rearrange("p (h a d) -> p h a d", h=4, a=2)
            TVm = Tile("Vm")
            ghi = ar.alloc(S, BF16)
            glo = ar.alloc(S, BF16)
            Tgh = Tile("ghi")
            Tgl = Tile("glo")
            win = b_w_in[lb].rearrange("(k p) n -> p k n", p=128)
            ar.push()
            hT = ar.alloc(KC * S, BF16)
            hT3 = hT.rearrange("p (k t) -> p k t", k=KC)
            ThT = [Tile("hT%d" % i) for i in range(NT)]
            wb = [ar.alloc(KC * 128, BF16).rearrange("p (k n) -> p k n", k=KC) for _ in range(2)]
            Twb = [Tile("wb%d" % i) for i in range(2)]
            hTm = ar.alloc(KC * 256, BF16).rearrange("p (k t) -> p k t", k=KC)
            ThTm = [Tile("hTm0"), Tile("hTm1")]
            kb.mem_kv(mem, Tmem, norm_g[l, 4:5, :], mem_w_kv[l], hTm, ThTm, kTm, TkTm, Vm, TVm, wb, Twb)
            if state["first"]:
                kb.normT(lambda tt: x_in[tt * 128:(tt + 1) * 128, :], [Tin] * NT, norm_g[l, 0:1, :], hT3, ThT)
            else:
                kb.normT(lambda tt: out[tt * 128:(tt + 1) * 128, :], Tout, norm_g[l, 0:1, :], hT3, ThT)
            qtmp = [ar.alloc(S, BF16) for _ in range(2)]
            Tqtmp = [Tile("qtmp%d" % i) for i in range(2)]
            for h in range(16):
                b = h % 2
                col = h * 128 if h < 12 else 1572 + (h - 12) * 128
                kb.wload(win[:, :, col:col + 128], wb[b], Twb[b], k=KC, n=128)
                kb.proj_fm(wb[b], Twb[b], hT3, ThT, qtmp[b], Tqtmp[b], ISQ)
                kb.DMA(lambda e, b=b, h=h: e.dma_start(out=qT_d[h], in_=qtmp[b]), r=[Tqtmp[b]], w=[TqT[h]], q=STQ)
            wgt = ar.alloc(KC * 36, BF16).rearrange("p (k n) -> p k n", k=KC)
            Twgt = Tile("wgt")
            kb.wload(win[:, :, 1536:1572], wgt, Twgt, k=KC, n=36)
            gtok = [ar.alloc(36, F32) for _ in range(2)]
            Tgtok = [Tile("gtok%d" % i) for i in range(2)]
            gTf = ar.alloc(S, F32)
            TgTf = Tile("gTf")
            kb.V(lambda e: e.memset(ghi, 0.0), w=[Tgh])
            kb.V(lambda e: e.memset(glo, 0.0), w=[Tgl])
            for tt in range(NT):
                b = tt % 2
                bk = 6 + b
                bk2 = 4 + b
                for k in range(KC):
                    kb.T(lambda e, k=k, tt=tt, bk=bk: e.matmul(ps[:, bk, 0:36], lhsT=hT3[:, k, tt * 128:(tt + 1) * 128], rhs=wgt[:, k, :], start=(k == 0), stop=(k == KC - 1)),
                         r=[Twgt, ThT[tt]], w=[PB[bk]], silent=(k != KC - 1))
                kb.A(lambda e, b=b, bk=bk: e.activation(out=gtok[b], in_=ps[:, bk, 0:36], func=AF.Sigmoid), r=[PB[bk]], w=[Tgtok[b]])
                kb.T(lambda e, b=b, bk2=bk2: e.transpose(out=ps[0:36, bk2, 0:128], in_=gtok[b], identity=kb.identf), r=[Tgtok[b], kb.Tidf], w=[PB[bk2]])
                kb.V(lambda e, tt=tt, bk2=bk2: e.tensor_copy(out=gTf[0:36, tt * 128:(tt + 1) * 128], in_=ps[0:36, bk2, 0:128]), r=[PB[bk2]], w=[TgTf])
            kb.V(lambda e: e.tensor_copy(out=ghi[0:36, :], in_=gTf[0:36, :]), r=[TgTf], w=[Tgh])
            kb.V(lambda e: e.tensor_tensor(out=gTf[0:36, :], in0=gTf[0:36, :], in1=ghi[0:36, :], op=ALU.subtract), r=[TgTf, Tgh], w=[TgTf])
            kb.V(lambda e: e.tensor_copy(out=glo[0:36, :], in_=gTf[0:36, :]), r=[TgTf], w=[Tgl])
            ar.pop()
            P.barrier()
            ar.push()
            tabB = ar.alloc(2 * TABW, F32).rearrange("p (g c) -> p g c", g=2)
            TtabB = Tile("tabB")
            kb.DMA(lambda e: e.dma_start(out=tabB, in_=cin["c_tabB"]), w=[TtabB])
            cmpT = ar.alloc(S, F32)
            Tcmp = Tile("cmpT")
            kb.DMA(lambda e: e.dma_start(out=cmpT, in_=cin["c_cmp"]), w=[Tcmp])
            keep = ar.alloc(16 * 32, F32).rearrange("p (t j) -> p t j", t=16)
            force = ar.alloc(16 * 32, F32).rearrange("p (t j) -> p t j", t=16)
            Tkeep = Tile("keep")
            Tforce = Tile("force")
            kb.DMA(lambda e: e.dma_start(out=keep, in_=cin["c_keep"]), w=[Tkeep])
            kb.DMA(lambda e: e.dma_start(out=force, in_=cin["c_force"]), w=[Tforce])
            ovf = ar.alloc(32, F32)
            Tovf = Tile("ovf")
            ov_b = ar.alloc(32, BF16)
            Tov = Tile("ov")
            kb.DMA(lambda e: e.dma_start(out=ovf, in_=cin["c_ov"]), w=[Tovf])
            kb.V(lambda e: e.tensor_copy(out=ov_b, in_=ovf), r=[Tovf], w=[Tov])
            E_b = ar.alloc(16 * 128, BF16).rearrange("p (k s) -> p k s", k=16)
            TE = Tile("E")
            kb.wload(cin["c_E"], E_b, TE, k=16, n=128)
            oh_b = ar.alloc(36 * 128, BF16).rearrange("p (r m) -> p r m", r=36)
            Toh = Tile("oh")
            for q in range(3):
                kb.wload(cin["c_onehot"][:, q * 12:(q + 1) * 12, :], oh_b[:, q * 12:(q + 1) * 12, :], Toh, k=12, n=128)
            kcs = ar.alloc(4 * 128, BF16).rearrange("p (g n) -> p g n", g=4)
            vcs = ar.alloc(4 * 128, BF16).rearrange("p (g n) -> p g n", g=4)
            Tkcs = Tile("kcs")
            kb.DMA(lambda e: e.dma_start(out=kcs, in_=kc_d[0]), r=[Tkc], w=[Tkcs])
            kb.DMA(lambda e: e.dma_start(out=vcs, in_=kc_d[1]), r=[Tkc], w=[Tkcs])
            kb.attn_bufs()
            shg = [ar.alloc(4 * S, BF16).rearrange("p (s t) -> p s t", s=4) for _ in range(2)]
            Tshg = [Tile("shg%d" % i) for i in range(2)]
            qb = [ar.alloc(S, BF16) for _ in range(3)]
            Tqb = [Tile("qb%d" % i) for i in range(3)]
            ocmp = [ar.alloc(S, F32) for _ in range(3)]
            Toc = [Tile("ocmp%d" % i) for i in range(3)]
            impT = ar.alloc(S, F32)
            Timp = Tile("impT")
            penT = ar.alloc(S, BF16)
            Tpen = Tile("penT")
            kb.V(lambda e: e.memset(penT, 0.0), w=[Tpen])
            rg = ar.alloc(512, F32)
            Trg = Tile("rg")
            tmpo = ar.alloc(512, F32)
            Ttmpo = Tile("tmpo")
            v1 = ar.alloc(32, F32)
            v2 = ar.alloc(32, F32)
            mxa = ar.alloc(8, F32)
            mxb = ar.alloc(8, F32)
            selp = ar.alloc(32, F32)
            Tv1, Tv2, Tmxa, Tmxb, Tselp = Tile("v1"), Tile("v2"), Tile("mxa"), Tile("mxb"), Tile("selp")

            def finalize_gated(h, branch, tb, dst, Tdst, mode, dram=None, Tdram=None):
                kb.recip_den()
                recL = kb.rec
                r_ = h * 3 + branch
                kb.T(lambda e: e.matmul(ps[:, 6, :], lhsT=oh_b[:, r_, :], rhs=ghi[:, tb * 512:(tb + 1) * 512], start=True, stop=False), r=[Toh, Tgh], w=[PB[6]], silent=True)
                kb.T(lambda e: e.matmul(ps[:, 6, :], lhsT=oh_b[:, r_, :], rhs=glo[:, tb * 512:(tb + 1) * 512], start=False, stop=True), r=[Toh, Tgl], w=[PB[6]])
                kb.V(lambda e: e.tensor_tensor(out=rg, in0=ps[:, 6, :], in1=recL, op=ALU.mult), r=[PB[6], kb.Trec], w=[Trg])
                if mode == "set":
                    kb.V(lambda e: e.tensor_tensor(out=dst, in0=ps[:, 2, :], in1=rg, op=ALU.mult), r=[PB[2], Trg], w=[Tdst])
                elif mode == "add":
                    kb.V(lambda e: e.tensor_tensor(out=tmpo, in0=ps[:, 2, :], in1=rg, op=ALU.mult), r=[PB[2], Trg], w=[Ttmpo])
                    kb.G(lambda e: e.tensor_tensor(out=dst, in0=dst, in1=tmpo, op=ALU.add), r=[Ttmpo, Tdst], w=[Tdst])
                else:
                    kb.V(lambda e: e.tensor_tensor(out=tmpo, in0=ps[:, 2, :], in1=rg, op=ALU.mult), r=[PB[2], Trg], w=[Ttmpo])
                    ob = kb.osb_i % 2
                    kb.osb_i += 1
                    osbL = kb.osb[ob]
                    kb.G(lambda e: e.tensor_tensor(out=osbL, in0=dst, in1=tmpo, op=ALU.add), r=[Ttmpo, Tdst], w=[kb.Tosb[ob]])
                    kb.DMA(lambda e: e.dma_start(out=dram, in_=osbL.rearrange("p (t c) -> p t c", t=4)), r=[kb.Tosb[ob]], w=[Tdram], q=STQ)

            for g in range(4):
                sb_ = g % 2
                kb.DMA(lambda e, g=g, sb_=sb_: e.dma_start(out=shg[sb_], in_=sh_d[g].rearrange("s p t -> p s t")), r=[Tsh[g]], w=[Tshg[sb_]])
                ksT = shg[sb_][:, 0, :]
                vs = shg[sb_][:, 1, :].rearrange("p (t d) -> p t d", t=NT)
                kwT = shg[sb_][:, 2, :]
                vw = shg[sb_][:, 3, :].rearrange("p (t d) -> p t d", t=NT)
                for hh in range(3):
                    h = 3 * g + hh
                    slope = alibi(12, h)
                    kb.DMA(lambda e, hh=hh, h=h: e.dma_start(out=qb[hh], in_=qT_d[h]), r=[TqT[h]], w=[Tqb[hh]])
                    for tb in range(4):
                        units = [dict(kT=kcs[:, g, :], V=vcs[:, g, :], rd=[Tkcs], tab=cmpT[:, tb * 512:(tb + 1) * 512], tabrd=[Tcmp], slope=slope, bias=0.0, imp=(ov_b, Tov))]
                        run_units_chain(units, qb[hh][:, tb * 512:(tb + 1) * 512], Tqb[hh], True, True)
                        finalize_gated(h, 0, tb, ocmp[hh][:, tb * 512:(tb + 1) * 512], Toc[hh], "set")
                        recI = kb.rec
                        if hh == 0:
                            kb.V(lambda e, tb=tb: e.tensor_tensor(out=impT[0:32, tb * 512:(tb + 1) * 512], in0=ps[0:32, 7, :], in1=recI[0:32, :], op=ALU.mult), r=[PB[7], kb.Trec], w=[Timp])
                        else:
                            kb.V(lambda e: e.tensor_tensor(out=tmpo[0:32, :], in0=ps[0:32, 7, :], in1=recI[0:32, :], op=ALU.mult), r=[PB[7], kb.Trec], w=[Ttmpo])
                            kb.V(lambda e, tb=tb: e.tensor_tensor(out=impT[0:32, tb * 512:(tb + 1) * 512], in0=impT[0:32, tb * 512:(tb + 1) * 512], in1=tmpo[0:32, :], op=ALU.add), r=[Ttmpo, Timp], w=[Timp])
                for tt in range(NT):
                    kb.T(lambda e, tt=tt: e.transpose(out=ps[:, 6, 0:32], in_=impT[0:32, tt * 128:(tt + 1) * 128], identity=kb.identf[0:32, 0:32]), r=[Timp, kb.Tidf], w=[PB[6]])
                    kb.V(lambda e, tt=tt: e.tensor_tensor(out=v1, in0=ps[:, 6, 0:32], in1=keep[:, tt, :], op=ALU.mult), r=[PB[6], Tkeep], w=[Tv1])
                    kb.V(lambda e, tt=tt: e.tensor_tensor(out=v1, in0=v1, in1=force[:, tt, :], op=ALU.add), r=[Tv1, Tforce], w=[Tv1])
                    kb.V(lambda e: e.max(out=mxa, in_=v1), r=[Tv1], w=[Tmxa])
                    kb.V(lambda e: e.match_replace(out=v2, in_to_replace=mxa, in_values=v1, imm_value=-1.0e9), r=[Tv1, Tmxa], w=[Tv2])
                    kb.V(lambda e: e.max(out=mxb, in_=v2), r=[Tv2], w=[Tmxb])
                    kb.V(lambda e: e.tensor_scalar(out=selp, in0=v1, scalar1=mxb[:, 7:8], scalar2=None, op0=ALU.is_ge), r=[Tv1, Tmxb], w=[Tselp])
                    kb.V(lambda e: e.tensor_scalar(out=selp, in0=selp, scalar1=1.0, scalar2=30000.0, op0=ALU.subtract, op1=ALU.mult), r=[Tselp], w=[Tselp])
                    kb.T(lambda e: e.transpose(out=ps[0:32, 7, 0:128], in_=selp, identity=kb.identf), r=[Tselp, kb.Tidf], w=[PB[7]])
                    kb.V(lambda e, tt=tt: e.tensor_copy(out=penT[0:32, tt * 128:(tt + 1) * 128], in_=ps[0:32, 7, 0:128]), r=[PB[7]], w=[Tpen])
                for hh in range(3):
                    h = 3 * g + hh
                    slope = alibi(12, h)
                    for tb in range(4):
                        q_ap = qb[hh][:, tb * 512:(tb + 1) * 512]
                        units = []
                        for kt in range(0, 4 * tb + 4):
                            delta = 512 * tb - 128 * kt
                            de = min(delta, 128)
                            units.append(dict(kT=ksT[:, kt * 128:(kt + 1) * 128], V=vs[:, kt, :], rd=[Tshg[sb_]], tab=tabB[:, 0, de + 511:de + 511 + 512], tabrd=[TtabB],
                                              slope=slope, bias=-slope * (delta - de), pen=(E_b[:, kt, :], penT[:, tb * 512:(tb + 1) * 512], [TE, Tpen])))
                        run_units_chain(units, q_ap, Tqb[hh], True, True)
                        finalize_gated(h, 1, tb, ocmp[hh][:, tb * 512:(tb + 1) * 512], Toc[hh], "add")
                        units = []
                        for kt in range(max(0, 4 * tb - 4), 4 * tb + 4):
                            delta = 512 * tb - 128 * kt
                            units.append(dict(kT=kwT[:, kt * 128:(kt + 1) * 128], V=vw[:, kt, :], rd=[Tshg[sb_]], tab=tabB[:, 1, delta + 511:delta + 511 + 512], tabrd=[TtabB],
                                              slope=slope, bias=0.0))
                        run_units_chain(units, q_ap, Tqb[hh], True, True)
                        finalize_gated(h, 2, tb, ocmp[hh][:, tb * 512:(tb + 1) * 512], Toc[hh], "final", dram=oT_d[tb * 4:(tb + 1) * 4, :, h, :].rearrange("t p c -> p t c"), Tdram=ToT[h])
            for h in range(4):
                kb.DMA(lambda e, h=h: e.dma_start(out=qb[0], in_=qT_d[12 + h]), r=[TqT[12 + h]], w=[Tqb[0]])
                for tb in range(4):
                    units = [dict(kT=kTm[:, h, kt * 128:(kt + 1) * 128], V=Vm[:, h, kt, :], rd=[TkTm, TVm], tab=None) for kt in range(2)]
                    run_units_chain(units, qb[0][:, tb * 512:(tb + 1) * 512], Tqb[0], True, True)
                    kb.finalize_plain(oT_d[tb * 4:(tb + 1) * 4, :, 12 + h, :].rearrange("t p c -> p t c"), ToT[12 + h])
            ar.pop()
            ar.pop()
            w_out_phase(b_w_out[lb], 16, norm_g[l, 1:2, :])

        if seq is not None:
            for ph in seq:
                if ph == "a0":
                    layer_a(0)
                elif ph == "skv":
                    shared_kv_phase()
                elif ph == "b2":
                    layer_b(2)
                elif ph == "f0":
                    ffn_phase(0)
                if dbg and ph == "a0":
                    P.barrier()
                    Td = Tile("dbg")
                    for q in range(4):
                        kb.DMA(lambda e, q=q: e.dma_start(out=dbg_t["xm0"][q * 512:(q + 1) * 512, :], in_=out[q * 512:(q + 1) * 512, :]), r=Tout, w=[Td])
                    P.barrier()
        for l in range(l_start, n_layers if seq is None else 0):
            if l < 2:
                for _r in range(rep):
                    layer_a(l)
            else:
                if l == 2 or l == l_start:
                    shared_kv_phase()
                layer_b(l)
            def snap(name):
                if not dbg:
                    return
                P.barrier()
                Td = Tile("dbg")
                for q in range(4):
                    kb.DMA(lambda e, q=q: e.dma_start(out=dbg_t[name][q * 512:(q + 1) * 512, :], in_=out[q * 512:(q + 1) * 512, :]), r=Tout, w=[Td])
                P.barrier()
            snap("xm%d" % l)
            if stop_after_mixer and l == n_layers - 1:
                break
            ffn_phase(l)
            snap("x%d" % l)

        if dummy:
            P.barrier()
            dz = ar.alloc(64, F32)
            Tdz = Tile("dz")
            dz8 = ar.alloc(8, F32)
            kb.V(lambda e: e.memset(dz, 0.5), w=[Tdz])
            if "sigmoid" in dummy:
                kb.A(lambda e: e.activation(out=dz, in_=dz, func=AF.Sigmoid, scale=1.5), r=[Tdz], w=[Tdz])
            if "max" in dummy:
                kb.V(lambda e: e.max(out=dz8, in_=dz[:, 0:32]), r=[Tdz], w=[Tdz])
                kb.V(lambda e: e.match_replace(out=dz[:, 32:64], in_to_replace=dz8, in_values=dz[:, 0:32], imm_value=-1.0e9), r=[Tdz], w=[Tdz])
            if "isge" in dummy:
                kb.V(lambda e: e.tensor_scalar(out=dz[:, 0:32], in0=dz[:, 0:32], scalar1=dz8[:, 7:8], scalar2=None, op0=ALU.is_ge), r=[Tdz], w=[Tdz])
        P.final_wait("sync")
        P.emit(es)
        print("arena peak bytes/partition:", ar.peak * 2, "ops:", {e: len(P.ops[e]) for e in ENGS}, "semcounts:", P.count, max(P.dma_cnt))
    return nc


CONSTS = None


def kernel(**inputs):
    global CONSTS
    if CONSTS is None:
        CONSTS = make_consts()
    nc = build()
    x = np.ascontiguousarray(inputs["x"], dtype=np.float32)
    shared = {k: np.ascontiguousarray(v, dtype=np.float32) for k, v in inputs.items() if k not in ("x", "mem")}
    shared["kv_norm_g"] = shared["kv_norm_g"].reshape(1, D)
    active = [0, 1, 4, 5]
    zeros = {k: np.zeros_like(v) for k, v in shared.items()}
    zx = np.zeros_like(x[0])
    zm = np.zeros((256, D), np.float32)
    in_maps = []
    for c in range(8):
        if c in active:
            b = active.index(c)
            m = dict(shared)
            m.update(CONSTS)
            m["x"] = x[b]
            m["mem"] = np.ascontiguousarray(inputs["mem"][b], dtype=np.float32)
        else:
            m = dict(zeros)
            m.update(CONSTS)
            m["x"] = zx
            m["mem"] = zm
        in_maps.append(m)
    res = run_bass_kernel_spmd(nc, in_maps, core_ids=list(range(8)))
    return np.stack([res.results[c]["out"] for c in active], 0).astype(np.float32)
```
